# Optimizing a Trainium2 kernel written in Bass

```python
import math
import jax
import jax.numpy as jnp
from jax import lax
import numpy as np

D_MODEL = 2048
BATCH = 1
SEQ = 8192
DEPTH = 2

GRID_W = 64
CTX_LEN = 256
N_EVEN = (DEPTH + 1) // 2
N_ODD = DEPTH // 2
RMS_EPS = 1e-6
N_MOD = 6

HY_WIDTH = D_MODEL // 2
HY_ORDER = 2
HY_CONV_W = 3
HY_EMB = 33
HY_BANDS = (HY_EMB - 1) // 2
HY_FFN = 64
HY_N_FILT = 2 * HY_ORDER * HY_WIDTH
HY_DECAY_TARGET = 1e-2
HY_FAST_PCT = 0.3
HY_SLOW_PCT = 1.5
GM_WIDTH = D_MODEL // 2
GM_CHUNK = 128
GM_HEADS = 8
GM_HEAD_DIM = GM_WIDTH // GM_HEADS
EVEN_PROJ = (HY_ORDER + 1) * HY_WIDTH + 2 * GM_WIDTH
LRU_WIDTH = D_MODEL // 2
LRU_HEADS = 8
LRU_HEAD_DIM = LRU_WIDTH // LRU_HEADS
LRU_CONV_W = 4
LRU_C = 8.0
POOL_WIDTH = D_MODEL // 2
POOL_WINDOWS = (2, 4, 8, 16)
POOL_GROUP = POOL_WIDTH // len(POOL_WINDOWS)
ODD_PROJ = 2 * LRU_WIDTH + POOL_WIDTH
PEER_HEADS = 8
PEER_NKEYS = 128
PEER_N = PEER_NKEYS * PEER_NKEYS
PEER_DKEY = 256
PEER_TOPK = 16
PEER_BLOCK = 128

kernel_name = 'hyena_gmlp_rglru_pool_peer_flow_backbone'


def rmsnorm(x, g):
    x32 = x.astype(jnp.float32)
    y = x32 * lax.rsqrt(jnp.mean(x32 * x32, axis=-1, keepdims=True) + RMS_EPS)
    return y.astype(x.dtype) * g


def modulate(x, shift, scale):
    return x * (1 + scale) + shift


def dwconv_centred(x, w, b):
    width = w.shape[0]
    left = (width - 1) // 2
    L = x.shape[1]
    xp = jnp.pad(x, ((0, 0), (left, width - 1 - left), (0, 0)))
    y = b
    for k in range(width):
        y = y + w[k] * xp[:, k:k + L]
    return y


def hyena_filters(L, w1, b1, w2, b2, w3, freq, deltas):
    t = jnp.arange(L, dtype=jnp.float32)
    t_norm = t / max(L - 1, 1)
    bands = jnp.linspace(1e-4, HY_BANDS - 1, HY_BANDS, dtype=jnp.float32)
    ang = (2.0 * math.pi / L) * t[:, None] * bands[None, :]
    feats = jnp.concatenate([t_norm[:, None], jnp.cos(ang), -jnp.sin(ang)], axis=-1)
    h = jnp.sin(freq * (feats @ w1 + b1))
    h = jnp.sin(freq * (h @ w2 + b2))
    h = (h @ w3) * jnp.exp(-t_norm[:, None] * jnp.abs(deltas))
    h = h.astype(jnp.float32).reshape(L, 2, HY_ORDER, HY_WIDTH)
    return h / jnp.sum(jnp.abs(h), axis=(0, 1), keepdims=True)


def long_conv_bidir(u, h_fwd, h_bwd):
    L = u.shape[1]
    k = jnp.concatenate([h_fwd, h_bwd[::-1]], axis=0)
    U = jnp.fft.rfft(u.astype(jnp.float32), n=2 * L, axis=1)
    K = jnp.fft.rfft(k, n=2 * L, axis=0)
    y = jnp.fft.irfft(U * K[None], n=2 * L, axis=1)[:, :L]
    return y.astype(u.dtype)


def even_mixer(h, w_in, conv_w, conv_b, f_w1, f_b1, f_w2, f_b2, f_w3, f_freq, f_deltas, hy_bias,
               gm_norm, gm_ws, gm_bs, w_out):
    B, L, _ = h.shape
    proj = h @ w_in
    za = dwconv_centred(proj[..., :(HY_ORDER + 1) * HY_WIDTH], conv_w, conv_b)
    parts = jnp.split(za, HY_ORDER + 1, axis=-1)
    filt = hyena_filters(L, f_w1, f_b1, f_w2, f_b2, f_w3, f_freq, f_deltas)
    z = parts[0]
    for n in range(HY_ORDER):
        z = parts[n + 1] * (long_conv_bidir(z, filt[:, 0, n], filt[:, 1, n]) + hy_bias[n] * z)
    u, vg = jnp.split(jax.nn.gelu(proj[..., (HY_ORDER + 1) * HY_WIDTH:]), 2, axis=-1)
    vg = rmsnorm(vg, gm_norm).reshape(B, L // GM_CHUNK, GM_CHUNK, GM_HEADS, GM_HEAD_DIM)
    s = jnp.einsum('hpq,bnqhc->bnphc', gm_ws, vg) + gm_bs.T[:, :, None]
    yb = u * s.reshape(B, L, GM_WIDTH)
    return jnp.concatenate([z, yb], axis=-1) @ w_out


def linear_scan(a, b, h0, reverse):
    if reverse:
        b = b.at[:, -1].add(a[:, -1] * h0)
    else:
        b = b.at[:, 0].add(a[:, 0] * h0)

    def combine(e1, e2):
        a1, b1 = e1
        a2, b2 = e2
        return a1 * a2, a2 * b1 + b2

    _, hs = lax.associative_scan(combine, (a, b), reverse=reverse, axis=1)
    return hs


def rglru_direction(xr, w_a, b_a, w_x, b_x, lam, h0, reverse):
    B, L, _ = xr.shape
    x32 = xr.astype(jnp.float32)
    xh = x32.reshape(B, L, LRU_HEADS, LRU_HEAD_DIM)
    r = jax.nn.sigmoid(jnp.einsum('blhi,hij->blhj', xh, w_a).reshape(B, L, LRU_WIDTH) + b_a)
    i = jax.nn.sigmoid(jnp.einsum('blhi,hij->blhj', xh, w_x).reshape(B, L, LRU_WIDTH) + b_x)
    log_a = -LRU_C * r * jax.nn.softplus(-lam)
    a = jnp.exp(log_a)
    b = jnp.sqrt(-jnp.expm1(2.0 * log_a)) * (i * x32)
    return linear_scan(a, b, h0, reverse)


def multiscale_pool(xp, w, b, scale):
    B, L, _ = xp.shape
    t = jnp.arange(L)
    outs = []
    for g, win in enumerate(POOL_WINDOWS):
        xg = xp[..., g * POOL_GROUP:(g + 1) * POOL_GROUP].astype(jnp.float32)
        cs = jnp.concatenate([jnp.zeros((B, 1, POOL_GROUP), jnp.float32), jnp.cumsum(xg, axis=1)], axis=1)
        lo = jnp.clip(t - win // 2, 0, L)
        hi = jnp.clip(t + win // 2, 0, L)
        mean = (cs[:, hi] - cs[:, lo]) / (hi - lo).astype(jnp.float32)[None, :, None]
        outs.append((mean - xg).astype(xp.dtype) @ w[g] + b[g])
    return jnp.concatenate(outs, axis=-1) * scale


def odd_mixer(h, init_f, init_b, w_in, conv_w, conv_b, lru_wa, lru_ba, lru_wx, lru_bx, lru_lam,
              pool_w, pool_b, pool_scale, w_out, with_output):
    proj = h @ w_in
    xr = dwconv_centred(proj[..., LRU_WIDTH:2 * LRU_WIDTH], conv_w, conv_b)
    hf = rglru_direction(xr, lru_wa[0], lru_ba[0], lru_wx[0], lru_bx[0], lru_lam[0], init_f, False)
    hb = rglru_direction(xr, lru_wa[1], lru_ba[1], lru_wx[1], lru_bx[1], lru_lam[1], init_b, True)
    states = (hf[:, -1], hb[:, 0])
    if not with_output:
        return None, states
    yc = jax.nn.gelu(proj[..., :LRU_WIDTH]) * (hf + hb).astype(h.dtype)
    yd = multiscale_pool(proj[..., 2 * LRU_WIDTH:], pool_w, pool_b, pool_scale)
    return jnp.concatenate([yc, yd], axis=-1) @ w_out, states


def peer(h, q_w, sub_keys, u_tab, v_tab):
    B, L, D = h.shape
    T = B * L
    hf = h.reshape(T, D)
    q = (hf @ q_w).reshape(T, PEER_HEADS, 2, PEER_DKEY // 2).astype(jnp.float32)
    s = jnp.einsum('thsk,hsnk->thsn', q, sub_keys.astype(jnp.float32))
    s_top, i_top = lax.top_k(s, PEER_TOPK)
    cand = (s_top[:, :, 0, :, None] + s_top[:, :, 1, None, :]).reshape(T, PEER_HEADS, PEER_TOPK * PEER_TOPK)
    c_top, c_idx = lax.top_k(cand, PEER_TOPK)
    idx_a = jnp.take_along_axis(i_top[:, :, 0], c_idx // PEER_TOPK, axis=-1)
    idx_b = jnp.take_along_axis(i_top[:, :, 1], c_idx % PEER_TOPK, axis=-1)
    expert = idx_a * PEER_NKEYS + idx_b
    gate = jax.nn.softmax(c_top, axis=-1).astype(h.dtype)
    nb = T // PEER_BLOCK

    def block(args):
        xb, eb, gb = args
        act = jax.nn.gelu(jnp.einsum('td,thkd->thk', xb, u_tab[eb]))
        return jnp.einsum('thk,thkd->td', gb * act, v_tab[eb])

    out = lax.map(block, (hf.reshape(nb, PEER_BLOCK, D),
                          expert.reshape(nb, PEER_BLOCK, PEER_HEADS, PEER_TOPK),
                          gate.reshape(nb, PEER_BLOCK, PEER_HEADS, PEER_TOPK)))
    return out.reshape(B, L, D)


def setup_inputs(seed: int = 0) -> dict:
    key = jax.random.key(seed)
    ks = iter(jax.random.split(key, 64))
    D = D_MODEL

    def nrm(shape, scale):
        return scale * jax.random.normal(next(ks), shape, jnp.float32)

    min_decay = math.log(HY_DECAY_TARGET) / HY_SLOW_PCT
    max_decay = math.log(HY_DECAY_TARGET) / HY_FAST_PCT
    deltas = jnp.linspace(min_decay, max_decay, HY_N_FILT, dtype=jnp.float32)[None]
    hy_deltas = deltas * (1.0 + nrm((N_EVEN, HY_N_FILT), 0.05))
    a_pow = jax.random.uniform(next(ks), (N_ODD, 2, LRU_WIDTH), jnp.float32, 0.9, 0.999)
    s_lam = a_pow ** (1.0 / LRU_C)
    lru_lam = jnp.log(s_lam) - jnp.log1p(-s_lam)
    return {
        'x': nrm((BATCH, SEQ, D), 1.0),
        'c': nrm((BATCH, D), 1.0),
        'ctx': nrm((BATCH, CTX_LEN, D), 1.0),
        'c_ctx': nrm((D,), 1.0),
        'ada_w': nrm((DEPTH, D, N_MOD * D), 0.5 * D ** -0.5),
        'ada_b': nrm((DEPTH, N_MOD * D), 0.02),
        'norm_mix': 1.0 + nrm((DEPTH, D), 0.05),
        'norm_ffn': 1.0 + nrm((DEPTH, D), 0.05),
        'norm_final': 1.0 + nrm((D,), 0.05),
        'ev_w_in': nrm((N_EVEN, D, EVEN_PROJ), D ** -0.5),
        'ev_conv_w': nrm((N_EVEN, HY_CONV_W, (HY_ORDER + 1) * HY_WIDTH), HY_CONV_W ** -0.5),
        'ev_conv_b': nrm((N_EVEN, (HY_ORDER + 1) * HY_WIDTH), 0.02),
        'hy_w1': nrm((N_EVEN, HY_EMB, HY_FFN), HY_EMB ** -0.5),
        'hy_b1': nrm((N_EVEN, HY_FFN), 0.1),
        'hy_w2': nrm((N_EVEN, HY_FFN, HY_FFN), HY_FFN ** -0.5),
        'hy_b2': nrm((N_EVEN, HY_FFN), 0.1),
        'hy_w3': nrm((N_EVEN, HY_FFN, HY_N_FILT), HY_FFN ** -0.5),
        'hy_freq': 1.0 + nrm((N_EVEN, HY_FFN), 0.05),
        'hy_deltas': hy_deltas,
        'hy_bias': nrm((N_EVEN, HY_ORDER, HY_WIDTH), 0.5),
        'gm_norm': 1.0 + nrm((N_EVEN, GM_WIDTH), 0.05),
        'gm_ws': nrm((N_EVEN, GM_HEADS, GM_CHUNK, GM_CHUNK), GM_CHUNK ** -0.5),
        'gm_bs': 1.0 + nrm((N_EVEN, GM_HEADS, GM_CHUNK), 0.02),
        'ev_w_out': nrm((N_EVEN, D, D), D ** -0.5),
        'od_w_in': nrm((N_ODD, D, ODD_PROJ), D ** -0.5),
        'od_conv_w': nrm((N_ODD, LRU_CONV_W, LRU_WIDTH), LRU_CONV_W ** -0.5),
        'od_conv_b': nrm((N_ODD, LRU_WIDTH), 0.02),
        'lru_wa': nrm((N_ODD, 2, LRU_HEADS, LRU_HEAD_DIM, LRU_HEAD_DIM), LRU_HEAD_DIM ** -0.5),
        'lru_ba': nrm((N_ODD, 2, LRU_WIDTH), 0.02),
        'lru_wx': nrm((N_ODD, 2, LRU_HEADS, LRU_HEAD_DIM, LRU_HEAD_DIM), LRU_HEAD_DIM ** -0.5),
        'lru_bx': nrm((N_ODD, 2, LRU_WIDTH), 0.02),
        'lru_lam': lru_lam,
        'pool_w': nrm((N_ODD, len(POOL_WINDOWS), POOL_GROUP, POOL_GROUP), POOL_GROUP ** -0.5),
        'pool_b': nrm((N_ODD, len(POOL_WINDOWS), POOL_GROUP), 0.02),
        'pool_scale': 1.0 + nrm((N_ODD, POOL_WIDTH), 0.05),
        'od_w_out': nrm((N_ODD, D, D), D ** -0.5),
        'peer_q': nrm((DEPTH, D, PEER_HEADS * PEER_DKEY), D ** -0.5),
        'peer_keys': nrm((DEPTH, PEER_HEADS, 2, PEER_NKEYS, PEER_DKEY // 2), (PEER_DKEY // 2) ** -0.5),
        'peer_u': nrm((DEPTH, PEER_N, D), D ** -0.5),
        'peer_v': nrm((DEPTH, PEER_N, D), PEER_HEADS ** -0.5),
    }


def reference(x, c, ctx, c_ctx, ada_w, ada_b, norm_mix, norm_ffn, norm_final,
              ev_w_in, ev_conv_w, ev_conv_b, hy_w1, hy_b1, hy_w2, hy_b2, hy_w3, hy_freq, hy_deltas, hy_bias,
              gm_norm, gm_ws, gm_bs, ev_w_out,
              od_w_in, od_conv_w, od_conv_b, lru_wa, lru_ba, lru_wx, lru_bx, lru_lam,
              pool_w, pool_b, pool_scale, od_w_out,
              peer_q, peer_keys, peer_u, peer_v):
    xc = ctx
    for l in range(DEPTH):
        last = l == DEPTH - 1
        sh1, sc1, g1, sh2, sc2, g2 = jnp.split((jax.nn.silu(c) @ ada_w[l] + ada_b[l])[:, None, :], N_MOD, axis=-1)
        csh1, csc1, cg1, csh2, csc2, cg2 = jnp.split(jax.nn.silu(c_ctx) @ ada_w[l] + ada_b[l], N_MOD, axis=-1)
        hx = modulate(rmsnorm(x, norm_mix[l]), sh1, sc1)
        if l % 2 == 0:
            e = l // 2
            ev_args = (ev_w_in[e], ev_conv_w[e], ev_conv_b[e], hy_w1[e], hy_b1[e], hy_w2[e], hy_b2[e], hy_w3[e],
                       hy_freq[e], hy_deltas[e], hy_bias[e], gm_norm[e], gm_ws[e], gm_bs[e], ev_w_out[e])
            x = x + g1 * even_mixer(hx, *ev_args)
            if not last:
                hc = modulate(rmsnorm(xc, norm_mix[l]), csh1, csc1)
                xc = xc + cg1 * even_mixer(hc, *ev_args)
        else:
            o = l // 2
            od_args = (od_w_in[o], od_conv_w[o], od_conv_b[o], lru_wa[o], lru_ba[o], lru_wx[o], lru_bx[o],
                       lru_lam[o], pool_w[o], pool_b[o], pool_scale[o], od_w_out[o])
            hc = modulate(rmsnorm(xc, norm_mix[l]), csh1, csc1)
            zero_state = jnp.zeros((hc.shape[0], LRU_WIDTH), jnp.float32)
            yc, (st_f, st_b) = odd_mixer(hc, zero_state, zero_state, *od_args, with_output=not last)
            y, _ = odd_mixer(hx, st_f, st_b, *od_args, with_output=True)
            x = x + g1 * y
            if not last:
                xc = xc + cg1 * yc
        x = x + g2 * peer(modulate(rmsnorm(x, norm_ffn[l]), sh2, sc2), peer_q[l], peer_keys[l], peer_u[l], peer_v[l])
        if not last:
            xc = xc + cg2 * peer(modulate(rmsnorm(xc, norm_ffn[l]), csh2, csc2),
                                 peer_q[l], peer_keys[l], peer_u[l], peer_v[l])
    return rmsnorm(x, norm_final)
```

```python
import contextlib
import numpy as np
import concourse.bass as bass
import concourse.mybir as mybir
from concourse.bass_utils import run_bass_kernel_spmd

F32 = mybir.dt.float32
AF = mybir.ActivationFunctionType
ALU = mybir.AluOpType
AX = mybir.AxisListType


class Reg:
    __slots__ = ("name", "last_w", "reads")

    def __init__(self, name):
        self.name = name
        self.last_w = None
        self.reads = {}


class TT:
    def __init__(self, name, handle):
        self.name = name
        self.h = handle
        self.whole = Reg(name)
        self.sub = {}

    def ap(self):
        return self.h.ap() if hasattr(self.h, "ap") and callable(self.h.ap) else self.h

    def __getitem__(self, idx):
        return self.h[idx]

    def r(self, key=None):
        if key is None:
            return (self, None)
        return (self, key)


class Prog:
    SEM_ROLL = 2000

    def __init__(self, n_dma_sems=24):
        self.nc = bass.Bass("TRN2", target_bir_lowering=False)
        nc = self.nc
        self.stack = contextlib.ExitStack()
        self.engs = {"pe": nc.tensor, "dve": nc.vector, "act": nc.scalar, "pool": nc.gpsimd, "sp": nc.sync}
        self.sems = {}
        self.cur_sem = {}
        self.cnt = {}
        self.sem_gen = {}
        for e in self.engs:
            self.sem_gen[e] = 0
            self._new_eng_sem(e)
        self.dma_sems = []
        for i in range(n_dma_sems):
            nm = f"dq{i}"
            self.sems[nm] = self.stack.enter_context(nc.semaphore(nm))
            self.cnt[nm] = 0
            self.dma_sems.append(nm)
        self.dma_rr = 0
        self.waited = {e: {} for e in self.engs}
        self.n_inst = 0
        self.n_wait = 0
        self._uid = 0

    def _new_eng_sem(self, e):
        nm = f"s_{e}_{self.sem_gen[e]}"
        self.sem_gen[e] += 1
        self.sems[nm] = self.stack.enter_context(self.nc.semaphore(nm))
        self.cnt[nm] = 0
        self.cur_sem[e] = nm

    def sbuf(self, name, shape, dtype=F32):
        h = self.stack.enter_context(self.nc.sbuf_tensor(name, list(shape), dtype))
        return TT(name, h)

    def psum(self, name, shape, dtype=F32):
        h = self.stack.enter_context(self.nc.psum_tensor(name, list(shape), dtype))
        return TT(name, h)

    def dram(self, name, shape, kind, dtype=F32):
        h = self.nc.dram_tensor(name, list(shape), dtype, kind=kind)
        t = TT(name, h)
        return t

    def _regs(self, lst):
        out = []
        for x in lst:
            if isinstance(x, TT):
                out.append((x, None))
            else:
                out.append(x)
        return out

    def _collect(self, reads, writes):
        deps = {}

        def add(ev):
            if ev is None:
                return
            s, v = ev
            if deps.get(s, 0) < v:
                deps[s] = v

        for (t, k) in self._regs(reads):
            add(t.whole.last_w)
            if k is None:
                for r in t.sub.values():
                    add(r.last_w)
            else:
                r = t.sub.get(k)
                if r is not None:
                    add(r.last_w)
        for (t, k) in self._regs(writes):
            regs = [t.whole]
            if k is None:
                regs += list(t.sub.values())
            else:
                r = t.sub.get(k)
                if r is not None:
                    regs.append(r)
            for r in regs:
                add(r.last_w)
                for s, v in r.reads.items():
                    add((s, v))
        return deps

    def _record(self, reads, writes, ev):
        s, v = ev
        for (t, k) in self._regs(reads):
            if k is None:
                r = t.whole
            else:
                r = t.sub.get(k)
                if r is None:
                    r = t.sub[k] = Reg(f"{t.name}:{k}")
            if r.reads.get(s, 0) < v:
                r.reads[s] = v
        for (t, k) in self._regs(writes):
            if k is None:
                t.whole.last_w = ev
                t.whole.reads = {}
                t.sub = {}
            else:
                r = t.sub.get(k)
                if r is None:
                    r = t.sub[k] = Reg(f"{t.name}:{k}")
                r.last_w = ev
                r.reads = {}

    def _emit_waits(self, e, deps, skip_self=False):
        eng = self.engs[e]
        w = self.waited[e]
        for s, v in deps.items():
            if skip_self and s.startswith(f"s_{e}_"):
                continue
            if w.get(s, 0) >= v:
                continue
            eng.wait_ge(self.sems[s], v)
            w[s] = v
            self.n_wait += 1

    def op(self, e, fn, reads=(), writes=(), skip_self=False):
        deps = self._collect(reads, writes)
        self._emit_waits(e, deps, skip_self=skip_self)
        s = self.cur_sem[e]
        if self.cnt[s] >= self.SEM_ROLL:
            self._new_eng_sem(e)
            s = self.cur_sem[e]
        ins = fn(self.engs[e])
        ins.then_inc(self.sems[s], 1)
        self.cnt[s] += 1
        ev = (s, self.cnt[s])
        self._record(reads, writes, ev)
        self.n_inst += 1
        return ev

    def dma(self, q, out, in_, reads=(), writes=(), **kw):
        deps = self._collect(reads, writes)
        s = self.dma_sems[self.dma_rr % len(self.dma_sems)]
        self.dma_rr += 1
        n = self.cnt[s]
        if n > 0:
            if deps.get(s, 0) < 16 * n:
                deps[s] = 16 * n
        self._emit_waits(q, deps)
        ins = self.engs[q].dma_start(out=out, in_=in_, **kw)
        ins.then_inc(self.sems[s], 16)
        self.cnt[s] = n + 1
        ev = (s, 16 * (n + 1))
        self._record(reads, writes, ev)
        self.n_inst += 1
        return ev

    def finish(self, q="sp"):
        deps = {}
        for s, c in self.cnt.items():
            if c == 0:
                continue
            v = c * 16 if s.startswith("dq") else c
            deps[s] = v
        self._emit_waits(q, deps)

    def close(self):
        self.stack.close()


NCORES = 8
D = 2048
KC = 16
EPS = 1e-6


def _run(p, in_maps):
    p.finish()
    res = run_bass_kernel_spmd(p.nc, in_maps, core_ids=list(range(NCORES)))
    p.close()
    return res.results


def fm(x2d):
    T, F = x2d.shape
    return np.ascontiguousarray(x2d.T.reshape(F // 128, 128, T).transpose(1, 0, 2))


def unfm(a):
    P, C, T = a.shape
    return np.ascontiguousarray(a.transpose(2, 1, 0).reshape(T, C * P))


def vec_fm(v):
    return np.ascontiguousarray(v.reshape(-1, 128).T)


def wblocks(w):
    K, N = w.shape
    return np.ascontiguousarray(w.reshape(K // 128, 128, N // 128, 128).transpose(2, 1, 0, 3).reshape(N // 128, 128, (K // 128) * 128))


def phase_A(c, c_ctx, ada_w, ada_b):
    NJ = 12
    p = Prog()
    cT = p.dram("cT", [128, KC, 2], "ExternalInput")
    aw = p.dram("aw", [2, KC, 128, NJ * 128], "ExternalInput")
    ab = p.dram("ab", [128, 2, NJ], "ExternalInput")
    mo = p.dram("mo", [128, 2, NJ, 2], "ExternalOutput")
    cs = p.sbuf("cs", [128, KC, 2]); ss = p.sbuf("ss", [128, KC, 2]); abs_ = p.sbuf("abs", [128, 2, NJ])
    wb = p.sbuf("wb", [128, KC, NJ * 128]); mos = p.sbuf("mos", [128, 2, NJ, 2])
    ps = p.psum("ps", [128, 512])
    p.dma("sp", cs[:], cT.ap(), writes=[cs])
    p.dma("sp", abs_[:], ab.ap(), writes=[abs_])
    p.op("act", lambda e: e.activation(out=ss[:], in_=cs[:], func=AF.Silu), reads=[cs], writes=[ss])
    for l in range(2):
        for kc in range(KC):
            p.dma("sp", wb[:, kc, :], aw.ap()[l, kc], writes=[wb.r(kc)])
        for j in range(NJ):
            for kc in range(KC):
                p.op("pe", lambda e: e.matmul(ps[:, 2 * j:2 * j + 2], wb[:, kc, j * 128:(j + 1) * 128], ss[:, kc, :],
                                              start=(kc == 0), stop=(kc == KC - 1)),
                     reads=[wb.r(kc), ss], writes=[ps.r(j)], skip_self=True)
            p.op("act", lambda e: e.activation(out=mos[:, l, j, :], in_=ps[:, 2 * j:2 * j + 2], func=AF.Identity,
                                               bias=abs_[:, l, j:j + 1]), reads=[ps.r(j), abs_], writes=[mos.r((l, j))])
    p.dma("sp", mo.ap(), mos[:], reads=[mos], writes=[mo])
    cTh = np.stack([vec_fm(c.reshape(-1)), vec_fm(c_ctx.reshape(-1))], axis=2)
    in_maps = []
    for i in range(NCORES):
        cols = slice(i * NJ * 128, (i + 1) * NJ * 128)
        awi = np.ascontiguousarray(ada_w[:, :, cols].reshape(2, KC, 128, NJ * 128))
        abi = np.ascontiguousarray(ada_b[:, cols].reshape(2, NJ, 128).transpose(2, 0, 1))
        in_maps.append({"cT": cTh, "aw": awi, "ab": abi})
    res = _run(p, in_maps)
    mod = np.zeros((2, 12288, 2), np.float32)
    for i in range(NCORES):
        r = res[i]["mo"]
        mod[:, i * NJ * 128:(i + 1) * NJ * 128, :] = r.transpose(1, 2, 0, 3).reshape(2, NJ * 128, 2)
    return mod


class Ring:
    def __init__(self, items):
        self.items = items
        self.i = 0

    def next(self):
        t = self.items[self.i % len(self.items)]
        self.i += 1
        return t


def emit_consts(p):
    ones = p.sbuf("c_ones", [128, 128])
    p.op("dve", lambda e: e.memset(ones[:], 1.0), writes=[ones])
    return ones


def emit_gp(p, g, sc, gp):
    p.op("dve", lambda e: e.tensor_scalar(out=gp[:], in0=sc[:], scalar1=1.0, scalar2=None, op0=ALU.add), reads=[sc], writes=[gp])
    p.op("dve", lambda e: e.tensor_tensor(gp[:], gp[:], g[:], ALU.mult), reads=[gp, g], writes=[gp])


def emit_norm_mod(p, src, dst, C, off, n, gp, sh, ones, scr, stag, dtag, doff=None):
    if doff is None:
        doff = off
    sq, ps, rt, rstd = scr["sq"], scr["ps"], scr["rt"], scr["rstd"]
    p.op("act", lambda e: e.activation(out=sq[:, 0:C, 0:n], in_=src[:, 0:C, off:off + n], func=AF.Square),
         reads=[src.r(stag)], writes=[sq])
    for c in range(C):
        p.op("pe", lambda e: e.matmul(ps[:, 0:n], ones[:], sq[:, c, 0:n], start=(c == 0), stop=(c == C - 1)),
             reads=[sq, ones], writes=[ps], skip_self=True)
    p.op("act", lambda e: e.activation(out=rt[:, 0:n], in_=ps[:, 0:n], func=AF.Sqrt, scale=1.0 / (128 * C), bias=EPS),
         reads=[ps], writes=[rt])
    p.op("dve", lambda e: e.reciprocal(rstd[:, 0:n], rt[:, 0:n]), reads=[rt], writes=[rstd])
    p.op("dve", lambda e: e.tensor_tensor(sq[:, 0:C, 0:n], src[:, 0:C, off:off + n],
                                          rstd[:, 0:n].unsqueeze(1).broadcast_to([128, C, n]), ALU.mult),
         reads=[src.r(stag), rstd], writes=[sq])
    for c in range(C):
        if sh is not None:
            p.op("act", lambda e: e.activation(out=dst[:, c, doff:doff + n], in_=sq[:, c, 0:n], func=AF.Identity,
                                               scale=gp[:, c:c + 1], bias=sh[:, c:c + 1]),
                 reads=[sq, gp, sh], writes=[dst.r(dtag)])
        else:
            p.op("act", lambda e: e.activation(out=dst[:, c, doff:doff + n], in_=sq[:, c, 0:n], func=AF.Identity,
                                               scale=gp[:, c:c + 1]),
                 reads=[sq, gp], writes=[dst.r(dtag)])


def norm_scratch(p, C, nmax, pfx=""):
    return {"sq": p.sbuf(pfx + "n_sq", [128, C, nmax]), "ps": p.psum(pfx + "n_ps", [128, 512]),
            "rt": p.sbuf(pfx + "n_rt", [128, nmax]), "rstd": p.sbuf(pfx + "n_rstd", [128, nmax])}


def split_tiles(nx, nc_, tmax):
    tiles = []
    o = 0
    while o < nx:
        n = min(tmax, nx - o)
        tiles.append((o, n, "x"))
        o += n
    o = 0
    while o < nc_:
        n = min(tmax, nc_ - o)
        tiles.append((nx + o, n, "c"))
        o += n
    return tiles


def phase_B(xT_cores, mv_cores, wblk, epi, NX, NCX, gmn=None):
    M = wblk.shape[0]
    NT = NX + NCX
    tiles = split_tiles(NX, NCX, 512)
    p = Prog()
    xTd = p.dram("xT", [128, KC, NT], "ExternalInput")
    mvd = p.dram("mv", [128, KC, 5], "ExternalInput")
    wd = p.dram("w", [M, 128, KC * 128], "ExternalInput")
    od = p.dram("o", [M, 128, NT], "ExternalOutput")
    ones = emit_consts(p)
    xs = p.sbuf("xs", [128, KC, 512]); hT = p.sbuf("hT", [128, KC, NT]); mv = p.sbuf("mvs", [128, KC, 5])
    gpx = p.sbuf("gpx", [128, KC]); gpc = p.sbuf("gpc", [128, KC])
    g = p.sbuf("g_", [128, KC]); shx = p.sbuf("shx", [128, KC]); scx = p.sbuf("scx", [128, KC])
    shc = p.sbuf("shc", [128, KC]); scc = p.sbuf("scc", [128, KC])
    scr = norm_scratch(p, KC, 512)
    p.dma("sp", mv[:], mvd.ap(), writes=[mv])
    for i, t in enumerate([g, shx, scx, shc, scc]):
        p.op("dve", lambda e: e.tensor_copy(t[:], mv[:, :, i]), reads=[mv], writes=[t])
    emit_gp(p, g, scx, gpx)
    emit_gp(p, g, scc, gpc)
    for (off, n, kind) in tiles:
        p.dma("sp", xs[:, :, 0:n], xTd.ap()[:, :, off:off + n], writes=[xs])
        emit_norm_mod(p, xs, hT, KC, 0, n, gpx if kind == "x" else gpc, shx if kind == "x" else shc, ones, scr, None, off, doff=off)
    wring = Ring([p.sbuf(f"wb{i}", [128, KC * 128]) for i in range(3)])
    pring = Ring([p.psum(f"pm{i}", [128, 512]) for i in range(4)])
    oring = Ring([p.sbuf(f"ob{i}", [128, NT]) for i in range(2)])
    vgT = None
    if gmn is not None:
        vgT = p.sbuf("vgT", [128, 8, NT])
        gm = p.sbuf("gm", [128, 8])
        gmd = p.dram("gmn", [128, 8], "ExternalInput")
        p.dma("sp", gm[:], gmd.ap(), writes=[gm])
    nvg = 0
    for m in range(M):
        wb = wring.next()
        p.dma("sp", wb[:], wd.ap()[m], writes=[wb])
        if epi[m] == "vg":
            ob = None
        else:
            ob = oring.next()
        for (off, n, kind) in tiles:
            ps = pring.next()
            for kc in range(KC):
                p.op("pe", lambda e: e.matmul(ps[:, 0:n], wb[:, kc * 128:(kc + 1) * 128], hT[:, kc, off:off + n],
                                              start=(kc == 0), stop=(kc == KC - 1)),
                     reads=[wb, hT.r(off)], writes=[ps], skip_self=True)
            if epi[m] == "copy":
                p.op("dve", lambda e: e.tensor_copy(ob[:, off:off + n], ps[:, 0:n]), reads=[ps], writes=[ob.r(off)])
            elif epi[m] == "gelu":
                p.op("act", lambda e: e.activation(out=ob[:, off:off + n], in_=ps[:, 0:n], func=AF.Gelu), reads=[ps], writes=[ob.r(off)])
            else:
                p.op("act", lambda e: e.activation(out=vgT[:, nvg, off:off + n], in_=ps[:, 0:n], func=AF.Gelu), reads=[ps], writes=[vgT.r(off)])
        if epi[m] == "vg":
            nvg += 1
        else:
            p.dma("pool", od.ap()[m], ob[:], reads=[ob], writes=[od.r(m)])
    if gmn is not None:
        vo = vgT
        m0 = epi.index("vg")
        for (off, n, kind) in tiles:
            emit_norm_mod(p, vgT, vo, 8, off, n, gm, None, ones, scr, off, off)
        for j in range(8):
            p.dma("pool", od.ap()[m0 + j], vo[:, j, :], reads=[vo], writes=[od.r(m0 + j)])
    in_maps = []
    for i in range(NCORES):
        mp = {"xT": xT_cores[i], "mv": mv_cores[i], "w": wblk}
        if gmn is not None:
            mp["gmn"] = gmn
        in_maps.append(mp)
    res = _run(p, in_maps)
    return [r["o"] for r in res]


NEG = -1.0e30
_DBG_NA = 128
_DBG_NP = 99
PASS_N = 352
GA = 2


def phase_D(catT_cores, xT_cores, mv_cores, woblk, pqblk, keysT, uT, vt, NX, NCX, final_g=None, pool=None):
    NT = NX + NCX
    tiles = split_tiles(NX, NCX, 512)
    passes = []
    o = 0
    while o < NT:
        n = min(PASS_N, NT - o)
        passes.append((o, n))
        o += n
    p = Prog()
    catd = p.dram("catT", [128, KC, NT], "ExternalInput")
    xd = p.dram("xT", [128, KC, NT], "ExternalInput")
    mvd = p.dram("mv", [128, KC, 9], "ExternalInput")
    wod = p.dram("wo", [16, 128, 2048], "ExternalInput")
    pqd = p.dram("pq", [16, 128, 2048], "ExternalInput")
    kd = p.dram("keysT", [16, 128, 128], "ExternalInput")
    ud = p.dram("uT", [128, 128, 2048], "ExternalInput")
    vd = p.dram("vt", [128, 128, 2048], "ExternalInput")
    idd = p.dram("ident", [128, 128], "ExternalInput")
    x1d = p.dram("x1s", [128, KC, NT], "Internal")
    outd = p.dram("o", [128, KC, NT], "ExternalOutput")
    ones = emit_consts(p)
    ident = p.sbuf("ident_s", [128, 128])
    p.dma("sp", ident[:], idd.ap(), writes=[ident])
    mv = p.sbuf("mvs", [128, KC, 9])
    p.dma("sp", mv[:], mvd.ap(), writes=[mv])
    names = ["g1x", "g1c", "gf", "sh2x", "sc2x", "sh2c", "sc2c", "g2x", "g2c"]
    V = {}
    for i, nm in enumerate(names):
        V[nm] = p.sbuf(nm, [128, KC])
        p.op("dve", lambda e: e.tensor_copy(V[nm][:], mv[:, :, i]), reads=[mv], writes=[V[nm]])
    gpx = p.sbuf("gpx", [128, KC]); gpc = p.sbuf("gpc", [128, KC])
    emit_gp(p, V["gf"], V["sc2x"], gpx)
    emit_gp(p, V["gf"], V["sc2c"], gpc)
    if final_g is not None:
        fgd = p.dram("fg", [128, KC], "ExternalInput")
        fg = p.sbuf("fg_s", [128, KC])
        p.dma("sp", fg[:], fgd.ap(), writes=[fg])
    wring = Ring([p.sbuf(f"wb{i}", [128, 2048]) for i in range(2)])
    vring = Ring([p.sbuf(f"vb{i}", [128, 2048]) for i in range(2 * GA)])
    pm = Ring([p.psum(f"pm{i}", [128, 512]) for i in range(2)])
    pact = Ring([p.psum(f"pa{i}", [128, 512]) for i in range(2)])
    pbc = Ring([p.psum(f"pb{i}", [128, 512]) for i in range(2)])
    pso = p.psum("pso", [128, 512])
    SaT = p.sbuf("SaT", [128, 8, PASS_N]); SbT = p.sbuf("SbT", [128, 8, PASS_N])
    alias_views = []
    BIGN = max(KC * NT, 3 * KC * PASS_N)
    bigf = p.sbuf("bigf", [128, BIGN])
    big = TT("big", bigf.h[:, 0:KC * NT].rearrange("p (c t) -> p c t", c=KC))
    kbc = p.sbuf("kbc", [128, 8, PASS_N])
    assert 8 * PASS_N >= 2 * NT
    xrows = [TT(f"xrow{i}", kbc.h[:].rearrange("p a b -> p (a b)")[:, i * NT:(i + 1) * NT]) for i in range(2)]
    p.dma("sp", big[:], catd.ap(), writes=[big])
    if pool is not None:
        pwd = p.dram("pw", [4, 2, 128, 256], "ExternalInput")
        ppd = p.dram("pp", [128, 8, 2], "ExternalInput")
        pw = TT("pw_v", SbT.h[:].rearrange("p a b -> p (a b)")[:, 0:2048].rearrange("p (g i j) -> p g i j", g=4, i=2))
        pp = p.sbuf("pp_s", [128, 8, 2]); pbs = p.sbuf("pbs", [128, 8])
        p.dma("sp", pw[:].rearrange("p g i j -> p (g i) j"), pwd.ap().rearrange("g i p j -> p (g i) j"), writes=[pw])
        p.dma("sp", pp[:], ppd.ap(), writes=[pp])
        p.op("dve", lambda e: e.tensor_tensor(pbs[:], pp[:, :, 0], pp[:, :, 1], ALU.mult), reads=[pp], writes=[pbs])
        assert 8 * PASS_N >= 2 * NT
        tmpg = TT("tmpg", SaT.h[:].rearrange("p a b -> p (a b)")[:, 0:2 * NT].rearrange("p (a b) -> p a b", a=2))
        alias_views += [pw, tmpg]
        for g_ in range(4):
            for (off, n, kind) in tiles:
                for jb in range(2):
                    ps = pm.next()
                    for ib in range(2):
                        p.op("pe", lambda e: e.matmul(ps[:, 0:n], pw[:, g_, ib, jb * 128:(jb + 1) * 128], big[:, 8 + 2 * g_ + ib, off:off + n],
                                                      start=(ib == 0), stop=(ib == 1)), reads=[pw, big], writes=[ps], skip_self=True)
                    ob = 2 * g_ + jb
                    p.op("act", lambda e: e.activation(out=tmpg[:, jb, off:off + n], in_=ps[:, 0:n], func=AF.Identity, scale=pp[:, ob, 0:1], bias=pbs[:, ob:ob + 1]),
                         reads=[ps, pp, pbs], writes=[tmpg])
            for jb in range(2):
                p.op("dve", lambda e: e.tensor_copy(big[:, 8 + 2 * g_ + jb, :], tmpg[:, jb, :]), reads=[tmpg], writes=[big])
    xrow = Ring(xrows)
    for m in range(16):
        wb = wring.next()
        p.dma("sp", wb[:], wod.ap()[m], writes=[wb])
        xr = xrow.next()
        p.dma("sp", xr[:], xd.ap()[:, m, :], writes=[xr])
        for (off, n, kind) in tiles:
            ps = pm.next()
            for kc in range(KC):
                p.op("pe", lambda e: e.matmul(ps[:, 0:n], wb[:, kc * 128:(kc + 1) * 128], big[:, kc, off:off + n],
                                              start=(kc == 0), stop=(kc == KC - 1)), reads=[wb, big], writes=[ps], skip_self=True)
            g1 = V["g1x"] if kind == "x" else V["g1c"]
            p.op("dve", lambda e: e.scalar_tensor_tensor(out=xr[:, off:off + n], in0=ps[:, 0:n], scalar=g1[:, m:m + 1],
                                                         in1=xr[:, off:off + n], op0=ALU.mult, op1=ALU.add),
                 reads=[ps, g1, xr.r(off)], writes=[xr.r(off)])
        p.dma("pool", x1d.ap()[:, m, :], xr[:], reads=[xr], writes=[x1d])
    PN = PASS_N
    def _view(nm, i):
        return TT(nm, bigf.h[:, i * KC * PN:(i + 1) * KC * PN].rearrange("p (c t) -> p c t", c=KC))
    xp = _view("xp", 0); h2 = _view("h2", 1); acc = _view("acc", 2)
    scr = {"sq": acc, "ps": p.psum("dn_ps", [128, 512]), "rt": p.sbuf("dn_rt", [128, PN]), "rstd": p.sbuf("dn_rstd", [128, PN])}
    qT = acc
    kT = p.sbuf("kT", [128, 16, 128])
    p.dma("sp", kT[:].rearrange("p a b -> p a b"), kd.ap().rearrange("a p b -> p a b"), writes=[kT])
    Stm = p.sbuf("Stm", [128, 16, 128]); top = p.sbuf("top", [128, 16, 24]); tmpS = p.sbuf("tmpS", [128, 128]); tmpS2 = p.sbuf("tmpS2", [128, 128])
    cand = p.sbuf("cand", [128, 576]); cand2 = p.sbuf("cand2", [128, 576]); c24 = p.sbuf("c24", [128, 24])
    st = {nm: p.sbuf("st_" + nm, [128, 8]) for nm in ["thr", "ncmax", "Z", "kap", "t1"]}
    e16 = p.sbuf("e16", [128, 16]); diag = Ring([p.sbuf(f"diag{i}", [128, 128]) for i in range(2)])
    actS = Ring([p.sbuf(f"actS{i}", [128, PN]) for i in range(2)])
    dS = Ring([p.sbuf(f"dS{i}", [128, PN]) for i in range(2)])
    eS = Ring([p.sbuf(f"eS{i}", [128, PN]) for i in range(2)])
    wS = Ring([p.sbuf(f"wS{i}", [128, PN]) for i in range(2)])
    w2S = Ring([p.sbuf(f"w2S{i}", [128, PN]) for i in range(2)])
    Wacc = Ring([p.sbuf(f"Wacc{i}", [128, PN]) for i in range(2)])
    Gt = Ring([p.sbuf(f"Gt{i}", [128, PN]) for i in range(2 * GA)])
    for (p0, pn) in passes[:_DBG_NP]:
        p.dma("sp", xp[:, :, 0:pn], x1d.ap()[:, :, p0:p0 + pn], reads=[x1d], writes=[xp] + ([big, h2, acc] + xrows + [kbc, SaT, SbT] + alias_views if p0 == 0 else []))
        segs = []
        if p0 < NX:
            segs.append((0, min(pn, NX - p0), "x"))
        if p0 + pn > NX:
            s0 = max(0, NX - p0)
            segs.append((s0, pn - s0, "c"))
        for (s0, sn, kind) in segs:
            emit_norm_mod(p, xp, h2, KC, s0, sn, gpx if kind == "x" else gpc, V["sh2x"] if kind == "x" else V["sh2c"], ones, scr, None, None)
        for hs in range(16):
            wb = wring.next()
            p.dma("sp", wb[:], pqd.ap()[hs], writes=[wb])
            ps = pm.next()
            for kc in range(KC):
                p.op("pe", lambda e: e.matmul(ps[:, 0:pn], wb[:, kc * 128:(kc + 1) * 128], h2[:, kc, 0:pn],
                                              start=(kc == 0), stop=(kc == KC - 1)), reads=[wb, h2], writes=[ps], skip_self=True)
            p.op("dve", lambda e: e.tensor_copy(qT[:, hs, 0:pn], ps[:, 0:pn]), reads=[ps], writes=[qT.r(hs)])
        for hs in range(16):
            ps = pm.next()
            p.op("pe", lambda e: e.matmul(ps[:, 0:pn], kT[:, hs, :], qT[:, hs, 0:pn], start=True, stop=True),
                 reads=[kT, qT.r(hs)], writes=[ps], skip_self=True)
            dst = SaT if hs % 2 == 0 else SbT
            p.op("act", lambda e: e.activation(out=dst[:, hs // 2, 0:pn], in_=ps[:, 0:pn], func=AF.Identity), reads=[ps], writes=[dst.r(hs // 2)])
        t0 = 0
        while t0 < pn:
            nt = min(128, pn - t0)
            for hs in range(16):
                ps = pm.next()
                p.op("pe", lambda e: e.matmul(ps[0:nt, 0:128], qT[:, hs, t0:t0 + nt], kT[:, hs, :], start=True, stop=True),
                     reads=[kT, qT.r(hs)], writes=[ps], skip_self=True)
                p.op("act", lambda e: e.activation(out=Stm[0:nt, hs, :], in_=ps[0:nt, 0:128], func=AF.Identity), reads=[ps], writes=[Stm.r(hs)])
                p.op("dve", lambda e: e.max(top[0:nt, hs, 0:8], Stm[0:nt, hs, :]), reads=[Stm.r(hs)], writes=[top.r(hs)])
                p.op("dve", lambda e: e.match_replace(tmpS[0:nt, :], top[0:nt, hs, 0:8], Stm[0:nt, hs, :], NEG), reads=[Stm.r(hs), top.r(hs)], writes=[tmpS])
                p.op("dve", lambda e: e.max(top[0:nt, hs, 8:16], tmpS[0:nt, :]), reads=[tmpS], writes=[top.r(hs)])
                p.op("dve", lambda e: e.match_replace(tmpS2[0:nt, :], top[0:nt, hs, 8:16], tmpS[0:nt, :], NEG), reads=[tmpS, top.r(hs)], writes=[tmpS2])
                p.op("dve", lambda e: e.max(top[0:nt, hs, 16:24], tmpS2[0:nt, :]), reads=[tmpS2], writes=[top.r(hs)])
            for h in range(8):
                p.op("dve", lambda e: e.tensor_tensor(cand[0:nt, :].rearrange("p (i j) -> p i j", i=24),
                                                      top[0:nt, 2 * h, :].unsqueeze(2).broadcast_to([nt, 24, 24]),
                                                      top[0:nt, 2 * h + 1, :].unsqueeze(1).broadcast_to([nt, 24, 24]), ALU.add),
                     reads=[top.r(2 * h), top.r(2 * h + 1)], writes=[cand])
                p.op("dve", lambda e: e.max(c24[0:nt, 0:8], cand[0:nt, :]), reads=[cand], writes=[c24])
                p.op("dve", lambda e: e.match_replace(cand2[0:nt, :], c24[0:nt, 0:8], cand[0:nt, :], NEG), reads=[cand, c24], writes=[cand2])
                p.op("dve", lambda e: e.max(c24[0:nt, 8:16], cand2[0:nt, :]), reads=[cand2], writes=[c24])
                p.op("dve", lambda e: e.match_replace(cand[0:nt, :], c24[0:nt, 8:16], cand2[0:nt, :], NEG), reads=[cand2, c24], writes=[cand])
                p.op("dve", lambda e: e.max(c24[0:nt, 16:24], cand[0:nt, :]), reads=[cand], writes=[c24])
                p.op("dve", lambda e: e.tensor_tensor(st["thr"][0:nt, h:h + 1], c24[0:nt, 15:16], c24[0:nt, 16:17], ALU.add), reads=[c24], writes=[st["thr"]])
                p.op("dve", lambda e: e.tensor_scalar(out=st["thr"][0:nt, h:h + 1], in0=st["thr"][0:nt, h:h + 1], scalar1=0.5, scalar2=None, op0=ALU.mult), reads=[st["thr"]], writes=[st["thr"]])
                p.op("dve", lambda e: e.tensor_scalar(out=st["ncmax"][0:nt, h:h + 1], in0=c24[0:nt, 0:1], scalar1=-1.0, scalar2=None, op0=ALU.mult), reads=[c24], writes=[st["ncmax"]])
                p.op("act", lambda e: e.activation(out=e16[0:nt, :], in_=c24[0:nt, 0:16], func=AF.Exp, bias=st["ncmax"][0:nt, h:h + 1],
                                                   accum_out=st["Z"][0:nt, h:h + 1]), reads=[c24, st["ncmax"]], writes=[e16, st["Z"]])
                p.op("act", lambda e: e.activation(out=st["t1"][0:nt, h:h + 1], in_=st["thr"][0:nt, h:h + 1], func=AF.Exp, bias=st["ncmax"][0:nt, h:h + 1]),
                     reads=[st["thr"], st["ncmax"]], writes=[st["t1"]])
                p.op("dve", lambda e: e.reciprocal(st["kap"][0:nt, h:h + 1], st["Z"][0:nt, h:h + 1]), reads=[st["Z"]], writes=[st["kap"]])
                p.op("dve", lambda e: e.tensor_tensor(st["kap"][0:nt, h:h + 1], st["kap"][0:nt, h:h + 1], st["t1"][0:nt, h:h + 1], ALU.mult), reads=[st["kap"], st["t1"]], writes=[st["kap"]])
                for which in ("thr", "kap"):
                    dg = diag.next()
                    p.op("dve", lambda e: e.tensor_scalar(out=dg[0:nt, 0:nt], in0=ident[0:nt, 0:nt], scalar1=st[which][0:nt, h:h + 1], scalar2=None, op0=ALU.mult),
                         reads=[ident, st[which]], writes=[dg])
                    ps = pbc.next()
                    p.op("pe", lambda e: e.matmul(ps[:, 0:nt], ones[0:nt, :], dg[0:nt, 0:nt], start=True, stop=True), reads=[ones, dg], writes=[ps], skip_self=True)
                    if which == "thr":
                        p.op("dve", lambda e: e.tensor_tensor(SaT[:, h, t0:t0 + nt], SaT[:, h, t0:t0 + nt], ps[:, 0:nt], ALU.subtract),
                             reads=[ps, SaT.r(h)], writes=[SaT.r(h)])
                    else:
                        p.op("act", lambda e: e.activation(out=kbc[:, h, t0:t0 + nt], in_=ps[:, 0:nt], func=AF.Identity), reads=[ps], writes=[kbc.r(h)])
            t0 += nt
        first_group = True
        grp = []
        for a in range(_DBG_NA):
            ub = wring.next()
            p.dma("sp", ub[:], ud.ap()[a], writes=[ub])
            vb = vring.next()
            p.dma("sp", vb[:], vd.ap()[a], writes=[vb])
            psa = pact.next()
            for kc in range(KC):
                p.op("pe", lambda e: e.matmul(psa[:, 0:pn], ub[:, kc * 128:(kc + 1) * 128], h2[:, kc, 0:pn],
                                              start=(kc == 0), stop=(kc == KC - 1)), reads=[ub, h2], writes=[psa], skip_self=True)
            aS = actS.next()
            p.op("act", lambda e: e.activation(out=aS[:, 0:pn], in_=psa[:, 0:pn], func=AF.Gelu), reads=[psa], writes=[aS])
            wa = Wacc.next()
            for h in range(8):
                psb = pbc.next()
                p.op("pe", lambda e: e.matmul(psb[:, 0:pn], ident[:, a:a + 1].broadcast_to([128, 128]), SaT[:, h, 0:pn], start=True, stop=True),
                     reads=[ident, SaT.r(h)], writes=[psb], skip_self=True)
                d_ = dS.next(); e_ = eS.next(); w_ = wS.next()
                p.op("dve", lambda e: e.tensor_tensor(d_[:, 0:pn], psb[:, 0:pn], SbT[:, h, 0:pn], ALU.add), reads=[psb, SbT.r(h)], writes=[d_])
                p.op("act", lambda e: e.activation(out=e_[:, 0:pn], in_=d_[:, 0:pn], func=AF.Exp), reads=[d_], writes=[e_])
                p.op("dve", lambda e: e.scalar_tensor_tensor(out=w_[:, 0:pn], in0=d_[:, 0:pn], scalar=0.0, in1=e_[:, 0:pn], op0=ALU.is_gt, op1=ALU.mult),
                     reads=[d_, e_], writes=[w_])
                if h == 0:
                    p.op("pool", lambda e: e.tensor_tensor(wa[:, 0:pn], w_[:, 0:pn], kbc[:, h, 0:pn], ALU.mult), reads=[w_, kbc.r(h)], writes=[wa])
                else:
                    w2 = w2S.next()
                    p.op("pool", lambda e: e.tensor_tensor(w2[:, 0:pn], w_[:, 0:pn], kbc[:, h, 0:pn], ALU.mult), reads=[w_, kbc.r(h)], writes=[w2])
                    p.op("pool", lambda e: e.tensor_tensor(wa[:, 0:pn], wa[:, 0:pn], w2[:, 0:pn], ALU.add), reads=[wa, w2], writes=[wa])
            gt = Gt.next()
            p.op("pool", lambda e: e.tensor_tensor(gt[:, 0:pn], aS[:, 0:pn], wa[:, 0:pn], ALU.mult), reads=[aS, wa], writes=[gt])
            grp.append((vb, gt))
            if len(grp) == GA:
                for db in range(16):
                    for gi, (vb_, gt_) in enumerate(grp):
                        p.op("pe", lambda e: e.matmul(pso[:, 0:pn], vb_[:, db * 128:(db + 1) * 128], gt_[:, 0:pn],
                                                      start=(gi == 0), stop=(gi == GA - 1)), reads=[vb_, gt_], writes=[pso], skip_self=True)
                    if first_group:
                        p.op("dve", lambda e: e.tensor_copy(acc[:, db, 0:pn], pso[:, 0:pn]), reads=[pso], writes=[acc.r(db)])
                    else:
                        p.op("dve", lambda e: e.tensor_tensor(acc[:, db, 0:pn], acc[:, db, 0:pn], pso[:, 0:pn], ALU.add), reads=[pso, acc.r(db)], writes=[acc.r(db)])
                first_group = False
                grp = []
        for (s0, sn, kind) in segs:
            g2 = V["g2x"] if kind == "x" else V["g2c"]
            for db in range(16):
                p.op("dve", lambda e: e.scalar_tensor_tensor(out=xp[:, db, s0:s0 + sn], in0=acc[:, db, s0:s0 + sn], scalar=g2[:, db:db + 1],
                                                             in1=xp[:, db, s0:s0 + sn], op0=ALU.mult, op1=ALU.add),
                     reads=[acc.r(db), g2, xp], writes=[xp])
        if final_g is not None:
            emit_norm_mod(p, xp, h2, KC, 0, pn, fg, None, ones, scr, None, None)
            p.dma("pool", outd.ap()[:, :, p0:p0 + pn], h2[:, :, 0:pn], reads=[h2], writes=[outd])
        else:
            p.dma("pool", outd.ap()[:, :, p0:p0 + pn], xp[:, :, 0:pn], reads=[xp], writes=[outd])
    in_maps = []
    idn = np.eye(128, dtype=np.float32)
    for i in range(NCORES):
        mp = {"catT": catT_cores[i], "xT": xT_cores[i], "mv": mv_cores[i], "wo": woblk, "pq": pqblk, "keysT": keysT,
              "uT": uT, "vt": vt, "ident": idn}
        if final_g is not None:
            mp["fg"] = final_g
        if pool is not None:
            mp["pw"] = pool[0]; mp["pp"] = pool[1]
        in_maps.append(mp)
    print("phase_D: n_inst", p.n_inst, "n_wait", p.n_wait, flush=True)
    res = _run(p, in_maps)
    return [r["o"] for r in res]


TWO_PI = 6.283185307179586
I32 = mybir.dt.int32


def hyena_feats(L):
    t = np.arange(L, dtype=np.float32)
    t_norm = (t / np.float32(max(L - 1, 1))).astype(np.float32)
    bands = np.linspace(1e-4, 15, 16, dtype=np.float32)
    ang = (np.float32(2.0 * np.pi / L) * t[:, None] * bands[None, :]).astype(np.float32)
    feats = np.concatenate([t_norm[:, None], np.cos(ang), -np.sin(ang)], axis=-1).astype(np.float32)
    return feats, t_norm


def phase_C(P3_cores, cw_cores, cb_cores, w3_cores, dl_cores, hb_cores, vg_cores, u_cores, wsT_cores, bs_cores,
            w1, b1, w2, b2, freq, LX, LC):
    LT = LX + LC
    p = Prog()
    P3d = p.dram("P3", [3, 128, LT], "ExternalInput")
    cwd = p.dram("cw", [128, 9], "ExternalInput")
    cbd = p.dram("cb", [128, 3], "ExternalInput")
    w3d = p.dram("w3", [64, 4 * 128], "ExternalInput")
    dld = p.dram("dl", [128, 4], "ExternalInput")
    hbd = p.dram("hb", [128, 2], "ExternalInput")
    vgd = p.dram("vg", [LT // 128, 128, 128], "ExternalInput")
    ud = p.dram("u", [128, LT], "ExternalInput")
    wsd = p.dram("wsT", [128, 128], "ExternalInput")
    bsd = p.dram("bs", [1, 128], "ExternalInput")
    w1d = p.dram("w1", [33, 64], "ExternalInput"); w2d = p.dram("w2", [64, 64], "ExternalInput")
    mlpd = p.dram("mlpv", [64, 3], "ExternalInput")
    seqs = [("x", 0, LX), ("c", LX, LC)]
    fd = {}; tnd = {}
    for nm, o, L in seqs:
        fd[nm] = p.dram("f_" + nm, [33, 2 * L], "ExternalInput")
        tnd[nm] = p.dram("tn_" + nm, [1, 2 * L], "ExternalInput")
    outd = p.dram("o", [2, 128, LT], "ExternalOutput")
    cw = p.sbuf("cw_s", [128, 9]); cb = p.sbuf("cb_s", [128, 3]); w3 = p.sbuf("w3_s", [64, 512]); dl = p.sbuf("dl_s", [128, 4])
    hb = p.sbuf("hb_s", [128, 2]); ws = p.sbuf("ws_s", [128, 128]); bsb = p.sbuf("bs_s", [128, 128])
    w1s = p.sbuf("w1_s", [33, 64]); w2s = p.sbuf("w2_s", [64, 64]); mlpv = p.sbuf("mlpv_s", [64, 3]); fq = p.sbuf("fq_s", [64, 1])
    for t, dd in [(cw, cwd), (cb, cbd), (w3, w3d), (dl, dld), (hb, hbd), (ws, wsd), (w1s, w1d), (w2s, w2d), (mlpv, mlpd)]:
        p.dma("sp", t[:], dd.ap(), writes=[t])
    p.dma("sp", bsb[:], bsd.ap().broadcast_to([128, 128]), writes=[bsb])
    p.op("dve", lambda e: e.tensor_scalar(out=fq[:], in0=mlpv[:, 2:3], scalar1=1.0 / TWO_PI, scalar2=None, op0=ALU.mult), reads=[mlpv], writes=[fq])
    nad = p.sbuf("nad", [128, 4])
    p.op("act", lambda e: e.activation(out=nad[:], in_=dl[:], func=AF.Abs), reads=[dl], writes=[nad])
    p.op("dve", lambda e: e.tensor_scalar(out=nad[:], in0=nad[:], scalar1=-1.0, scalar2=None, op0=ALU.mult), reads=[nad], writes=[nad])
    pg = Ring([p.psum(f"pg{i}", [128, 512]) for i in range(2)])
    vgr = Ring([p.sbuf(f"vgc{i}", [128, 2, 128]) for i in range(2)])
    ur = Ring([p.sbuf(f"uc{i}", [128, 256]) for i in range(2)])
    tr = Ring([p.sbuf(f"tc{i}", [128, 256]) for i in range(2)])
    nch = LT // 128
    for g0 in range(0, nch, 2):
        vgt = vgr.next(); ut = ur.next(); tt = tr.next(); ps = pg.next()
        p.dma("sp", vgt[:], vgd.ap()[g0:g0 + 2].rearrange("a q c -> q a c"), writes=[vgt])
        p.dma("sp", ut[:], ud.ap()[:, g0 * 128:(g0 + 2) * 128], writes=[ut])
        for k in range(2):
            p.op("pe", lambda e: e.matmul(ps[:, k * 128:(k + 1) * 128], vgt[:, k, :], ws[:], start=True, stop=True),
                 reads=[vgt, ws], writes=[ps.r(k)], skip_self=True)
        p.op("dve", lambda e: e.tensor_tensor(tt[:].rearrange("p (a b) -> p a b", a=2), ps[:, 0:256].rearrange("p (a b) -> p a b", a=2),
                                              bsb[:].unsqueeze(1).broadcast_to([128, 2, 128]), ALU.add), reads=[ps, bsb], writes=[tt])
        p.op("pool", lambda e: e.tensor_tensor(tt[:], tt[:], ut[:], ALU.mult), reads=[tt, ut], writes=[tt])
        p.dma("pool", outd.ap()[1, :, g0 * 128:(g0 + 2) * 128], tt[:], reads=[tt], writes=[outd.r(("yb", g0))])
    zb = p.sbuf("zb", [128, LT]); yb_ = p.sbuf("ybuf", [128, LT]); tmp = yb_; xg = p.sbuf("xg", [128, LT])
    kk = {"x": p.sbuf("kk_x", [128, 2 * LX]), "c": p.sbuf("kk_c", [128, 2 * LC])}
    pmm = Ring([p.psum(f"pc{i}", [128, 512]) for i in range(4)])
    ft = Ring([p.sbuf(f"ft{i}", [33, 512]) for i in range(2)])
    tnb = Ring([p.sbuf(f"tnb{i}", [128, 512]) for i in range(2)])
    hA = Ring([p.sbuf(f"hA{i}", [64, 512]) for i in range(2)])
    hI = p.sbuf("hI", [64, 512], I32); hF = p.sbuf("hF", [64, 512])
    dec = Ring([p.sbuf(f"dec{i}", [128, 512]) for i in range(2)])
    absd = p.sbuf("absd", [128, 512]); part = p.sbuf("part", [128, 64]); den = p.sbuf("den", [128, 1]); rden = p.sbuf("rden", [128, 1])

    def sin_layer(ps, n, bcol, out):
        p.op("dve", lambda e: e.tensor_scalar(out=out[:, 0:n], in0=ps[0:64, 0:n], scalar1=mlpv[:, bcol:bcol + 1], scalar2=fq[:, 0:1], op0=ALU.add, op1=ALU.mult),
             reads=[ps, mlpv, fq], writes=[out])
        p.op("dve", lambda e: e.tensor_copy(hI[:, 0:n], out[:, 0:n]), reads=[out], writes=[hI])
        p.op("dve", lambda e: e.tensor_copy(hF[:, 0:n], hI[:, 0:n]), reads=[hI], writes=[hF])
        p.op("dve", lambda e: e.tensor_tensor(out[:, 0:n], out[:, 0:n], hF[:, 0:n], ALU.subtract), reads=[out, hF], writes=[out])
        p.op("act", lambda e: e.activation(out=out[:, 0:n], in_=out[:, 0:n], func=AF.Sin, scale=TWO_PI * (1 - 2e-7)), reads=[out], writes=[out])

    def dwconv3(j):
        p.dma("sp", tmp[:], P3d.ap()[j], writes=[tmp])

    def conv_into(dst, j):
        dwconv3(j)
        for nm, o, L in seqs:
            p.op("dve", lambda e: e.tensor_scalar(out=dst[:, o:o + L], in0=tmp[:, o:o + L], scalar1=cw[:, 3 * j + 1:3 * j + 2], scalar2=cb[:, j:j + 1], op0=ALU.mult, op1=ALU.add),
                 reads=[tmp, cw, cb], writes=[dst])
            p.op("dve", lambda e: e.scalar_tensor_tensor(out=dst[:, o + 1:o + L], in0=tmp[:, o:o + L - 1], scalar=cw[:, 3 * j:3 * j + 1], in1=dst[:, o + 1:o + L], op0=ALU.mult, op1=ALU.add),
                 reads=[tmp, cw, dst], writes=[dst])
            p.op("dve", lambda e: e.scalar_tensor_tensor(out=dst[:, o:o + L - 1], in0=tmp[:, o + 1:o + L], scalar=cw[:, 3 * j + 2:3 * j + 3], in1=dst[:, o:o + L - 1], op0=ALU.mult, op1=ALU.add),
                 reads=[tmp, cw, dst], writes=[dst])

    conv_into(zb, 0)
    for n in range(2):
        conv_into(xg, n + 1)
        for nm, o, L in seqs:
            K_ = kk[nm]
            TS = min(512, L)
            ntile = 2 * L // TS
            for ti in range(ntile):
                c0 = ti * TS
                dirn = 1 if c0 < L else 0
                f = ft.next(); tb = tnb.next()
                p.dma("sp", f[:, 0:TS], fd[nm].ap()[:, c0:c0 + TS], writes=[f])
                p.dma("sp", tb[:, 0:TS], tnd[nm].ap()[:, c0:c0 + TS].broadcast_to([128, TS]), writes=[tb])
                ps = pmm.next()
                p.op("pe", lambda e: e.matmul(ps[0:64, 0:TS], w1s[:], f[:, 0:TS], start=True, stop=True), reads=[w1s, f], writes=[ps], skip_self=True)
                h1 = hA.next()
                sin_layer(ps, TS, 0, h1)
                ps = pmm.next()
                p.op("pe", lambda e: e.matmul(ps[0:64, 0:TS], w2s[:], h1[:, 0:TS], start=True, stop=True), reads=[w2s, h1], writes=[ps], skip_self=True)
                h2 = hA.next()
                sin_layer(ps, TS, 1, h2)
                ps = pmm.next()
                col = (dirn * 2 + n) * 128
                p.op("pe", lambda e: e.matmul(ps[:, 0:TS], w3[:, col:col + 128], h2[:, 0:TS], start=True, stop=True), reads=[w3, h2], writes=[ps], skip_self=True)
                dc = dec.next()
                p.op("act", lambda e: e.activation(out=dc[:, 0:TS], in_=tb[:, 0:TS], func=AF.Exp, scale=nad[:, dirn * 2 + n:dirn * 2 + n + 1]), reads=[tb, nad], writes=[dc])
                p.op("dve", lambda e: e.tensor_tensor(K_[:, c0:c0 + TS], ps[:, 0:TS], dc[:, 0:TS], ALU.mult), reads=[ps, dc], writes=[K_.r(ti)])
                p.op("act", lambda e: e.activation(out=absd[:, 0:TS], in_=K_[:, c0:c0 + TS], func=AF.Abs, accum_out=part[:, ti:ti + 1]), reads=[K_.r(ti)], writes=[absd, part.r(ti)])
            p.op("dve", lambda e: e.reduce_sum(den[:], part[:, 0:ntile], AX.X), reads=[part], writes=[den])
            p.op("dve", lambda e: e.reciprocal(rden[:], den[:]), reads=[den], writes=[rden])
            Y = yb_
            p.op("dve", lambda e: e.tensor_scalar(out=Y[:, o:o + L], in0=K_[:, L:2 * L], scalar1=zb[:, o:o + 1], scalar2=None, op0=ALU.mult),
                 reads=[K_, zb], writes=[Y.r(nm)])
            for s in range(1, L):
                p.op("dve", lambda e: e.scalar_tensor_tensor(out=Y[:, o:o + L], in0=K_[:, L - s:2 * L - s], scalar=zb[:, o + s:o + s + 1], in1=Y[:, o:o + L], op0=ALU.mult, op1=ALU.add),
                     reads=[K_, zb, Y.r(nm)], writes=[Y.r(nm)])
            p.op("dve", lambda e: e.tensor_scalar(out=Y[:, o:o + L], in0=Y[:, o:o + L], scalar1=rden[:, 0:1], scalar2=None, op0=ALU.mult), reads=[Y.r(nm), rden], writes=[Y.r(nm)])
            p.op("dve", lambda e: e.scalar_tensor_tensor(out=Y[:, o:o + L], in0=zb[:, o:o + L], scalar=hb[:, n:n + 1], in1=Y[:, o:o + L], op0=ALU.mult, op1=ALU.add),
                 reads=[zb, hb, Y.r(nm)], writes=[Y.r(nm)])
            p.op("dve", lambda e: e.tensor_tensor(zb[:, o:o + L], xg[:, o:o + L], Y[:, o:o + L], ALU.mult), reads=[xg, Y.r(nm)], writes=[zb])
    p.dma("pool", outd.ap()[0], zb[:], reads=[zb], writes=[outd.r("z")])
    consts = {}
    for nm, o, L in seqs:
        feats, tn = hyena_feats(L)
        pos = np.concatenate([np.arange(L - 1, -1, -1), np.arange(L)])
        consts["f_" + nm] = np.ascontiguousarray(feats[pos].T)
        consts["tn_" + nm] = np.ascontiguousarray(tn[pos][None, :])
    mlpvh = np.ascontiguousarray(np.stack([b1, b2, freq], axis=1).astype(np.float32))
    in_maps = []
    for i in range(NCORES):
        mp = {"P3": P3_cores[i], "cw": cw_cores[i], "cb": cb_cores[i], "w3": w3_cores[i], "dl": dl_cores[i], "hb": hb_cores[i],
              "vg": vg_cores[i], "u": u_cores[i], "wsT": wsT_cores[i], "bs": bs_cores[i], "w1": w1, "w2": w2, "mlpv": mlpvh}
        mp.update(consts)
        in_maps.append(mp)
    print("phase_C: n_inst", p.n_inst, "n_wait", p.n_wait, flush=True)
    res = _run(p, in_maps)
    return [r["o"] for r in res]


def phase_F(xr_cores, gate_cores, xp_cores, cw_cores, wa_cores, wx_cores, vec_cores, selw_cores, btab_cores, LX, LC):
    LT = LX + LC
    p = Prog()
    xrd = p.dram("xr", [128, LT], "ExternalInput"); gd = p.dram("gate", [128, LX], "ExternalInput"); xpd = p.dram("xp", [128, LX], "ExternalInput")
    cwd = p.dram("cw", [128, 5], "ExternalInput")
    wad = p.dram("wa", [2, 128, 128], "ExternalInput"); wxd = p.dram("wx", [2, 128, 128], "ExternalInput")
    vd = p.dram("vec", [128, 6], "ExternalInput")
    sd = p.dram("selw", [128, 4], "ExternalInput"); bd = p.dram("btab", [128, 4, 16], "ExternalInput")
    outd = p.dram("o", [2, 128, LX], "ExternalOutput")
    cw = p.sbuf("cw_s", [128, 5]); wa = p.sbuf("wa_s", [128, 2, 128]); wx = p.sbuf("wx_s", [128, 2, 128]); vec = p.sbuf("vec_s", [128, 6])
    selw = p.sbuf("selw_s", [128, 4]); btab = p.sbuf("btab_s", [128, 4, 16])
    for t, dd in [(cw, cwd), (vec, vd), (selw, sd), (btab, bd)]:
        p.dma("sp", t[:], dd.ap(), writes=[t])
    p.dma("sp", wa[:], wad.ap().rearrange("d i j -> i d j"), writes=[wa])
    p.dma("sp", wx[:], wxd.ap().rearrange("d i j -> i d j"), writes=[wx])
    ones = emit_consts(p)
    raw = p.sbuf("raw", [128, LT]); xr = p.sbuf("xrs", [128, LT])
    A = p.sbuf("A", [128, LX]); Bv = p.sbuf("Bv", [128, LX]); Hf = p.sbuf("Hf", [128, LX]); Hb = p.sbuf("Hb", [128, LX])
    Hc = p.sbuf("Hc", [128, 2, LC])
    seqs = [("c", LX, LC), ("x", 0, LX)]
    p.dma("sp", raw[:], xrd.ap(), writes=[raw])
    for nm, o, L in seqs:
        p.op("dve", lambda e: e.tensor_scalar(out=xr[:, o:o + L], in0=raw[:, o:o + L], scalar1=cw[:, 1:2], scalar2=cw[:, 4:5], op0=ALU.mult, op1=ALU.add), reads=[raw, cw], writes=[xr])
        p.op("dve", lambda e: e.scalar_tensor_tensor(out=xr[:, o + 1:o + L], in0=raw[:, o:o + L - 1], scalar=cw[:, 0:1], in1=xr[:, o + 1:o + L], op0=ALU.mult, op1=ALU.add), reads=[raw, cw, xr], writes=[xr])
        p.op("dve", lambda e: e.scalar_tensor_tensor(out=xr[:, o:o + L - 1], in0=raw[:, o + 1:o + L], scalar=cw[:, 2:3], in1=xr[:, o:o + L - 1], op0=ALU.mult, op1=ALU.add), reads=[raw, cw, xr], writes=[xr])
        p.op("dve", lambda e: e.scalar_tensor_tensor(out=xr[:, o:o + L - 2], in0=raw[:, o + 2:o + L], scalar=cw[:, 3:4], in1=xr[:, o:o + L - 2], op0=ALU.mult, op1=ALU.add), reads=[raw, cw, xr], writes=[xr])
    m8 = p.sbuf("m8", [128, 2])
    p.op("act", lambda e: e.activation(out=m8[:], in_=vec[:, 4:6], func=AF.Exp, scale=-1.0), reads=[vec], writes=[m8])
    p.op("act", lambda e: e.activation(out=m8[:], in_=m8[:], func=AF.Ln, bias=1.0), reads=[m8], writes=[m8])
    p.op("dve", lambda e: e.tensor_scalar(out=m8[:], in0=m8[:], scalar1=-8.0, scalar2=None, op0=ALU.mult), reads=[m8], writes=[m8])
    pr = Ring([p.psum(f"pr{i}", [128, 512]) for i in range(4)])
    rT = Ring([p.sbuf(f"rT{i}", [128, 512]) for i in range(1)]); iT = Ring([p.sbuf(f"iT{i}", [128, 512]) for i in range(1)])
    a2 = Ring([p.sbuf(f"a2{i}", [128, 512]) for i in range(1)])
    for d_ in range(2):
        H = Hf if d_ == 0 else Hb
        for nm, o, L in seqs:
            for c0 in range(0, L, 512):
                n = min(512, L - c0)
                ps1 = pr.next(); ps2 = pr.next()
                p.op("pe", lambda e: e.matmul(ps1[:, 0:n], wa[:, d_, :], xr[:, o + c0:o + c0 + n], start=True, stop=True), reads=[wa, xr], writes=[ps1], skip_self=True)
                p.op("pe", lambda e: e.matmul(ps2[:, 0:n], wx[:, d_, :], xr[:, o + c0:o + c0 + n], start=True, stop=True), reads=[wx, xr], writes=[ps2], skip_self=True)
                r_ = rT.next(); i_ = iT.next(); q_ = a2.next()
                p.op("act", lambda e: e.activation(out=r_[:, 0:n], in_=ps1[:, 0:n], func=AF.Sigmoid, bias=vec[:, d_:d_ + 1]), reads=[ps1, vec], writes=[r_])
                p.op("act", lambda e: e.activation(out=i_[:, 0:n], in_=ps2[:, 0:n], func=AF.Sigmoid, bias=vec[:, 2 + d_:3 + d_]), reads=[ps2, vec], writes=[i_])
                p.op("act", lambda e: e.activation(out=A[:, c0:c0 + n], in_=r_[:, 0:n], func=AF.Exp, scale=m8[:, d_:d_ + 1]), reads=[r_, m8], writes=[A.r(c0)])
                p.op("dve", lambda e: e.tensor_tensor(q_[:, 0:n], A[:, c0:c0 + n], A[:, c0:c0 + n], ALU.mult), reads=[A.r(c0)], writes=[q_])
                p.op("dve", lambda e: e.tensor_scalar(out=q_[:, 0:n], in0=q_[:, 0:n], scalar1=-1.0, scalar2=1.0, op0=ALU.mult, op1=ALU.add), reads=[q_], writes=[q_])
                p.op("act", lambda e: e.activation(out=q_[:, 0:n], in_=q_[:, 0:n], func=AF.Sqrt), reads=[q_], writes=[q_])
                p.op("dve", lambda e: e.tensor_tensor(i_[:, 0:n], i_[:, 0:n], xr[:, o + c0:o + c0 + n], ALU.mult), reads=[i_, xr], writes=[i_])
                p.op("dve", lambda e: e.tensor_tensor(Bv[:, c0:c0 + n], i_[:, 0:n], q_[:, 0:n], ALU.mult), reads=[i_, q_], writes=[Bv.r(c0)])
            if nm == "c":
                dst = Hc[:, d_, :]; dreg = Hc
                init = 0.0
            else:
                dst = H[:, 0:L]; dreg = H
                init = Hc[:, 0, LC - 1:LC] if d_ == 0 else Hc[:, 1, 0:1]
            if d_ == 0:
                p.op("dve", lambda e: e.tensor_tensor_scan(dst, A[:, 0:L], Bv[:, 0:L], init, ALU.mult, ALU.add), reads=[A, Bv, Hc], writes=[dreg])
            else:
                p.op("dve", lambda e: e.tensor_tensor_scan(dst[:, ::-1], A[:, 0:L][:, ::-1], Bv[:, 0:L][:, ::-1], init, ALU.mult, ALU.add), reads=[A, Bv, Hc], writes=[dreg])
    p.dma("sp", A[:], gd.ap(), writes=[A])
    p.op("dve", lambda e: e.tensor_tensor(Hf[:], Hf[:], Hb[:], ALU.add), reads=[Hf, Hb], writes=[Hf])
    p.op("dve", lambda e: e.tensor_tensor(Hf[:], Hf[:], A[:], ALU.mult), reads=[Hf, A], writes=[Hf])
    p.dma("pool", outd.ap()[0], Hf[:], reads=[Hf], writes=[outd.r(0)])
    L = LX
    xpb = Bv; diff = Hb; accp = A
    csp = raw
    p.dma("sp", xpb[:], xpd.ap(), writes=[xpb])
    p.op("dve", lambda e: e.memset(csp[:, 0:9], 0.0), writes=[csp])
    p.op("dve", lambda e: e.tensor_tensor_scan(csp[:, 9:9 + L], ones[:, 0:1].broadcast_to([128, L]), xpb[:], 0.0, ALU.mult, ALU.add), reads=[ones, xpb, csp], writes=[csp])
    p.op("dve", lambda e: e.tensor_copy(csp[:, 9 + L:17 + L], csp[:, 8 + L:9 + L].broadcast_to([128, 8])), reads=[csp], writes=[csp])
    bacc = p.sbuf("bacc", [128, 16]); bt = p.sbuf("bt", [128, 16])
    for wi, w in enumerate((2, 4, 8, 16)):
        hf_ = w // 2
        p.op("dve", lambda e: e.tensor_tensor(diff[:], csp[:, 8 + hf_:8 + hf_ + L], csp[:, 8 - hf_:8 - hf_ + L], ALU.subtract), reads=[csp], writes=[diff])
        if wi == 0:
            p.op("dve", lambda e: e.tensor_scalar(out=accp[:], in0=diff[:], scalar1=selw[:, wi:wi + 1], scalar2=None, op0=ALU.mult), reads=[diff, selw], writes=[accp])
        else:
            p.op("dve", lambda e: e.scalar_tensor_tensor(out=accp[:], in0=diff[:], scalar=selw[:, wi:wi + 1], in1=accp[:], op0=ALU.mult, op1=ALU.add), reads=[diff, selw, accp], writes=[accp])
        for half, cols in ((0, slice(0, 8)), (1, slice(L - 8, L))):
            bs_ = slice(8 * half, 8 * half + 8)
            if wi == 0:
                p.op("dve", lambda e: e.tensor_tensor(bacc[:, bs_], diff[:, cols], btab[:, wi, bs_], ALU.mult), reads=[diff, btab], writes=[bacc])
            else:
                p.op("dve", lambda e: e.tensor_tensor(bt[:, bs_], diff[:, cols], btab[:, wi, bs_], ALU.mult), reads=[diff, btab], writes=[bt])
                p.op("dve", lambda e: e.tensor_tensor(bacc[:, bs_], bacc[:, bs_], bt[:, bs_], ALU.add), reads=[bacc, bt], writes=[bacc])
    p.op("dve", lambda e: e.tensor_copy(accp[:, 0:8], bacc[:, 0:8]), reads=[bacc], writes=[accp])
    p.op("dve", lambda e: e.tensor_copy(accp[:, L - 8:L], bacc[:, 8:16]), reads=[bacc], writes=[accp])
    p.op("dve", lambda e: e.tensor_tensor(accp[:], accp[:], xpb[:], ALU.subtract), reads=[accp, xpb], writes=[accp])
    p.dma("pool", outd.ap()[1], accp[:], reads=[accp], writes=[outd.r(1)])
    in_maps = []
    for i in range(NCORES):
        in_maps.append({"xr": xr_cores[i], "gate": gate_cores[i], "xp": xp_cores[i], "cw": cw_cores[i], "wa": wa_cores[i], "wx": wx_cores[i],
                        "vec": vec_cores[i], "selw": selw_cores[i], "btab": btab_cores[i]})
    res = _run(p, in_maps)
    return [r["o"] for r in res]


def _assemble(projs, NX, NCX):
    px = np.concatenate([o[:, :, :NX] for o in projs], axis=2)
    if NCX:
        pc = np.concatenate([o[:, :, NX:] for o in projs], axis=2)
        return np.concatenate([px, pc], axis=2)
    return px


def _token_shard(full, NX, NCX, LX):
    outs = []
    for i in range(NCORES):
        parts = [full[:, :, i * NX:(i + 1) * NX]]
        if NCX:
            parts.append(full[:, :, LX + i * NCX:LX + (i + 1) * NCX])
        outs.append(np.ascontiguousarray(np.concatenate(parts, axis=2).transpose(1, 0, 2)))
    return outs


def kernel(x, c, ctx, c_ctx, ada_w, ada_b, norm_mix, norm_ffn, norm_final,
           ev_w_in, ev_conv_w, ev_conv_b, hy_w1, hy_b1, hy_w2, hy_b2, hy_w3, hy_freq, hy_deltas, hy_bias,
           gm_norm, gm_ws, gm_bs, ev_w_out,
           od_w_in, od_conv_w, od_conv_b, lru_wa, lru_ba, lru_wx, lru_bx, lru_lam,
           pool_w, pool_b, pool_scale, od_w_out,
           peer_q, peer_keys, peer_u, peer_v):
    f32 = lambda a: np.ascontiguousarray(np.asarray(a, dtype=np.float32))
    x = f32(x)[0]; ctxa = f32(ctx)[0]
    LX, LC, NX, NCX = 8192, 256, 1024, 32
    mod = phase_A(f32(c), f32(c_ctx), f32(ada_w), f32(ada_b))

    def mvv(l, which, col):
        return vec_fm(mod[l][which * 2048:(which + 1) * 2048, col])

    def mvB(l):
        return np.ascontiguousarray(np.stack([vec_fm(f32(norm_mix)[l]), mvv(l, 0, 0), mvv(l, 1, 0), mvv(l, 0, 1), mvv(l, 1, 1)], axis=2))

    def mvD(l):
        return np.ascontiguousarray(np.stack([mvv(l, 2, 0), mvv(l, 2, 1), vec_fm(f32(norm_ffn)[l]), mvv(l, 3, 0), mvv(l, 4, 0),
                                              mvv(l, 3, 1), mvv(l, 4, 1), mvv(l, 5, 0), mvv(l, 5, 1)], axis=2))

    def peer_args(l):
        keysT = np.ascontiguousarray(f32(peer_keys)[l].reshape(16, 128, 128).transpose(0, 2, 1))
        uT = np.ascontiguousarray(f32(peer_u)[l].reshape(128, 128, 16, 128).transpose(0, 3, 2, 1).reshape(128, 128, 2048))
        vt = np.ascontiguousarray(f32(peer_v)[l].reshape(128, 128, 2048))
        return wblocks(f32(peer_q)[l]), keysT, uT, vt

    xTs = [fm(np.concatenate([x[i * NX:(i + 1) * NX], ctxa[i * NCX:(i + 1) * NCX]], 0)) for i in range(NCORES)]
    epi0 = ["copy"] * 24 + ["gelu"] * 8 + ["vg"] * 8
    projs = phase_B(xTs, [mvB(0)] * NCORES, wblocks(f32(ev_w_in)[0]), epi0, NX, NCX, gmn=vec_fm(f32(gm_norm)[0]))
    PT = _assemble(projs, NX, NCX)
    cwf = f32(ev_conv_w)[0]; cbf = f32(ev_conv_b)[0]; w3f = f32(hy_w3)[0]; dlf = f32(hy_deltas)[0]; hbf = f32(hy_bias)[0]
    P3c, cwc, cbc, w3c, dlc, hbc, vgc, uc, wsc, bsc = [], [], [], [], [], [], [], [], [], []
    for i in range(NCORES):
        blk = slice(128 * i, 128 * i + 128)
        P3c.append(np.ascontiguousarray(PT[[i, 8 + i, 16 + i]]))
        cwc.append(np.ascontiguousarray(np.stack([cwf[k, j * 1024 + 128 * i:j * 1024 + 128 * i + 128] for j in range(3) for k in range(3)], axis=1)))
        cbc.append(np.ascontiguousarray(np.stack([cbf[j * 1024 + 128 * i:j * 1024 + 128 * i + 128] for j in range(3)], axis=1)))
        cols = np.concatenate([np.arange(q * 1024 + 128 * i, q * 1024 + 128 * i + 128) for q in range(4)])
        w3c.append(np.ascontiguousarray(w3f[:, cols]))
        dlc.append(np.ascontiguousarray(dlf[cols].reshape(4, 128).T))
        hbc.append(np.ascontiguousarray(hbf[:, blk].T))
        vgc.append(np.ascontiguousarray(PT[32 + i].T.reshape((LX + LC) // 128, 128, 128)))
        uc.append(np.ascontiguousarray(PT[24 + i]))
        wsc.append(np.ascontiguousarray(f32(gm_ws)[0][i].T))
        bsc.append(np.ascontiguousarray(f32(gm_bs)[0][i][None, :]))
    couts = phase_C(P3c, cwc, cbc, w3c, dlc, hbc, vgc, uc, wsc, bsc, f32(hy_w1)[0], f32(hy_b1)[0], f32(hy_w2)[0], f32(hy_b2)[0],
                    f32(hy_freq)[0], LX, LC)
    CAT = np.concatenate([np.stack([o[0] for o in couts]), np.stack([o[1] for o in couts])], axis=0)
    cats = _token_shard(CAT, NX, NCX, LX)
    pq, keysT, uT, vt = peer_args(0)
    x1Ts = phase_D(cats, xTs, [mvD(0)] * NCORES, wblocks(f32(ev_w_out)[0]), pq, keysT, uT, vt, NX, NCX)
    del uT, vt
    epi1 = ["gelu"] * 8 + ["copy"] * 16
    projs1 = phase_B(x1Ts, [mvB(1)] * NCORES, wblocks(f32(od_w_in)[0]), epi1, NX, NCX)
    PT1 = _assemble(projs1, NX, NCX)
    ocw = f32(od_conv_w)[0]; ocb = f32(od_conv_b)[0]
    xrc, gc, xpc, cw1, wac, wxc, vcc, swc, btc = [], [], [], [], [], [], [], [], []
    for i in range(NCORES):
        blk = slice(128 * i, 128 * i + 128)
        xrc.append(np.ascontiguousarray(PT1[8 + i]))
        gc.append(np.ascontiguousarray(PT1[i][:, :LX]))
        xpc.append(np.ascontiguousarray(PT1[16 + i][:, :LX]))
        cw1.append(np.ascontiguousarray(np.stack([ocw[0, blk], ocw[1, blk], ocw[2, blk], ocw[3, blk], ocb[blk]], axis=1)))
        wac.append(np.ascontiguousarray(f32(lru_wa)[0][:, i]))
        wxc.append(np.ascontiguousarray(f32(lru_wx)[0][:, i]))
        vcc.append(np.ascontiguousarray(np.stack([f32(lru_ba)[0][0, blk], f32(lru_ba)[0][1, blk], f32(lru_bx)[0][0, blk], f32(lru_bx)[0][1, blk],
                                                  f32(lru_lam)[0][0, blk], f32(lru_lam)[0][1, blk]], axis=1)))
        g = i // 2
        sw = np.zeros((128, 4), np.float32); bt = np.zeros((128, 4, 16), np.float32)
        w = (2, 4, 8, 16)[g]; half = w // 2
        sw[:, g] = 1.0 / w
        tcols = np.concatenate([np.arange(8), np.arange(LX - 8, LX)])
        cnt = np.minimum(tcols + half, LX) - np.maximum(tcols - half, 0)
        bt[:, g, :] = (1.0 / cnt.astype(np.float32))[None, :]
        swc.append(sw); btc.append(bt)
    fouts = phase_F(xrc, gc, xpc, cw1, wac, wxc, vcc, swc, btc, LX, LC)
    CAT1 = np.concatenate([np.stack([o[0] for o in fouts]), np.stack([o[1] for o in fouts])], axis=0)
    cats1 = _token_shard(CAT1, NX, 0, LX)
    x1only = [np.ascontiguousarray(a[:, :, :NX]) for a in x1Ts]
    pq, keysT, uT, vt = peer_args(1)
    pw = np.ascontiguousarray(f32(pool_w)[0].reshape(4, 2, 128, 256))
    pp = np.ascontiguousarray(np.stack([vec_fm(f32(pool_scale)[0]), vec_fm(f32(pool_b)[0].reshape(-1))], axis=2))
    outs = phase_D(cats1, x1only, [mvD(1)] * NCORES, wblocks(f32(od_w_out)[0]), pq, keysT, uT, vt, NX, 0,
                   final_g=vec_fm(f32(norm_final)), pool=(pw, pp))
    out = np.concatenate([unfm(o) for o in outs], axis=0)
    return out[None].astype(np.float32)
```

```python
import contextlib
import numpy as np
import concourse.bass as bass
import concourse.mybir as mybir
from concourse.bass_utils import run_bass_kernel_spmd

F32 = mybir.dt.float32
AF = mybir.ActivationFunctionType
ALU = mybir.AluOpType
AX = mybir.AxisListType


class Reg:
    __slots__ = ("name", "last_w", "reads")

    def __init__(self, name):
        self.name = name
        self.last_w = None
        self.reads = {}


class TT:
    def __init__(self, name, handle):
        self.name = name
        self.h = handle
        self.whole = Reg(name)
        self.sub = {}

    def ap(self):
        return self.h.ap() if hasattr(self.h, "ap") and callable(self.h.ap) else self.h

    def __getitem__(self, idx):
        return self.h[idx]

    def r(self, key=None):
        if key is None:
            return (self, None)
        return (self, key)


class Prog:
    SEM_ROLL = 2000

    def __init__(self, n_dma_sems=24):
        self.nc = bass.Bass("TRN2", target_bir_lowering=False)
        nc = self.nc
        self.stack = contextlib.ExitStack()
        self.engs = {"pe": nc.tensor, "dve": nc.vector, "act": nc.scalar, "pool": nc.gpsimd, "sp": nc.sync}
        self.sems = {}
        self.cur_sem = {}
        self.cnt = {}
        self.sem_gen = {}
        for e in self.engs:
            self.sem_gen[e] = 0
            self._new_eng_sem(e)
        self.dma_sems = []
        for i in range(n_dma_sems):
            nm = f"dq{i}"
            self.sems[nm] = self.stack.enter_context(nc.semaphore(nm))
            self.cnt[nm] = 0
            self.dma_sems.append(nm)
        self.dma_rr = 0
        self.waited = {e: {} for e in self.engs}
        self.n_inst = 0
        self.n_wait = 0
        self._uid = 0

    def _new_eng_sem(self, e):
        nm = f"s_{e}_{self.sem_gen[e]}"
        self.sem_gen[e] += 1
        self.sems[nm] = self.stack.enter_context(self.nc.semaphore(nm))
        self.cnt[nm] = 0
        self.cur_sem[e] = nm

    def sbuf(self, name, shape, dtype=F32):
        h = self.stack.enter_context(self.nc.sbuf_tensor(name, list(shape), dtype))
        return TT(name, h)

    def psum(self, name, shape, dtype=F32):
        h = self.stack.enter_context(self.nc.psum_tensor(name, list(shape), dtype))
        return TT(name, h)

    def dram(self, name, shape, kind, dtype=F32):
        h = self.nc.dram_tensor(name, list(shape), dtype, kind=kind)
        t = TT(name, h)
        return t

    def _regs(self, lst):
        out = []
        for x in lst:
            if isinstance(x, TT):
                out.append((x, None))
            else:
                out.append(x)
        return out

    def _collect(self, reads, writes):
        deps = {}

        def add(ev):
            if ev is None:
                return
            s, v = ev
            if deps.get(s, 0) < v:
                deps[s] = v

        for (t, k) in self._regs(reads):
            add(t.whole.last_w)
            if k is None:
                for r in t.sub.values():
                    add(r.last_w)
            else:
                r = t.sub.get(k)
                if r is not None:
                    add(r.last_w)
        for (t, k) in self._regs(writes):
            regs = [t.whole]
            if k is None:
                regs += list(t.sub.values())
            else:
                r = t.sub.get(k)
                if r is not None:
                    regs.append(r)
            for r in regs:
                add(r.last_w)
                for s, v in r.reads.items():
                    add((s, v))
        return deps

    def _record(self, reads, writes, ev):
        s, v = ev
        for (t, k) in self._regs(reads):
            if k is None:
                r = t.whole
            else:
                r = t.sub.get(k)
                if r is None:
                    r = t.sub[k] = Reg(f"{t.name}:{k}")
            if r.reads.get(s, 0) < v:
                r.reads[s] = v
        for (t, k) in self._regs(writes):
            if k is None:
                t.whole.last_w = ev
                t.whole.reads = {}
                t.sub = {}
            else:
                r = t.sub.get(k)
                if r is None:
                    r = t.sub[k] = Reg(f"{t.name}:{k}")
                r.last_w = ev
                r.reads = {}

    def _emit_waits(self, e, deps, skip_self=False):
        eng = self.engs[e]
        w = self.waited[e]
        for s, v in deps.items():
            if skip_self and s.startswith(f"s_{e}_"):
                continue
            if w.get(s, 0) >= v:
                continue
            eng.wait_ge(self.sems[s], v)
            w[s] = v
            self.n_wait += 1

    def op(self, e, fn, reads=(), writes=(), skip_self=False):
        deps = self._collect(reads, writes)
        self._emit_waits(e, deps, skip_self=skip_self)
        s = self.cur_sem[e]
        if self.cnt[s] >= self.SEM_ROLL:
            self._new_eng_sem(e)
            s = self.cur_sem[e]
        ins = fn(self.engs[e])
        ins.then_inc(self.sems[s], 1)
        self.cnt[s] += 1
        ev = (s, self.cnt[s])
        self._record(reads, writes, ev)
        self.n_inst += 1
        return ev

    def dma(self, q, out, in_, reads=(), writes=(), **kw):
        deps = self._collect(reads, writes)
        s = self.dma_sems[self.dma_rr % len(self.dma_sems)]
        self.dma_rr += 1
        n = self.cnt[s]
        if n > 0:
            if deps.get(s, 0) < 16 * n:
                deps[s] = 16 * n
        self._emit_waits(q, deps)
        ins = self.engs[q].dma_start(out=out, in_=in_, **kw)
        ins.then_inc(self.sems[s], 16)
        self.cnt[s] = n + 1
        ev = (s, 16 * (n + 1))
        self._record(reads, writes, ev)
        self.n_inst += 1
        return ev

    def finish(self, q="sp"):
        deps = {}
        for s, c in self.cnt.items():
            if c == 0:
                continue
            v = c * 16 if s.startswith("dq") else c
            deps[s] = v
        self._emit_waits(q, deps)

    def close(self):
        self.stack.close()


NCORES = 8
D = 2048
KC = 16
EPS = 1e-6


def _run(p, in_maps):
    p.finish()
    res = run_bass_kernel_spmd(p.nc, in_maps, core_ids=list(range(NCORES)))
    p.close()
    return res.results


def fm(x2d):
    T, F = x2d.shape
    return np.ascontiguousarray(x2d.T.reshape(F // 128, 128, T).transpose(1, 0, 2))


def unfm(a):
    P, C, T = a.shape
    return np.ascontiguousarray(a.transpose(2, 1, 0).reshape(T, C * P))


def vec_fm(v):
    return np.ascontiguousarray(v.reshape(-1, 128).T)


def wblocks(w):
    K, N = w.shape
    return np.ascontiguousarray(w.reshape(K // 128, 128, N // 128, 128).transpose(2, 1, 0, 3).reshape(N // 128, 128, (K // 128) * 128))


def phase_A(c, c_ctx, ada_w, ada_b):
    NJ = 12
    p = Prog()
    cT = p.dram("cT", [128, KC, 2], "ExternalInput")
    aw = p.dram("aw", [2, KC, 128, NJ * 128], "ExternalInput")
    ab = p.dram("ab", [128, 2, NJ], "ExternalInput")
    mo = p.dram("mo", [128, 2, NJ, 2], "ExternalOutput")
    cs = p.sbuf("cs", [128, KC, 2]); ss = p.sbuf("ss", [128, KC, 2]); abs_ = p.sbuf("abs", [128, 2, NJ])
    wb = p.sbuf("wb", [128, KC, NJ * 128]); mos = p.sbuf("mos", [128, 2, NJ, 2])
    ps = p.psum("ps", [128, 512])
    p.dma("sp", cs[:], cT.ap(), writes=[cs])
    p.dma("sp", abs_[:], ab.ap(), writes=[abs_])
    p.op("act", lambda e: e.activation(out=ss[:], in_=cs[:], func=AF.Silu), reads=[cs], writes=[ss])
    for l in range(2):
        for kc in range(KC):
            p.dma("sp", wb[:, kc, :], aw.ap()[l, kc], writes=[wb.r(kc)])
        for j in range(NJ):
            for kc in range(KC):
                p.op("pe", lambda e: e.matmul(ps[:, 2 * j:2 * j + 2], wb[:, kc, j * 128:(j + 1) * 128], ss[:, kc, :],
                                              start=(kc == 0), stop=(kc == KC - 1)),
                     reads=[wb.r(kc), ss], writes=[ps.r(j)], skip_self=True)
            p.op("act", lambda e: e.activation(out=mos[:, l, j, :], in_=ps[:, 2 * j:2 * j + 2], func=AF.Identity,
                                               bias=abs_[:, l, j:j + 1]), reads=[ps.r(j), abs_], writes=[mos.r((l, j))])
    p.dma("sp", mo.ap(), mos[:], reads=[mos], writes=[mo])
    cTh = np.stack([vec_fm(c.reshape(-1)), vec_fm(c_ctx.reshape(-1))], axis=2)
    in_maps = []
    for i in range(NCORES):
        cols = slice(i * NJ * 128, (i + 1) * NJ * 128)
        awi = np.ascontiguousarray(ada_w[:, :, cols].reshape(2, KC, 128, NJ * 128))
        abi = np.ascontiguousarray(ada_b[:, cols].reshape(2, NJ, 128).transpose(2, 0, 1))
        in_maps.append({"cT": cTh, "aw": awi, "ab": abi})
    res = _run(p, in_maps)
    mod = np.zeros((2, 12288, 2), np.float32)
    for i in range(NCORES):
        r = res[i]["mo"]
        mod[:, i * NJ * 128:(i + 1) * NJ * 128, :] = r.transpose(1, 2, 0, 3).reshape(2, NJ * 128, 2)
    return mod


class Ring:
    def __init__(self, items):
        self.items = items
        self.i = 0

    def next(self):
        t = self.items[self.i % len(self.items)]
        self.i += 1
        return t


def emit_consts(p):
    ones = p.sbuf("c_ones", [128, 128])
    p.op("dve", lambda e: e.memset(ones[:], 1.0), writes=[ones])
    return ones


def emit_gp(p, g, sc, gp):
    p.op("dve", lambda e: e.tensor_scalar(out=gp[:], in0=sc[:], scalar1=1.0, scalar2=None, op0=ALU.add), reads=[sc], writes=[gp])
    p.op("dve", lambda e: e.tensor_tensor(gp[:], gp[:], g[:], ALU.mult), reads=[gp, g], writes=[gp])


def emit_norm_mod(p, src, dst, C, off, n, gp, sh, ones, scr, stag, dtag, doff=None):
    if doff is None:
        doff = off
    sq, ps, rt, rstd = scr["sq"], scr["ps"], scr["rt"], scr["rstd"]
    p.op("act", lambda e: e.activation(out=sq[:, 0:C, 0:n], in_=src[:, 0:C, off:off + n], func=AF.Square),
         reads=[src.r(stag)], writes=[sq])
    for c in range(C):
        p.op("pe", lambda e: e.matmul(ps[:, 0:n], ones[:], sq[:, c, 0:n], start=(c == 0), stop=(c == C - 1)),
             reads=[sq, ones], writes=[ps], skip_self=True)
    p.op("act", lambda e: e.activation(out=rt[:, 0:n], in_=ps[:, 0:n], func=AF.Sqrt, scale=1.0 / (128 * C), bias=EPS),
         reads=[ps], writes=[rt])
    p.op("dve", lambda e: e.reciprocal(rstd[:, 0:n], rt[:, 0:n]), reads=[rt], writes=[rstd])
    p.op("dve", lambda e: e.tensor_tensor(sq[:, 0:C, 0:n], src[:, 0:C, off:off + n],
                                          rstd[:, 0:n].unsqueeze(1).broadcast_to([128, C, n]), ALU.mult),
         reads=[src.r(stag), rstd], writes=[sq])
    for c in range(C):
        if sh is not None:
            p.op("act", lambda e: e.activation(out=dst[:, c, doff:doff + n], in_=sq[:, c, 0:n], func=AF.Identity,
                                               scale=gp[:, c:c + 1], bias=sh[:, c:c + 1]),
                 reads=[sq, gp, sh], writes=[dst.r(dtag)])
        else:
            p.op("act", lambda e: e.activation(out=dst[:, c, doff:doff + n], in_=sq[:, c, 0:n], func=AF.Identity,
                                               scale=gp[:, c:c + 1]),
                 reads=[sq, gp], writes=[dst.r(dtag)])


def norm_scratch(p, C, nmax, pfx=""):
    return {"sq": p.sbuf(pfx + "n_sq", [128, C, nmax]), "ps": p.psum(pfx + "n_ps", [128, 512]),
            "rt": p.sbuf(pfx + "n_rt", [128, nmax]), "rstd": p.sbuf(pfx + "n_rstd", [128, nmax])}


def split_tiles(nx, nc_, tmax):
    tiles = []
    o = 0
    while o < nx:
        n = min(tmax, nx - o)
        tiles.append((o, n, "x"))
        o += n
    o = 0
    while o < nc_:
        n = min(tmax, nc_ - o)
        tiles.append((nx + o, n, "c"))
        o += n
    return tiles


def phase_B(xT_cores, mv_cores, wblk, epi, NX, NCX, gmn=None):
    M = wblk.shape[0]
    NT = NX + NCX
    tiles = split_tiles(NX, NCX, 512)
    p = Prog()
    xTd = p.dram("xT", [128, KC, NT], "ExternalInput")
    mvd = p.dram("mv", [128, KC, 5], "ExternalInput")
    wd = p.dram("w", [M, 128, KC * 128], "ExternalInput")
    od = p.dram("o", [M, 128, NT], "ExternalOutput")
    ones = emit_consts(p)
    xs = p.sbuf("xs", [128, KC, 512]); hT = p.sbuf("hT", [128, KC, NT]); mv = p.sbuf("mvs", [128, KC, 5])
    gpx = p.sbuf("gpx", [128, KC]); gpc = p.sbuf("gpc", [128, KC])
    g = p.sbuf("g_", [128, KC]); shx = p.sbuf("shx", [128, KC]); scx = p.sbuf("scx", [128, KC])
    shc = p.sbuf("shc", [128, KC]); scc = p.sbuf("scc", [128, KC])
    scr = norm_scratch(p, KC, 512)
    p.dma("sp", mv[:], mvd.ap(), writes=[mv])
    for i, t in enumerate([g, shx, scx, shc, scc]):
        p.op("dve", lambda e: e.tensor_copy(t[:], mv[:, :, i]), reads=[mv], writes=[t])
    emit_gp(p, g, scx, gpx)
    emit_gp(p, g, scc, gpc)
    for (off, n, kind) in tiles:
        p.dma("sp", xs[:, :, 0:n], xTd.ap()[:, :, off:off + n], writes=[xs])
        emit_norm_mod(p, xs, hT, KC, 0, n, gpx if kind == "x" else gpc, shx if kind == "x" else shc, ones, scr, None, off, doff=off)
    wring = Ring([p.sbuf(f"wb{i}", [128, KC * 128]) for i in range(3)])
    pring = Ring([p.psum(f"pm{i}", [128, 512]) for i in range(4)])
    oring = Ring([p.sbuf(f"ob{i}", [128, NT]) for i in range(2)])
    vgT = None
    if gmn is not None:
        vgT = p.sbuf("vgT", [128, 8, NT])
        gm = p.sbuf("gm", [128, 8])
        gmd = p.dram("gmn", [128, 8], "ExternalInput")
        p.dma("sp", gm[:], gmd.ap(), writes=[gm])
    nvg = 0
    for m in range(M):
        wb = wring.next()
        p.dma("sp", wb[:], wd.ap()[m], writes=[wb])
        if epi[m] == "vg":
            ob = None
        else:
            ob = oring.next()
        for (off, n, kind) in tiles:
            ps = pring.next()
            for kc in range(KC):
                p.op("pe", lambda e: e.matmul(ps[:, 0:n], wb[:, kc * 128:(kc + 1) * 128], hT[:, kc, off:off + n],
                                              start=(kc == 0), stop=(kc == KC - 1)),
                     reads=[wb, hT.r(off)], writes=[ps], skip_self=True)
            if epi[m] == "copy":
                p.op("dve", lambda e: e.tensor_copy(ob[:, off:off + n], ps[:, 0:n]), reads=[ps], writes=[ob.r(off)])
            elif epi[m] == "gelu":
                p.op("act", lambda e: e.activation(out=ob[:, off:off + n], in_=ps[:, 0:n], func=AF.Gelu), reads=[ps], writes=[ob.r(off)])
            else:
                p.op("act", lambda e: e.activation(out=vgT[:, nvg, off:off + n], in_=ps[:, 0:n], func=AF.Gelu), reads=[ps], writes=[vgT.r(off)])
        if epi[m] == "vg":
            nvg += 1
        else:
            p.dma("pool", od.ap()[m], ob[:], reads=[ob], writes=[od.r(m)])
    if gmn is not None:
        vo = vgT
        m0 = epi.index("vg")
        for (off, n, kind) in tiles:
            emit_norm_mod(p, vgT, vo, 8, off, n, gm, None, ones, scr, off, off)
        for j in range(8):
            p.dma("pool", od.ap()[m0 + j], vo[:, j, :], reads=[vo], writes=[od.r(m0 + j)])
    in_maps = []
    for i in range(NCORES):
        mp = {"xT": xT_cores[i], "mv": mv_cores[i], "w": wblk}
        if gmn is not None:
            mp["gmn"] = gmn
        in_maps.append(mp)
    res = _run(p, in_maps)
    return [r["o"] for r in res]


NEG = -1.0e30
_DBG_NA = 128
_DBG_NP = 99
PASS_N = 352
GA = 2


def phase_D(catT_cores, xT_cores, mv_cores, woblk, pqblk, keysT, uT, vt, NX, NCX, final_g=None, pool=None):
    NT = NX + NCX
    tiles = split_tiles(NX, NCX, 512)
    passes = []
    o = 0
    while o < NT:
        n = min(PASS_N, NT - o)
        passes.append((o, n))
        o += n
    p = Prog()
    catd = p.dram("catT", [128, KC, NT], "ExternalInput")
    xd = p.dram("xT", [128, KC, NT], "ExternalInput")
    mvd = p.dram("mv", [128, KC, 9], "ExternalInput")
    wod = p.dram("wo", [16, 128, 2048], "ExternalInput")
    pqd = p.dram("pq", [16, 128, 2048], "ExternalInput")
    kd = p.dram("keysT", [16, 128, 128], "ExternalInput")
    ud = p.dram("uT", [128, 128, 2048], "ExternalInput")
    vd = p.dram("vt", [128, 128, 2048], "ExternalInput")
    idd = p.dram("ident", [128, 128], "ExternalInput")
    x1d = p.dram("x1s", [128, KC, NT], "Internal")
    outd = p.dram("o", [128, KC, NT], "ExternalOutput")
    ones = emit_consts(p)
    ident = p.sbuf("ident_s", [128, 128])
    p.dma("sp", ident[:], idd.ap(), writes=[ident])
    mv = p.sbuf("mvs", [128, KC, 9])
    p.dma("sp", mv[:], mvd.ap(), writes=[mv])
    names = ["g1x", "g1c", "gf", "sh2x", "sc2x", "sh2c", "sc2c", "g2x", "g2c"]
    V = {}
    for i, nm in enumerate(names):
        V[nm] = p.sbuf(nm, [128, KC])
        p.op("dve", lambda e: e.tensor_copy(V[nm][:], mv[:, :, i]), reads=[mv], writes=[V[nm]])
    gpx = p.sbuf("gpx", [128, KC]); gpc = p.sbuf("gpc", [128, KC])
    emit_gp(p, V["gf"], V["sc2x"], gpx)
    emit_gp(p, V["gf"], V["sc2c"], gpc)
    if final_g is not None:
        fgd = p.dram("fg", [128, KC], "ExternalInput")
        fg = p.sbuf("fg_s", [128, KC])
        p.dma("sp", fg[:], fgd.ap(), writes=[fg])
    wring = Ring([p.sbuf(f"wb{i}", [128, 2048]) for i in range(2)])
    vring = Ring([p.sbuf(f"vb{i}", [128, 2048]) for i in range(2 * GA)])
    pm = Ring([p.psum(f"pm{i}", [128, 512]) for i in range(2)])
    pact = Ring([p.psum(f"pa{i}", [128, 512]) for i in range(2)])
    pbc = Ring([p.psum(f"pb{i}", [128, 512]) for i in range(2)])
    pso = p.psum("pso", [128, 512])
    pbc_main = Ring(pbc.items + pm.items)
    SaT = p.sbuf("SaT", [128, 8, PASS_N]); SbT = p.sbuf("SbT", [128, 8, PASS_N])
    alias_views = []
    BIGN = max(KC * NT, 3 * KC * PASS_N)
    bigf = p.sbuf("bigf", [128, BIGN])
    big = TT("big", bigf.h[:, 0:KC * NT].rearrange("p (c t) -> p c t", c=KC))
    kbc = p.sbuf("kbc", [128, 8, PASS_N])
    assert 8 * PASS_N >= 2 * NT
    xrows = [TT(f"xrow{i}", kbc.h[:].rearrange("p a b -> p (a b)")[:, i * NT:(i + 1) * NT]) for i in range(2)]
    p.dma("sp", big[:], catd.ap(), writes=[big])
    if pool is not None:
        pwd = p.dram("pw", [4, 2, 128, 256], "ExternalInput")
        ppd = p.dram("pp", [128, 8, 2], "ExternalInput")
        pw = TT("pw_v", SbT.h[:].rearrange("p a b -> p (a b)")[:, 0:2048].rearrange("p (g i j) -> p g i j", g=4, i=2))
        pp = p.sbuf("pp_s", [128, 8, 2]); pbs = p.sbuf("pbs", [128, 8])
        p.dma("sp", pw[:].rearrange("p g i j -> p (g i) j"), pwd.ap().rearrange("g i p j -> p (g i) j"), writes=[pw])
        p.dma("sp", pp[:], ppd.ap(), writes=[pp])
        p.op("dve", lambda e: e.tensor_tensor(pbs[:], pp[:, :, 0], pp[:, :, 1], ALU.mult), reads=[pp], writes=[pbs])
        assert 8 * PASS_N >= 2 * NT
        tmpg = TT("tmpg", SaT.h[:].rearrange("p a b -> p (a b)")[:, 0:2 * NT].rearrange("p (a b) -> p a b", a=2))
        alias_views += [pw, tmpg]
        for g_ in range(4):
            for (off, n, kind) in tiles:
                for jb in range(2):
                    ps = pm.next()
                    for ib in range(2):
                        p.op("pe", lambda e: e.matmul(ps[:, 0:n], pw[:, g_, ib, jb * 128:(jb + 1) * 128], big[:, 8 + 2 * g_ + ib, off:off + n],
                                                      start=(ib == 0), stop=(ib == 1)), reads=[pw, big], writes=[ps], skip_self=True)
                    ob = 2 * g_ + jb
                    p.op("act", lambda e: e.activation(out=tmpg[:, jb, off:off + n], in_=ps[:, 0:n], func=AF.Identity, scale=pp[:, ob, 0:1], bias=pbs[:, ob:ob + 1]),
                         reads=[ps, pp, pbs], writes=[tmpg])
            for jb in range(2):
                p.op("dve", lambda e: e.tensor_copy(big[:, 8 + 2 * g_ + jb, :], tmpg[:, jb, :]), reads=[tmpg], writes=[big])
    xrow = Ring(xrows)
    for m in range(16):
        wb = wring.next()
        p.dma("sp", wb[:], wod.ap()[m], writes=[wb])
        xr = xrow.next()
        p.dma("sp", xr[:], xd.ap()[:, m, :], writes=[xr])
        for (off, n, kind) in tiles:
            ps = pm.next()
            for kc in range(KC):
                p.op("pe", lambda e: e.matmul(ps[:, 0:n], wb[:, kc * 128:(kc + 1) * 128], big[:, kc, off:off + n],
                                              start=(kc == 0), stop=(kc == KC - 1)), reads=[wb, big], writes=[ps], skip_self=True)
            g1 = V["g1x"] if kind == "x" else V["g1c"]
            p.op("dve", lambda e: e.scalar_tensor_tensor(out=xr[:, off:off + n], in0=ps[:, 0:n], scalar=g1[:, m:m + 1],
                                                         in1=xr[:, off:off + n], op0=ALU.mult, op1=ALU.add),
                 reads=[ps, g1, xr.r(off)], writes=[xr.r(off)])
        p.dma("pool", x1d.ap()[:, m, :], xr[:], reads=[xr], writes=[x1d])
    PN = PASS_N
    def _view(nm, i):
        return TT(nm, bigf.h[:, i * KC * PN:(i + 1) * KC * PN].rearrange("p (c t) -> p c t", c=KC))
    xp = _view("xp", 0); h2 = _view("h2", 1); acc = _view("acc", 2)
    scr = {"sq": acc, "ps": p.psum("dn_ps", [128, 512]), "rt": p.sbuf("dn_rt", [128, PN]), "rstd": p.sbuf("dn_rstd", [128, PN])}
    qT = acc
    kT = p.sbuf("kT", [128, 16, 128])
    p.dma("sp", kT[:].rearrange("p a b -> p a b"), kd.ap().rearrange("a p b -> p a b"), writes=[kT])
    Stm = p.sbuf("Stm", [128, 16, 128]); top = p.sbuf("top", [128, 16, 24]); tmpS = p.sbuf("tmpS", [128, 128]); tmpS2 = p.sbuf("tmpS2", [128, 128])
    cand = p.sbuf("cand", [128, 576]); cand2 = p.sbuf("cand2", [128, 576]); c24 = p.sbuf("c24", [128, 24])
    st = {nm: p.sbuf("st_" + nm, [128, 8]) for nm in ["thr", "ncmax", "Z", "kap", "t1"]}
    e16 = p.sbuf("e16", [128, 16]); diag = Ring([p.sbuf(f"diag{i}", [128, 128]) for i in range(2)])
    actS = Ring([p.sbuf(f"actS{i}", [128, PN]) for i in range(2)])
    dS = Ring([p.sbuf(f"dS{i}", [128, PN]) for i in range(4)])
    eS = Ring([p.sbuf(f"eS{i}", [128, PN]) for i in range(4)])
    wS = Ring([p.sbuf(f"wS{i}", [128, PN]) for i in range(2)])
    w2S = Ring([p.sbuf(f"w2S{i}", [128, PN]) for i in range(2)])
    Wacc = Ring([p.sbuf(f"Wacc{i}", [128, PN]) for i in range(2)])
    Gt = Ring([p.sbuf(f"Gt{i}", [128, PN]) for i in range(2 * GA)])
    for (p0, pn) in passes[:_DBG_NP]:
        p.dma("sp", xp[:, :, 0:pn], x1d.ap()[:, :, p0:p0 + pn], reads=[x1d], writes=[xp] + ([big, h2, acc] + xrows + [kbc, SaT, SbT] + alias_views if p0 == 0 else []))
        segs = []
        if p0 < NX:
            segs.append((0, min(pn, NX - p0), "x"))
        if p0 + pn > NX:
            s0 = max(0, NX - p0)
            segs.append((s0, pn - s0, "c"))
        for (s0, sn, kind) in segs:
            emit_norm_mod(p, xp, h2, KC, s0, sn, gpx if kind == "x" else gpc, V["sh2x"] if kind == "x" else V["sh2c"], ones, scr, None, None)
        for hs in range(16):
            wb = wring.next()
            p.dma("sp", wb[:], pqd.ap()[hs], writes=[wb])
            ps = pm.next()
            for kc in range(KC):
                p.op("pe", lambda e: e.matmul(ps[:, 0:pn], wb[:, kc * 128:(kc + 1) * 128], h2[:, kc, 0:pn],
                                              start=(kc == 0), stop=(kc == KC - 1)), reads=[wb, h2], writes=[ps], skip_self=True)
            p.op("dve", lambda e: e.tensor_copy(qT[:, hs, 0:pn], ps[:, 0:pn]), reads=[ps], writes=[qT.r(hs)])
        for hs in range(16):
            ps = pm.next()
            p.op("pe", lambda e: e.matmul(ps[:, 0:pn], kT[:, hs, :], qT[:, hs, 0:pn], start=True, stop=True),
                 reads=[kT, qT.r(hs)], writes=[ps], skip_self=True)
            dst = SaT if hs % 2 == 0 else SbT
            p.op("act", lambda e: e.activation(out=dst[:, hs // 2, 0:pn], in_=ps[:, 0:pn], func=AF.Identity), reads=[ps], writes=[dst.r(hs // 2)])
        t0 = 0
        while t0 < pn:
            nt = min(128, pn - t0)
            for hs in range(16):
                ps = pm.next()
                p.op("pe", lambda e: e.matmul(ps[0:nt, 0:128], qT[:, hs, t0:t0 + nt], kT[:, hs, :], start=True, stop=True),
                     reads=[kT, qT.r(hs)], writes=[ps], skip_self=True)
                p.op("act", lambda e: e.activation(out=Stm[0:nt, hs, :], in_=ps[0:nt, 0:128], func=AF.Identity), reads=[ps], writes=[Stm.r(hs)])
                p.op("dve", lambda e: e.max(top[0:nt, hs, 0:8], Stm[0:nt, hs, :]), reads=[Stm.r(hs)], writes=[top.r(hs)])
                p.op("dve", lambda e: e.match_replace(tmpS[0:nt, :], top[0:nt, hs, 0:8], Stm[0:nt, hs, :], NEG), reads=[Stm.r(hs), top.r(hs)], writes=[tmpS])
                p.op("dve", lambda e: e.max(top[0:nt, hs, 8:16], tmpS[0:nt, :]), reads=[tmpS], writes=[top.r(hs)])
                p.op("dve", lambda e: e.match_replace(tmpS2[0:nt, :], top[0:nt, hs, 8:16], tmpS[0:nt, :], NEG), reads=[tmpS, top.r(hs)], writes=[tmpS2])
                p.op("dve", lambda e: e.max(top[0:nt, hs, 16:24], tmpS2[0:nt, :]), reads=[tmpS2], writes=[top.r(hs)])
            for h in range(8):
                p.op("dve", lambda e: e.tensor_tensor(cand[0:nt, :].rearrange("p (i j) -> p i j", i=24),
                                                      top[0:nt, 2 * h, :].unsqueeze(2).broadcast_to([nt, 24, 24]),
                                                      top[0:nt, 2 * h + 1, :].unsqueeze(1).broadcast_to([nt, 24, 24]), ALU.add),
                     reads=[top.r(2 * h), top.r(2 * h + 1)], writes=[cand])
                p.op("dve", lambda e: e.max(c24[0:nt, 0:8], cand[0:nt, :]), reads=[cand], writes=[c24])
                p.op("dve", lambda e: e.match_replace(cand2[0:nt, :], c24[0:nt, 0:8], cand[0:nt, :], NEG), reads=[cand, c24], writes=[cand2])
                p.op("dve", lambda e: e.max(c24[0:nt, 8:16], cand2[0:nt, :]), reads=[cand2], writes=[c24])
                p.op("dve", lambda e: e.match_replace(cand[0:nt, :], c24[0:nt, 8:16], cand2[0:nt, :], NEG), reads=[cand2, c24], writes=[cand])
                p.op("dve", lambda e: e.max(c24[0:nt, 16:24], cand[0:nt, :]), reads=[cand], writes=[c24])
                p.op("dve", lambda e: e.tensor_tensor(st["thr"][0:nt, h:h + 1], c24[0:nt, 15:16], c24[0:nt, 16:17], ALU.add), reads=[c24], writes=[st["thr"]])
                p.op("dve", lambda e: e.tensor_scalar(out=st["thr"][0:nt, h:h + 1], in0=st["thr"][0:nt, h:h + 1], scalar1=0.5, scalar2=None, op0=ALU.mult), reads=[st["thr"]], writes=[st["thr"]])
                p.op("dve", lambda e: e.tensor_scalar(out=st["ncmax"][0:nt, h:h + 1], in0=c24[0:nt, 0:1], scalar1=-1.0, scalar2=None, op0=ALU.mult), reads=[c24], writes=[st["ncmax"]])
                p.op("act", lambda e: e.activation(out=e16[0:nt, :], in_=c24[0:nt, 0:16], func=AF.Exp, bias=st["ncmax"][0:nt, h:h + 1],
                                                   accum_out=st["Z"][0:nt, h:h + 1]), reads=[c24, st["ncmax"]], writes=[e16, st["Z"]])
                p.op("act", lambda e: e.activation(out=st["t1"][0:nt, h:h + 1], in_=st["thr"][0:nt, h:h + 1], func=AF.Exp, bias=st["ncmax"][0:nt, h:h + 1]),
                     reads=[st["thr"], st["ncmax"]], writes=[st["t1"]])
                p.op("dve", lambda e: e.reciprocal(st["kap"][0:nt, h:h + 1], st["Z"][0:nt, h:h + 1]), reads=[st["Z"]], writes=[st["kap"]])
                p.op("dve", lambda e: e.tensor_tensor(st["kap"][0:nt, h:h + 1], st["kap"][0:nt, h:h + 1], st["t1"][0:nt, h:h + 1], ALU.mult), reads=[st["kap"], st["t1"]], writes=[st["kap"]])
                for which in ("thr", "kap"):
                    dg = diag.next()
                    p.op("dve", lambda e: e.tensor_scalar(out=dg[0:nt, 0:nt], in0=ident[0:nt, 0:nt], scalar1=st[which][0:nt, h:h + 1], scalar2=None, op0=ALU.mult),
                         reads=[ident, st[which]], writes=[dg])
                    ps = pbc.next()
                    p.op("pe", lambda e: e.matmul(ps[:, 0:nt], ones[0:nt, :], dg[0:nt, 0:nt], start=True, stop=True), reads=[ones, dg], writes=[ps], skip_self=True)
                    if which == "thr":
                        p.op("dve", lambda e: e.tensor_tensor(SaT[:, h, t0:t0 + nt], SaT[:, h, t0:t0 + nt], ps[:, 0:nt], ALU.subtract),
                             reads=[ps, SaT.r(h)], writes=[SaT.r(h)])
                    else:
                        p.op("act", lambda e: e.activation(out=kbc[:, h, t0:t0 + nt], in_=ps[:, 0:nt], func=AF.Identity), reads=[ps], writes=[kbc.r(h)])
            t0 += nt
        first_group = True
        grp = []
        NA_ = _DBG_NA
        ustate = {}

        def _u_load(a_):
            ub_ = wring.next()
            p.dma("sp", ub_[:], ud.ap()[a_], writes=[ub_])
            ustate[a_] = (ub_, pact.next())

        def _u_mm(a_, kcs):
            ub_, psa_ = ustate[a_]
            for kc in kcs:
                p.op("pe", lambda e: e.matmul(psa_[:, 0:pn], ub_[:, kc * 128:(kc + 1) * 128], h2[:, kc, 0:pn],
                                              start=(kc == 0), stop=(kc == KC - 1)), reads=[ub_, h2], writes=[psa_], skip_self=True)

        _u_load(0)
        _u_mm(0, range(KC))
        for a in range(NA_):
            vb = vring.next()
            p.dma("sp", vb[:], vd.ap()[a], writes=[vb])
            ub, psa = ustate.pop(a)
            if a + 1 < NA_:
                _u_load(a + 1)
            aS = actS.next()
            p.op("act", lambda e: e.activation(out=aS[:, 0:pn], in_=psa[:, 0:pn], func=AF.Gelu), reads=[psa], writes=[aS])
            wa = Wacc.next()
            dlist = []

            def _emit_add(h):
                psb = pbc_main.next()
                p.op("pe", lambda e: e.matmul(psb[:, 0:pn], ident[:, a:a + 1].broadcast_to([128, 128]), SaT[:, h, 0:pn], start=True, stop=True),
                     reads=[ident, SaT.r(h)], writes=[psb], skip_self=True)
                d_ = dS.next()
                p.op("dve", lambda e: e.tensor_tensor(d_[:, 0:pn], psb[:, 0:pn], SbT[:, h, 0:pn], ALU.add), reads=[psb, SbT.r(h)], writes=[d_])
                e_ = eS.next()
                p.op("act", lambda e: e.activation(out=e_[:, 0:pn], in_=d_[:, 0:pn], func=AF.Exp), reads=[d_], writes=[e_])
                dlist.append((d_, e_))

            def _emit_rest(h):
                d_, e_ = dlist[h]
                w_ = wS.next()
                p.op("dve", lambda e: e.scalar_tensor_tensor(out=w_[:, 0:pn], in0=d_[:, 0:pn], scalar=0.0, in1=e_[:, 0:pn], op0=ALU.is_gt, op1=ALU.mult),
                     reads=[d_, e_], writes=[w_])
                if h == 0:
                    p.op("pool", lambda e: e.tensor_tensor(wa[:, 0:pn], w_[:, 0:pn], kbc[:, h, 0:pn], ALU.mult), reads=[w_, kbc.r(h)], writes=[wa])
                else:
                    w2 = w2S.next()
                    p.op("pool", lambda e: e.tensor_tensor(w2[:, 0:pn], w_[:, 0:pn], kbc[:, h, 0:pn], ALU.mult), reads=[w_, kbc.r(h)], writes=[w2])
                    p.op("dve", lambda e: e.tensor_tensor(wa[:, 0:pn], wa[:, 0:pn], w2[:, 0:pn], ALU.add), reads=[wa, w2], writes=[wa])

            _emit_add(0)
            _emit_add(1)
            for h in range(8):
                if a + 1 < NA_:
                    _u_mm(a + 1, [2 * h, 2 * h + 1])
                if h + 2 < 8:
                    _emit_add(h + 2)
                _emit_rest(h)
            gt = Gt.next()
            p.op("pool", lambda e: e.tensor_tensor(gt[:, 0:pn], aS[:, 0:pn], wa[:, 0:pn], ALU.mult), reads=[aS, wa], writes=[gt])
            grp.append((vb, gt))
            if len(grp) == GA:
                for db in range(16):
                    for gi, (vb_, gt_) in enumerate(grp):
                        p.op("pe", lambda e: e.matmul(pso[:, 0:pn], vb_[:, db * 128:(db + 1) * 128], gt_[:, 0:pn],
                                                      start=(gi == 0), stop=(gi == GA - 1)), reads=[vb_, gt_], writes=[pso], skip_self=True)
                    if first_group:
                        p.op("dve", lambda e: e.tensor_copy(acc[:, db, 0:pn], pso[:, 0:pn]), reads=[pso], writes=[acc.r(db)])
                    else:
                        p.op("dve", lambda e: e.tensor_tensor(acc[:, db, 0:pn], acc[:, db, 0:pn], pso[:, 0:pn], ALU.add), reads=[pso, acc.r(db)], writes=[acc.r(db)])
                first_group = False
                grp = []
        for (s0, sn, kind) in segs:
            g2 = V["g2x"] if kind == "x" else V["g2c"]
            for db in range(16):
                p.op("dve", lambda e: e.scalar_tensor_tensor(out=xp[:, db, s0:s0 + sn], in0=acc[:, db, s0:s0 + sn], scalar=g2[:, db:db + 1],
                                                             in1=xp[:, db, s0:s0 + sn], op0=ALU.mult, op1=ALU.add),
                     reads=[acc.r(db), g2, xp], writes=[xp])
        if final_g is not None:
            emit_norm_mod(p, xp, h2, KC, 0, pn, fg, None, ones, scr, None, None)
            p.dma("pool", outd.ap()[:, :, p0:p0 + pn], h2[:, :, 0:pn], reads=[h2], writes=[outd])
        else:
            p.dma("pool", outd.ap()[:, :, p0:p0 + pn], xp[:, :, 0:pn], reads=[xp], writes=[outd])
    in_maps = []
    idn = np.eye(128, dtype=np.float32)
    for i in range(NCORES):
        mp = {"catT": catT_cores[i], "xT": xT_cores[i], "mv": mv_cores[i], "wo": woblk, "pq": pqblk, "keysT": keysT,
              "uT": uT, "vt": vt, "ident": idn}
        if final_g is not None:
            mp["fg"] = final_g
        if pool is not None:
            mp["pw"] = pool[0]; mp["pp"] = pool[1]
        in_maps.append(mp)
    print("phase_D: n_inst", p.n_inst, "n_wait", p.n_wait, flush=True)
    res = _run(p, in_maps)
    return [r["o"] for r in res]


TWO_PI = 6.283185307179586
I32 = mybir.dt.int32


def hyena_feats(L):
    t = np.arange(L, dtype=np.float32)
    t_norm = (t / np.float32(max(L - 1, 1))).astype(np.float32)
    bands = np.linspace(1e-4, 15, 16, dtype=np.float32)
    ang = (np.float32(2.0 * np.pi / L) * t[:, None] * bands[None, :]).astype(np.float32)
    feats = np.concatenate([t_norm[:, None], np.cos(ang), -np.sin(ang)], axis=-1).astype(np.float32)
    return feats, t_norm


def phase_C(P3_cores, cw_cores, cb_cores, w3_cores, dl_cores, hb_cores, vg_cores, u_cores, wsT_cores, bs_cores,
            w1, b1, w2, b2, freq, LX, LC):
    LT = LX + LC
    p = Prog()
    P3d = p.dram("P3", [3, 128, LT], "ExternalInput")
    cwd = p.dram("cw", [128, 9], "ExternalInput")
    cbd = p.dram("cb", [128, 3], "ExternalInput")
    w3d = p.dram("w3", [64, 4 * 128], "ExternalInput")
    dld = p.dram("dl", [128, 4], "ExternalInput")
    hbd = p.dram("hb", [128, 2], "ExternalInput")
    vgd = p.dram("vg", [LT // 128, 128, 128], "ExternalInput")
    ud = p.dram("u", [128, LT], "ExternalInput")
    wsd = p.dram("wsT", [128, 128], "ExternalInput")
    bsd = p.dram("bs", [1, 128], "ExternalInput")
    w1d = p.dram("w1", [33, 64], "ExternalInput"); w2d = p.dram("w2", [64, 64], "ExternalInput")
    mlpd = p.dram("mlpv", [64, 3], "ExternalInput")
    seqs = [("x", 0, LX), ("c", LX, LC)]
    fd = {}; tnd = {}
    for nm, o, L in seqs:
        fd[nm] = p.dram("f_" + nm, [33, 2 * L], "ExternalInput")
        tnd[nm] = p.dram("tn_" + nm, [1, 2 * L], "ExternalInput")
    outd = p.dram("o", [2, 128, LT], "ExternalOutput")
    cw = p.sbuf("cw_s", [128, 9]); cb = p.sbuf("cb_s", [128, 3]); w3 = p.sbuf("w3_s", [64, 512]); dl = p.sbuf("dl_s", [128, 4])
    hb = p.sbuf("hb_s", [128, 2]); ws = p.sbuf("ws_s", [128, 128]); bsb = p.sbuf("bs_s", [128, 128])
    w1s = p.sbuf("w1_s", [33, 64]); w2s = p.sbuf("w2_s", [64, 64]); mlpv = p.sbuf("mlpv_s", [64, 3]); fq = p.sbuf("fq_s", [64, 1])
    for t, dd in [(cw, cwd), (cb, cbd), (w3, w3d), (dl, dld), (hb, hbd), (ws, wsd), (w1s, w1d), (w2s, w2d), (mlpv, mlpd)]:
        p.dma("sp", t[:], dd.ap(), writes=[t])
    p.dma("sp", bsb[:], bsd.ap().broadcast_to([128, 128]), writes=[bsb])
    p.op("dve", lambda e: e.tensor_scalar(out=fq[:], in0=mlpv[:, 2:3], scalar1=1.0 / TWO_PI, scalar2=None, op0=ALU.mult), reads=[mlpv], writes=[fq])
    nad = p.sbuf("nad", [128, 4])
    p.op("act", lambda e: e.activation(out=nad[:], in_=dl[:], func=AF.Abs), reads=[dl], writes=[nad])
    p.op("dve", lambda e: e.tensor_scalar(out=nad[:], in0=nad[:], scalar1=-1.0, scalar2=None, op0=ALU.mult), reads=[nad], writes=[nad])
    pg = Ring([p.psum(f"pg{i}", [128, 512]) for i in range(2)])
    vgr = Ring([p.sbuf(f"vgc{i}", [128, 2, 128]) for i in range(2)])
    ur = Ring([p.sbuf(f"uc{i}", [128, 256]) for i in range(2)])
    tr = Ring([p.sbuf(f"tc{i}", [128, 256]) for i in range(2)])
    nch = LT // 128
    for g0 in range(0, nch, 2):
        vgt = vgr.next(); ut = ur.next(); tt = tr.next(); ps = pg.next()
        p.dma("sp", vgt[:], vgd.ap()[g0:g0 + 2].rearrange("a q c -> q a c"), writes=[vgt])
        p.dma("sp", ut[:], ud.ap()[:, g0 * 128:(g0 + 2) * 128], writes=[ut])
        for k in range(2):
            p.op("pe", lambda e: e.matmul(ps[:, k * 128:(k + 1) * 128], vgt[:, k, :], ws[:], start=True, stop=True),
                 reads=[vgt, ws], writes=[ps.r(k)], skip_self=True)
        p.op("dve", lambda e: e.tensor_tensor(tt[:].rearrange("p (a b) -> p a b", a=2), ps[:, 0:256].rearrange("p (a b) -> p a b", a=2),
                                              bsb[:].unsqueeze(1).broadcast_to([128, 2, 128]), ALU.add), reads=[ps, bsb], writes=[tt])
        p.op("pool", lambda e: e.tensor_tensor(tt[:], tt[:], ut[:], ALU.mult), reads=[tt, ut], writes=[tt])
        p.dma("pool", outd.ap()[1, :, g0 * 128:(g0 + 2) * 128], tt[:], reads=[tt], writes=[outd.r(("yb", g0))])
    NFFT = 2 * LX
    assert NFFT == 128 * 128
    CG = 16
    zb = p.sbuf("zb", [128, LT]); yb_ = p.sbuf("ybuf", [128, LT]); tmp = yb_; xg = p.sbuf("xg", [128, LT])
    kkc = p.sbuf("kk_c", [128, 2 * LC])
    gD = p.dram("gD", [128, NFFT], "Internal"); uD = p.dram("uD", [128, LX], "Internal"); yD = p.dram("yD", [128, LX], "Internal")
    tabd = p.dram("ffttab", [128, 10, 128], "ExternalInput")
    tab = p.sbuf("tab_s", [128, 10, 128])
    p.dma("sp", tab[:], tabd.ap(), writes=[tab])
    Cm = tab[:, 0, :]; Sm = tab[:, 1, :]; nSm = tab[:, 2, :]; Tc = tab[:, 3, :]; Ts = tab[:, 4, :]
    F1cat = tab[:, 5:7, :].rearrange("p a b -> p (a b)")
    Fi1 = tab[:, 0:2, :].rearrange("p a b -> p (a b)")
    Fi2 = tab[:, 7:9, :].rearrange("p a b -> p (a b)")
    pmm = Ring([p.psum(f"pc{i}", [128, 512]) for i in range(2)])
    pf = Ring([p.psum(f"pf{i}", [128, 512]) for i in range(4)])
    ft = Ring([p.sbuf(f"ft{i}", [33, 512]) for i in range(2)])
    tnb = Ring([p.sbuf(f"tnb{i}", [128, 512]) for i in range(2)])
    hA = Ring([p.sbuf(f"hA{i}", [64, 512]) for i in range(2)])
    hI = p.sbuf("hI", [64, 512], I32); hF = p.sbuf("hF", [64, 512])
    dec = Ring([p.sbuf(f"dec{i}", [128, 512]) for i in range(2)])
    kt = Ring([p.sbuf(f"kt{i}", [128, 512]) for i in range(2)])
    absd = p.sbuf("absd", [128, 512]); part = p.sbuf("part", [128, 64]); den = p.sbuf("den", [128, 1]); rden = p.sbuf("rden", [128, 1])
    Tu = p.sbuf("Tu", [64, CG, 128]); Tg = p.sbuf("Tg", [128, CG, 128]); Ty = p.sbuf("Ty", [64, CG, 128])
    Btr = Ring([p.sbuf(f"Bt{i}", [128, 2, 4, 128]) for i in range(2)])
    Gs = p.sbuf("Gs", [128, 2, 512]); Yt = p.sbuf("Yt", [128, 2, 4, 128]); Et = p.sbuf("Et", [128, 2, 4, 128])
    tq = Ring([p.sbuf(f"tq{i}", [128, 512]) for i in range(6)])

    def sin_layer(ps, n, bcol, out):
        p.op("dve", lambda e: e.tensor_scalar(out=out[:, 0:n], in0=ps[0:64, 0:n], scalar1=mlpv[:, bcol:bcol + 1], scalar2=fq[:, 0:1], op0=ALU.add, op1=ALU.mult),
             reads=[ps, mlpv, fq], writes=[out])
        p.op("dve", lambda e: e.tensor_copy(hI[:, 0:n], out[:, 0:n]), reads=[out], writes=[hI])
        p.op("dve", lambda e: e.tensor_copy(hF[:, 0:n], hI[:, 0:n]), reads=[hI], writes=[hF])
        p.op("dve", lambda e: e.tensor_tensor(out[:, 0:n], out[:, 0:n], hF[:, 0:n], ALU.subtract), reads=[out, hF], writes=[out])
        p.op("act", lambda e: e.activation(out=out[:, 0:n], in_=out[:, 0:n], func=AF.Sin, scale=TWO_PI * (1 - 2e-7)), reads=[out], writes=[out])

    def conv_into(dst, j):
        p.dma("sp", tmp[:], P3d.ap()[j], writes=[tmp])
        for nm, o, L in seqs:
            p.op("dve", lambda e: e.tensor_scalar(out=dst[:, o:o + L], in0=tmp[:, o:o + L], scalar1=cw[:, 3 * j + 1:3 * j + 2], scalar2=cb[:, j:j + 1], op0=ALU.mult, op1=ALU.add),
                 reads=[tmp, cw, cb], writes=[dst])
            p.op("dve", lambda e: e.scalar_tensor_tensor(out=dst[:, o + 1:o + L], in0=tmp[:, o:o + L - 1], scalar=cw[:, 3 * j:3 * j + 1], in1=dst[:, o + 1:o + L], op0=ALU.mult, op1=ALU.add),
                 reads=[tmp, cw, dst], writes=[dst])
            p.op("dve", lambda e: e.scalar_tensor_tensor(out=dst[:, o:o + L - 1], in0=tmp[:, o + 1:o + L], scalar=cw[:, 3 * j + 2:3 * j + 3], in1=dst[:, o:o + L - 1], op0=ALU.mult, op1=ALU.add),
                 reads=[tmp, cw, dst], writes=[dst])

    def cmul_tw(ps, Bt, j0, conj):
        v = ps[:, 0:512].rearrange("p (c r k) -> p c r k", c=2, r=2)
        Ar = v[:, :, 0, :]; Ai = v[:, :, 1, :]
        Tcb = Tc.unsqueeze(1).broadcast_to([128, 2, 128]); Tsb = Ts.unsqueeze(1).broadcast_to([128, 2, 128])
        t = [tq.next() for _ in range(4)]
        tv = [x[:, 0:256].rearrange("p (c k) -> p c k", c=2) for x in t]
        p.op("dve", lambda e: e.tensor_tensor(tv[0], Ar, Tcb, ALU.mult), reads=[ps, tab], writes=[t[0]])
        p.op("dve", lambda e: e.tensor_tensor(tv[1], Ai, Tsb, ALU.mult), reads=[ps, tab], writes=[t[1]])
        p.op("dve", lambda e: e.tensor_tensor(tv[2], Ai, Tcb, ALU.mult), reads=[ps, tab], writes=[t[2]])
        p.op("dve", lambda e: e.tensor_tensor(tv[3], Ar, Tsb, ALU.mult), reads=[ps, tab], writes=[t[3]])
        p.op("pool", lambda e: e.tensor_tensor(Bt[:, 0, j0:j0 + 2, :], tv[0], tv[1], ALU.subtract if conj else ALU.add), reads=[t[0], t[1]], writes=[Bt.r((0, j0))])
        p.op("pool", lambda e: e.tensor_tensor(Bt[:, 1, j0:j0 + 2, :], tv[2], tv[3], ALU.add if conj else ALU.subtract), reads=[t[2], t[3]], writes=[Bt.r((1, j0))])

    def fwd_fft4(T, K, c0):
        Bt = Btr.next()
        for pr_ in range(2):
            ps = pf.next()
            for j in range(2):
                c = c0 + 2 * pr_ + j
                p.op("pe", lambda e: e.matmul(ps[:, j * 256:(j + 1) * 256], T[0:K, c, :], F1cat[0:K, :], start=True, stop=True),
                     reads=[T, tab], writes=[ps.r(j)], skip_self=True)
            cmul_tw(ps, Bt, 2 * pr_, False)
        Br = Bt[:, 0, :, :].rearrange("p c k -> p (c k)"); Bi = Bt[:, 1, :, :].rearrange("p c k -> p (c k)")
        pxr = pf.next(); pxi = pf.next()
        p.op("pe", lambda e: e.matmul(pxr[:, 0:512], Cm, Br, start=True, stop=False), reads=[tab, Bt], writes=[pxr], skip_self=True)
        p.op("pe", lambda e: e.matmul(pxr[:, 0:512], Sm, Bi, start=False, stop=True), reads=[tab, Bt], writes=[pxr], skip_self=True)
        p.op("pe", lambda e: e.matmul(pxi[:, 0:512], Cm, Bi, start=True, stop=False), reads=[tab, Bt], writes=[pxi], skip_self=True)
        p.op("pe", lambda e: e.matmul(pxi[:, 0:512], nSm, Br, start=False, stop=True), reads=[tab, Bt], writes=[pxi], skip_self=True)
        return pxr, pxi

    def fft_conv():
        for c0g in range(0, 128, CG):
            p.dma("sp", Tu[:], uD.ap()[c0g:c0g + CG, :].rearrange("c (a b) -> a c b", b=128), reads=[uD], writes=[Tu])
            p.dma("sp", Tg[:], gD.ap()[c0g:c0g + CG, :].rearrange("c (a b) -> a c b", b=128), reads=[gD], writes=[Tg])
            for c0 in range(0, CG, 4):
                gr, gi = fwd_fft4(Tg, 128, c0)
                p.op("act", lambda e: e.activation(out=Gs[:, 0, :], in_=gr[:, 0:512], func=AF.Identity), reads=[gr], writes=[Gs.r(0)])
                p.op("act", lambda e: e.activation(out=Gs[:, 1, :], in_=gi[:, 0:512], func=AF.Identity), reads=[gi], writes=[Gs.r(1)])
                ur, ui = fwd_fft4(Tu, 64, c0)
                t = [tq.next() for _ in range(4)]
                p.op("dve", lambda e: e.tensor_tensor(t[0][:], ur[:, 0:512], Gs[:, 0, :], ALU.mult), reads=[ur, Gs.r(0)], writes=[t[0]])
                p.op("dve", lambda e: e.tensor_tensor(t[1][:], ui[:, 0:512], Gs[:, 1, :], ALU.mult), reads=[ui, Gs.r(1)], writes=[t[1]])
                p.op("dve", lambda e: e.tensor_tensor(t[2][:], ur[:, 0:512], Gs[:, 1, :], ALU.mult), reads=[ur, Gs.r(1)], writes=[t[2]])
                p.op("dve", lambda e: e.tensor_tensor(t[3][:], ui[:, 0:512], Gs[:, 0, :], ALU.mult), reads=[ui, Gs.r(0)], writes=[t[3]])
                p.op("pool", lambda e: e.tensor_tensor(Yt[:, 0, :, :].rearrange("p c k -> p (c k)"), t[0][:], t[1][:], ALU.subtract), reads=[t[0], t[1]], writes=[Yt.r(0)])
                p.op("pool", lambda e: e.tensor_tensor(Yt[:, 1, :, :].rearrange("p c k -> p (c k)"), t[2][:], t[3][:], ALU.add), reads=[t[2], t[3]], writes=[Yt.r(1)])
                for pr_ in range(2):
                    ps = pf.next()
                    for j in range(2):
                        c = 2 * pr_ + j
                        p.op("pe", lambda e: e.matmul(ps[:, j * 256:(j + 1) * 256], Yt[:, 0, c, :], Fi1, start=True, stop=False), reads=[Yt.r(0), tab], writes=[ps.r(j)], skip_self=True)
                        p.op("pe", lambda e: e.matmul(ps[:, j * 256:(j + 1) * 256], Yt[:, 1, c, :], Fi2, start=False, stop=True), reads=[Yt.r(1), tab], writes=[ps.r(j)], skip_self=True)
                    cmul_tw(ps, Et, 2 * pr_, True)
                Er = Et[:, 0, :, :].rearrange("p c k -> p (c k)"); Ei = Et[:, 1, :, :].rearrange("p c k -> p (c k)")
                py = pf.next()
                p.op("pe", lambda e: e.matmul(py[0:64, 0:512], tab[:, 0, 0:64], Er, start=True, stop=False), reads=[tab, Et], writes=[py], skip_self=True)
                p.op("pe", lambda e: e.matmul(py[0:64, 0:512], tab[:, 2, 0:64], Ei, start=False, stop=True), reads=[tab, Et], writes=[py], skip_self=True)
                p.op("act", lambda e: e.activation(out=Ty[:, c0:c0 + 4, :].rearrange("p c k -> p (c k)"), in_=py[0:64, 0:512], func=AF.Identity), reads=[py], writes=[Ty.r(c0)])
            p.dma("pool", yD.ap()[c0g:c0g + CG, :].rearrange("c (a b) -> a c b", b=128), Ty[:], reads=[Ty], writes=[yD.r(c0g)])

    conv_into(zb, 0)
    for n in range(2):
        conv_into(xg, n + 1)
        for nm, o, L in seqs:
            use_fft = (nm == "x")
            TS = min(512, L)
            ntile = 2 * L // TS
            for ti in range(ntile):
                c0 = ti * TS
                if use_fft:
                    dirn = 0 if c0 < L else 1
                else:
                    dirn = 1 if c0 < L else 0
                f = ft.next(); tb = tnb.next()
                p.dma("sp", f[:, 0:TS], fd[nm].ap()[:, c0:c0 + TS], writes=[f])
                p.dma("sp", tb[:, 0:TS], tnd[nm].ap()[:, c0:c0 + TS].broadcast_to([128, TS]), writes=[tb])
                ps = pmm.next()
                p.op("pe", lambda e: e.matmul(ps[0:64, 0:TS], w1s[:], f[:, 0:TS], start=True, stop=True), reads=[w1s, f], writes=[ps], skip_self=True)
                h1 = hA.next()
                sin_layer(ps, TS, 0, h1)
                ps = pmm.next()
                p.op("pe", lambda e: e.matmul(ps[0:64, 0:TS], w2s[:], h1[:, 0:TS], start=True, stop=True), reads=[w2s, h1], writes=[ps], skip_self=True)
                h2 = hA.next()
                sin_layer(ps, TS, 1, h2)
                ps = pmm.next()
                col = (dirn * 2 + n) * 128
                p.op("pe", lambda e: e.matmul(ps[:, 0:TS], w3[:, col:col + 128], h2[:, 0:TS], start=True, stop=True), reads=[w3, h2], writes=[ps], skip_self=True)
                dc = dec.next()
                p.op("act", lambda e: e.activation(out=dc[:, 0:TS], in_=tb[:, 0:TS], func=AF.Exp, scale=nad[:, dirn * 2 + n:dirn * 2 + n + 1]), reads=[tb, nad], writes=[dc])
                if use_fft:
                    k_ = kt.next()
                    p.op("dve", lambda e: e.tensor_tensor(k_[:, 0:TS], ps[:, 0:TS], dc[:, 0:TS], ALU.mult), reads=[ps, dc], writes=[k_])
                    p.op("act", lambda e: e.activation(out=absd[:, 0:TS], in_=k_[:, 0:TS], func=AF.Abs, accum_out=part[:, ti:ti + 1]), reads=[k_], writes=[absd, part.r(ti)])
                    p.dma("pool", gD.ap()[:, c0:c0 + TS], k_[:, 0:TS], reads=[k_], writes=[gD.r(ti)])
                else:
                    p.op("dve", lambda e: e.tensor_tensor(kkc[:, c0:c0 + TS], ps[:, 0:TS], dc[:, 0:TS], ALU.mult), reads=[ps, dc], writes=[kkc.r(ti)])
                    p.op("act", lambda e: e.activation(out=absd[:, 0:TS], in_=kkc[:, c0:c0 + TS], func=AF.Abs, accum_out=part[:, ti:ti + 1]), reads=[kkc.r(ti)], writes=[absd, part.r(ti)])
            p.op("dve", lambda e: e.reduce_sum(den[:], part[:, 0:ntile], AX.X), reads=[part], writes=[den])
            p.op("dve", lambda e: e.reciprocal(rden[:], den[:]), reads=[den], writes=[rden])
            Y = yb_
            if use_fft:
                p.op("dve", lambda e: e.tensor_scalar(out=rden[:], in0=rden[:], scalar1=1.0 / NFFT, scalar2=None, op0=ALU.mult), reads=[rden], writes=[rden])
                p.dma("sp", uD.ap(), zb[:, o:o + L], reads=[zb], writes=[uD])
                fft_conv()
                p.dma("sp", Y[:, o:o + L], yD.ap(), reads=[yD], writes=[Y.r(nm)])
            else:
                p.op("dve", lambda e: e.tensor_scalar(out=Y[:, o:o + L], in0=kkc[:, L:2 * L], scalar1=zb[:, o:o + 1], scalar2=None, op0=ALU.mult),
                     reads=[kkc, zb], writes=[Y.r(nm)])
                for s_ in range(1, L):
                    p.op("dve", lambda e: e.scalar_tensor_tensor(out=Y[:, o:o + L], in0=kkc[:, L - s_:2 * L - s_], scalar=zb[:, o + s_:o + s_ + 1], in1=Y[:, o:o + L], op0=ALU.mult, op1=ALU.add),
                         reads=[kkc, zb, Y.r(nm)], writes=[Y.r(nm)])
            p.op("dve", lambda e: e.tensor_scalar(out=Y[:, o:o + L], in0=Y[:, o:o + L], scalar1=rden[:, 0:1], scalar2=None, op0=ALU.mult), reads=[Y.r(nm), rden], writes=[Y.r(nm)])
            p.op("dve", lambda e: e.scalar_tensor_tensor(out=Y[:, o:o + L], in0=zb[:, o:o + L], scalar=hb[:, n:n + 1], in1=Y[:, o:o + L], op0=ALU.mult, op1=ALU.add),
                 reads=[zb, hb, Y.r(nm)], writes=[Y.r(nm)])
            p.op("dve", lambda e: e.tensor_tensor(zb[:, o:o + L], xg[:, o:o + L], Y[:, o:o + L], ALU.mult), reads=[xg, Y.r(nm)], writes=[zb])
    p.dma("pool", outd.ap()[0], zb[:], reads=[zb], writes=[outd.r("z")])
    consts = {}
    for nm, o, L in seqs:
        feats, tn = hyena_feats(L)
        if nm == "x":
            pos = np.concatenate([np.arange(L), np.arange(L - 1, -1, -1)])
        else:
            pos = np.concatenate([np.arange(L - 1, -1, -1), np.arange(L)])
        consts["f_" + nm] = np.ascontiguousarray(feats[pos].T)
        consts["tn_" + nm] = np.ascontiguousarray(tn[pos][None, :])
    mlpvh = np.ascontiguousarray(np.stack([b1, b2, freq], axis=1).astype(np.float32))
    jk = np.outer(np.arange(128), np.arange(128)).astype(np.float64)
    Cn = np.cos(2 * np.pi * jk / 128); Sn = np.sin(2 * np.pi * jk / 128)
    Tcn = np.cos(2 * np.pi * jk / (128 * 128)); Tsn = np.sin(2 * np.pi * jk / (128 * 128))
    consts["ffttab"] = np.ascontiguousarray(np.stack([Cn, Sn, -Sn, Tcn, Tsn, Cn, -Sn, -Sn, Cn, Cn], axis=1).astype(np.float32))
    in_maps = []
    for i in range(NCORES):
        mp = {"P3": P3_cores[i], "cw": cw_cores[i], "cb": cb_cores[i], "w3": w3_cores[i], "dl": dl_cores[i], "hb": hb_cores[i],
              "vg": vg_cores[i], "u": u_cores[i], "wsT": wsT_cores[i], "bs": bs_cores[i], "w1": w1, "w2": w2, "mlpv": mlpvh}
        mp.update(consts)
        in_maps.append(mp)
    print("phase_C: n_inst", p.n_inst, "n_wait", p.n_wait, flush=True)
    res = _run(p, in_maps)
    return [r["o"] for r in res]


def phase_F(xr_cores, gate_cores, xp_cores, cw_cores, wa_cores, wx_cores, vec_cores, selw_cores, btab_cores, LX, LC):
    LT = LX + LC
    p = Prog()
    xrd = p.dram("xr", [128, LT], "ExternalInput"); gd = p.dram("gate", [128, LX], "ExternalInput"); xpd = p.dram("xp", [128, LX], "ExternalInput")
    cwd = p.dram("cw", [128, 5], "ExternalInput")
    wad = p.dram("wa", [2, 128, 128], "ExternalInput"); wxd = p.dram("wx", [2, 128, 128], "ExternalInput")
    vd = p.dram("vec", [128, 6], "ExternalInput")
    sd = p.dram("selw", [128, 4], "ExternalInput"); bd = p.dram("btab", [128, 4, 16], "ExternalInput")
    outd = p.dram("o", [2, 128, LX], "ExternalOutput")
    cw = p.sbuf("cw_s", [128, 5]); wa = p.sbuf("wa_s", [128, 2, 128]); wx = p.sbuf("wx_s", [128, 2, 128]); vec = p.sbuf("vec_s", [128, 6])
    selw = p.sbuf("selw_s", [128, 4]); btab = p.sbuf("btab_s", [128, 4, 16])
    for t, dd in [(cw, cwd), (vec, vd), (selw, sd), (btab, bd)]:
        p.dma("sp", t[:], dd.ap(), writes=[t])
    p.dma("sp", wa[:], wad.ap().rearrange("d i j -> i d j"), writes=[wa])
    p.dma("sp", wx[:], wxd.ap().rearrange("d i j -> i d j"), writes=[wx])
    ones = emit_consts(p)
    raw = p.sbuf("raw", [128, LT]); xr = p.sbuf("xrs", [128, LT])
    A = p.sbuf("A", [128, LX]); Bv = p.sbuf("Bv", [128, LX]); Hf = p.sbuf("Hf", [128, LX]); Hb = p.sbuf("Hb", [128, LX])
    Hc = p.sbuf("Hc", [128, 2, LC])
    seqs = [("c", LX, LC), ("x", 0, LX)]
    p.dma("sp", raw[:], xrd.ap(), writes=[raw])
    for nm, o, L in seqs:
        p.op("dve", lambda e: e.tensor_scalar(out=xr[:, o:o + L], in0=raw[:, o:o + L], scalar1=cw[:, 1:2], scalar2=cw[:, 4:5], op0=ALU.mult, op1=ALU.add), reads=[raw, cw], writes=[xr])
        p.op("dve", lambda e: e.scalar_tensor_tensor(out=xr[:, o + 1:o + L], in0=raw[:, o:o + L - 1], scalar=cw[:, 0:1], in1=xr[:, o + 1:o + L], op0=ALU.mult, op1=ALU.add), reads=[raw, cw, xr], writes=[xr])
        p.op("dve", lambda e: e.scalar_tensor_tensor(out=xr[:, o:o + L - 1], in0=raw[:, o + 1:o + L], scalar=cw[:, 2:3], in1=xr[:, o:o + L - 1], op0=ALU.mult, op1=ALU.add), reads=[raw, cw, xr], writes=[xr])
        p.op("dve", lambda e: e.scalar_tensor_tensor(out=xr[:, o:o + L - 2], in0=raw[:, o + 2:o + L], scalar=cw[:, 3:4], in1=xr[:, o:o + L - 2], op0=ALU.mult, op1=ALU.add), reads=[raw, cw, xr], writes=[xr])
    m8 = p.sbuf("m8", [128, 2])
    p.op("act", lambda e: e.activation(out=m8[:], in_=vec[:, 4:6], func=AF.Exp, scale=-1.0), reads=[vec], writes=[m8])
    p.op("act", lambda e: e.activation(out=m8[:], in_=m8[:], func=AF.Ln, bias=1.0), reads=[m8], writes=[m8])
    p.op("dve", lambda e: e.tensor_scalar(out=m8[:], in0=m8[:], scalar1=-8.0, scalar2=None, op0=ALU.mult), reads=[m8], writes=[m8])
    pr = Ring([p.psum(f"pr{i}", [128, 512]) for i in range(4)])
    rT = Ring([p.sbuf(f"rT{i}", [128, 512]) for i in range(1)]); iT = Ring([p.sbuf(f"iT{i}", [128, 512]) for i in range(1)])
    a2 = Ring([p.sbuf(f"a2{i}", [128, 512]) for i in range(1)])
    for d_ in range(2):
        H = Hf if d_ == 0 else Hb
        for nm, o, L in seqs:
            for c0 in range(0, L, 512):
                n = min(512, L - c0)
                ps1 = pr.next(); ps2 = pr.next()
                p.op("pe", lambda e: e.matmul(ps1[:, 0:n], wa[:, d_, :], xr[:, o + c0:o + c0 + n], start=True, stop=True), reads=[wa, xr], writes=[ps1], skip_self=True)
                p.op("pe", lambda e: e.matmul(ps2[:, 0:n], wx[:, d_, :], xr[:, o + c0:o + c0 + n], start=True, stop=True), reads=[wx, xr], writes=[ps2], skip_self=True)
                r_ = rT.next(); i_ = iT.next(); q_ = a2.next()
                p.op("act", lambda e: e.activation(out=r_[:, 0:n], in_=ps1[:, 0:n], func=AF.Sigmoid, bias=vec[:, d_:d_ + 1]), reads=[ps1, vec], writes=[r_])
                p.op("act", lambda e: e.activation(out=i_[:, 0:n], in_=ps2[:, 0:n], func=AF.Sigmoid, bias=vec[:, 2 + d_:3 + d_]), reads=[ps2, vec], writes=[i_])
                p.op("act", lambda e: e.activation(out=A[:, c0:c0 + n], in_=r_[:, 0:n], func=AF.Exp, scale=m8[:, d_:d_ + 1]), reads=[r_, m8], writes=[A.r(c0)])
                p.op("dve", lambda e: e.tensor_tensor(q_[:, 0:n], A[:, c0:c0 + n], A[:, c0:c0 + n], ALU.mult), reads=[A.r(c0)], writes=[q_])
                p.op("dve", lambda e: e.tensor_scalar(out=q_[:, 0:n], in0=q_[:, 0:n], scalar1=-1.0, scalar2=1.0, op0=ALU.mult, op1=ALU.add), reads=[q_], writes=[q_])
                p.op("act", lambda e: e.activation(out=q_[:, 0:n], in_=q_[:, 0:n], func=AF.Sqrt), reads=[q_], writes=[q_])
                p.op("dve", lambda e: e.tensor_tensor(i_[:, 0:n], i_[:, 0:n], xr[:, o + c0:o + c0 + n], ALU.mult), reads=[i_, xr], writes=[i_])
                p.op("dve", lambda e: e.tensor_tensor(Bv[:, c0:c0 + n], i_[:, 0:n], q_[:, 0:n], ALU.mult), reads=[i_, q_], writes=[Bv.r(c0)])
            if nm == "c":
                dst = Hc[:, d_, :]; dreg = Hc
                init = 0.0
            else:
                dst = H[:, 0:L]; dreg = H
                init = Hc[:, 0, LC - 1:LC] if d_ == 0 else Hc[:, 1, 0:1]
            if d_ == 0:
                p.op("dve", lambda e: e.tensor_tensor_scan(dst, A[:, 0:L], Bv[:, 0:L], init, ALU.mult, ALU.add), reads=[A, Bv, Hc], writes=[dreg])
            else:
                p.op("dve", lambda e: e.tensor_tensor_scan(dst[:, ::-1], A[:, 0:L][:, ::-1], Bv[:, 0:L][:, ::-1], init, ALU.mult, ALU.add), reads=[A, Bv, Hc], writes=[dreg])
    p.dma("sp", A[:], gd.ap(), writes=[A])
    p.op("dve", lambda e: e.tensor_tensor(Hf[:], Hf[:], Hb[:], ALU.add), reads=[Hf, Hb], writes=[Hf])
    p.op("dve", lambda e: e.tensor_tensor(Hf[:], Hf[:], A[:], ALU.mult), reads=[Hf, A], writes=[Hf])
    p.dma("pool", outd.ap()[0], Hf[:], reads=[Hf], writes=[outd.r(0)])
    L = LX
    xpb = Bv; diff = Hb; accp = A
    csp = raw
    p.dma("sp", xpb[:], xpd.ap(), writes=[xpb])
    p.op("dve", lambda e: e.memset(csp[:, 0:9], 0.0), writes=[csp])
    p.op("dve", lambda e: e.tensor_tensor_scan(csp[:, 9:9 + L], ones[:, 0:1].broadcast_to([128, L]), xpb[:], 0.0, ALU.mult, ALU.add), reads=[ones, xpb, csp], writes=[csp])
    p.op("dve", lambda e: e.tensor_copy(csp[:, 9 + L:17 + L], csp[:, 8 + L:9 + L].broadcast_to([128, 8])), reads=[csp], writes=[csp])
    bacc = p.sbuf("bacc", [128, 16]); bt = p.sbuf("bt", [128, 16])
    for wi, w in enumerate((2, 4, 8, 16)):
        hf_ = w // 2
        p.op("dve", lambda e: e.tensor_tensor(diff[:], csp[:, 8 + hf_:8 + hf_ + L], csp[:, 8 - hf_:8 - hf_ + L], ALU.subtract), reads=[csp], writes=[diff])
        if wi == 0:
            p.op("dve", lambda e: e.tensor_scalar(out=accp[:], in0=diff[:], scalar1=selw[:, wi:wi + 1], scalar2=None, op0=ALU.mult), reads=[diff, selw], writes=[accp])
        else:
            p.op("dve", lambda e: e.scalar_tensor_tensor(out=accp[:], in0=diff[:], scalar=selw[:, wi:wi + 1], in1=accp[:], op0=ALU.mult, op1=ALU.add), reads=[diff, selw, accp], writes=[accp])
        for half, cols in ((0, slice(0, 8)), (1, slice(L - 8, L))):
            bs_ = slice(8 * half, 8 * half + 8)
            if wi == 0:
                p.op("dve", lambda e: e.tensor_tensor(bacc[:, bs_], diff[:, cols], btab[:, wi, bs_], ALU.mult), reads=[diff, btab], writes=[bacc])
            else:
                p.op("dve", lambda e: e.tensor_tensor(bt[:, bs_], diff[:, cols], btab[:, wi, bs_], ALU.mult), reads=[diff, btab], writes=[bt])
                p.op("dve", lambda e: e.tensor_tensor(bacc[:, bs_], bacc[:, bs_], bt[:, bs_], ALU.add), reads=[bacc, bt], writes=[bacc])
    p.op("dve", lambda e: e.tensor_copy(accp[:, 0:8], bacc[:, 0:8]), reads=[bacc], writes=[accp])
    p.op("dve", lambda e: e.tensor_copy(accp[:, L - 8:L], bacc[:, 8:16]), reads=[bacc], writes=[accp])
    p.op("dve", lambda e: e.tensor_tensor(accp[:], accp[:], xpb[:], ALU.subtract), reads=[accp, xpb], writes=[accp])
    p.dma("pool", outd.ap()[1], accp[:], reads=[accp], writes=[outd.r(1)])
    in_maps = []
    for i in range(NCORES):
        in_maps.append({"xr": xr_cores[i], "gate": gate_cores[i], "xp": xp_cores[i], "cw": cw_cores[i], "wa": wa_cores[i], "wx": wx_cores[i],
                        "vec": vec_cores[i], "selw": selw_cores[i], "btab": btab_cores[i]})
    res = _run(p, in_maps)
    return [r["o"] for r in res]


def _assemble(projs, NX, NCX):
    px = np.concatenate([o[:, :, :NX] for o in projs], axis=2)
    if NCX:
        pc = np.concatenate([o[:, :, NX:] for o in projs], axis=2)
        return np.concatenate([px, pc], axis=2)
    return px


def _token_shard(full, NX, NCX, LX):
    outs = []
    for i in range(NCORES):
        parts = [full[:, :, i * NX:(i + 1) * NX]]
        if NCX:
            parts.append(full[:, :, LX + i * NCX:LX + (i + 1) * NCX])
        outs.append(np.ascontiguousarray(np.concatenate(parts, axis=2).transpose(1, 0, 2)))
    return outs


def kernel(x, c, ctx, c_ctx, ada_w, ada_b, norm_mix, norm_ffn, norm_final,
           ev_w_in, ev_conv_w, ev_conv_b, hy_w1, hy_b1, hy_w2, hy_b2, hy_w3, hy_freq, hy_deltas, hy_bias,
           gm_norm, gm_ws, gm_bs, ev_w_out,
           od_w_in, od_conv_w, od_conv_b, lru_wa, lru_ba, lru_wx, lru_bx, lru_lam,
           pool_w, pool_b, pool_scale, od_w_out,
           peer_q, peer_keys, peer_u, peer_v):
    f32 = lambda a: np.ascontiguousarray(np.asarray(a, dtype=np.float32))
    x = f32(x)[0]; ctxa = f32(ctx)[0]
    LX, LC, NX, NCX = 8192, 256, 1024, 32
    mod = phase_A(f32(c), f32(c_ctx), f32(ada_w), f32(ada_b))

    def mvv(l, which, col):
        return vec_fm(mod[l][which * 2048:(which + 1) * 2048, col])

    def mvB(l):
        return np.ascontiguousarray(np.stack([vec_fm(f32(norm_mix)[l]), mvv(l, 0, 0), mvv(l, 1, 0), mvv(l, 0, 1), mvv(l, 1, 1)], axis=2))

    def mvD(l):
        return np.ascontiguousarray(np.stack([mvv(l, 2, 0), mvv(l, 2, 1), vec_fm(f32(norm_ffn)[l]), mvv(l, 3, 0), mvv(l, 4, 0),
                                              mvv(l, 3, 1), mvv(l, 4, 1), mvv(l, 5, 0), mvv(l, 5, 1)], axis=2))

    def peer_args(l):
        keysT = np.ascontiguousarray(f32(peer_keys)[l].reshape(16, 128, 128).transpose(0, 2, 1))
        uT = np.ascontiguousarray(f32(peer_u)[l].reshape(128, 128, 16, 128).transpose(0, 3, 2, 1).reshape(128, 128, 2048))
        vt = np.ascontiguousarray(f32(peer_v)[l].reshape(128, 128, 2048))
        return wblocks(f32(peer_q)[l]), keysT, uT, vt

    xTs = [fm(np.concatenate([x[i * NX:(i + 1) * NX], ctxa[i * NCX:(i + 1) * NCX]], 0)) for i in range(NCORES)]
    epi0 = ["copy"] * 24 + ["gelu"] * 8 + ["vg"] * 8
    projs = phase_B(xTs, [mvB(0)] * NCORES, wblocks(f32(ev_w_in)[0]), epi0, NX, NCX, gmn=vec_fm(f32(gm_norm)[0]))
    PT = _assemble(projs, NX, NCX)
    cwf = f32(ev_conv_w)[0]; cbf = f32(ev_conv_b)[0]; w3f = f32(hy_w3)[0]; dlf = f32(hy_deltas)[0]; hbf = f32(hy_bias)[0]
    P3c, cwc, cbc, w3c, dlc, hbc, vgc, uc, wsc, bsc = [], [], [], [], [], [], [], [], [], []
    for i in range(NCORES):
        blk = slice(128 * i, 128 * i + 128)
        P3c.append(np.ascontiguousarray(PT[[i, 8 + i, 16 + i]]))
        cwc.append(np.ascontiguousarray(np.stack([cwf[k, j * 1024 + 128 * i:j * 1024 + 128 * i + 128] for j in range(3) for k in range(3)], axis=1)))
        cbc.append(np.ascontiguousarray(np.stack([cbf[j * 1024 + 128 * i:j * 1024 + 128 * i + 128] for j in range(3)], axis=1)))
        cols = np.concatenate([np.arange(q * 1024 + 128 * i, q * 1024 + 128 * i + 128) for q in range(4)])
        w3c.append(np.ascontiguousarray(w3f[:, cols]))
        dlc.append(np.ascontiguousarray(dlf[cols].reshape(4, 128).T))
        hbc.append(np.ascontiguousarray(hbf[:, blk].T))
        vgc.append(np.ascontiguousarray(PT[32 + i].T.reshape((LX + LC) // 128, 128, 128)))
        uc.append(np.ascontiguousarray(PT[24 + i]))
        wsc.append(np.ascontiguousarray(f32(gm_ws)[0][i].T))
        bsc.append(np.ascontiguousarray(f32(gm_bs)[0][i][None, :]))
    couts = phase_C(P3c, cwc, cbc, w3c, dlc, hbc, vgc, uc, wsc, bsc, f32(hy_w1)[0], f32(hy_b1)[0], f32(hy_w2)[0], f32(hy_b2)[0],
                    f32(hy_freq)[0], LX, LC)
    CAT = np.concatenate([np.stack([o[0] for o in couts]), np.stack([o[1] for o in couts])], axis=0)
    cats = _token_shard(CAT, NX, NCX, LX)
    pq, keysT, uT, vt = peer_args(0)
    x1Ts = phase_D(cats, xTs, [mvD(0)] * NCORES, wblocks(f32(ev_w_out)[0]), pq, keysT, uT, vt, NX, NCX)
    del uT, vt
    epi1 = ["gelu"] * 8 + ["copy"] * 16
    projs1 = phase_B(x1Ts, [mvB(1)] * NCORES, wblocks(f32(od_w_in)[0]), epi1, NX, NCX)
    PT1 = _assemble(projs1, NX, NCX)
    ocw = f32(od_conv_w)[0]; ocb = f32(od_conv_b)[0]
    xrc, gc, xpc, cw1, wac, wxc, vcc, swc, btc = [], [], [], [], [], [], [], [], []
    for i in range(NCORES):
        blk = slice(128 * i, 128 * i + 128)
        xrc.append(np.ascontiguousarray(PT1[8 + i]))
        gc.append(np.ascontiguousarray(PT1[i][:, :LX]))
        xpc.append(np.ascontiguousarray(PT1[16 + i][:, :LX]))
        cw1.append(np.ascontiguousarray(np.stack([ocw[0, blk], ocw[1, blk], ocw[2, blk], ocw[3, blk], ocb[blk]], axis=1)))
        wac.append(np.ascontiguousarray(f32(lru_wa)[0][:, i]))
        wxc.append(np.ascontiguousarray(f32(lru_wx)[0][:, i]))
        vcc.append(np.ascontiguousarray(np.stack([f32(lru_ba)[0][0, blk], f32(lru_ba)[0][1, blk], f32(lru_bx)[0][0, blk], f32(lru_bx)[0][1, blk],
                                                  f32(lru_lam)[0][0, blk], f32(lru_lam)[0][1, blk]], axis=1)))
        g = i // 2
        sw = np.zeros((128, 4), np.float32); bt = np.zeros((128, 4, 16), np.float32)
        w = (2, 4, 8, 16)[g]; half = w // 2
        sw[:, g] = 1.0 / w
        tcols = np.concatenate([np.arange(8), np.arange(LX - 8, LX)])
        cnt = np.minimum(tcols + half, LX) - np.maximum(tcols - half, 0)
        bt[:, g, :] = (1.0 / cnt.astype(np.float32))[None, :]
        swc.append(sw); btc.append(bt)
    fouts = phase_F(xrc, gc, xpc, cw1, wac, wxc, vcc, swc, btc, LX, LC)
    CAT1 = np.concatenate([np.stack([o[0] for o in fouts]), np.stack([o[1] for o in fouts])], axis=0)
    cats1 = _token_shard(CAT1, NX, 0, LX)
    x1only = [np.ascontiguousarray(a[:, :, :NX]) for a in x1Ts]
    pq, keysT, uT, vt = peer_args(1)
    pw = np.ascontiguousarray(f32(pool_w)[0].reshape(4, 2, 128, 256))
    pp = np.ascontiguousarray(np.stack([vec_fm(f32(pool_scale)[0]), vec_fm(f32(pool_b)[0].reshape(-1))], axis=2))
    outs = phase_D(cats1, x1only, [mvD(1)] * NCORES, wblocks(f32(od_w_out)[0]), pq, keysT, uT, vt, NX, 0,
                   final_g=vec_fm(f32(norm_final)), pool=(pw, pp))
    out = np.concatenate([unfm(o) for o in outs], axis=0)
    return out[None].astype(np.float32)
```

```python
import contextlib
import numpy as np
import concourse.bass as bass
import concourse.mybir as mybir
from concourse.bass_utils import run_bass_kernel_spmd

F32 = mybir.dt.float32
AF = mybir.ActivationFunctionType
ALU = mybir.AluOpType
AX = mybir.AxisListType


class Reg:
    __slots__ = ("name", "last_w", "reads")

    def __init__(self, name):
        self.name = name
        self.last_w = None
        self.reads = {}


class TT:
    def __init__(self, name, handle):
        self.name = name
        self.h = handle
        self.whole = Reg(name)
        self.sub = {}

    def ap(self):
        return self.h.ap() if hasattr(self.h, "ap") and callable(self.h.ap) else self.h

    def __getitem__(self, idx):
        return self.h[idx]

    def r(self, key=None):
        if key is None:
            return (self, None)
        return (self, key)


class Prog:
    SEM_ROLL = 2000

    def __init__(self, n_dma_sems=24):
        self.nc = bass.Bass("TRN2", target_bir_lowering=False)
        nc = self.nc
        self.stack = contextlib.ExitStack()
        self.engs = {"pe": nc.tensor, "dve": nc.vector, "act": nc.scalar, "pool": nc.gpsimd, "sp": nc.sync}
        self.sems = {}
        self.cur_sem = {}
        self.cnt = {}
        self.sem_gen = {}
        for e in self.engs:
            self.sem_gen[e] = 0
            self._new_eng_sem(e)
        self.dma_sems = []
        for i in range(n_dma_sems):
            nm = f"dq{i}"
            self.sems[nm] = self.stack.enter_context(nc.semaphore(nm))
            self.cnt[nm] = 0
            self.dma_sems.append(nm)
        self.dma_rr = 0
        self.waited = {e: {} for e in self.engs}
        self.n_inst = 0
        self.n_wait = 0
        self._uid = 0

    def _new_eng_sem(self, e):
        nm = f"s_{e}_{self.sem_gen[e]}"
        self.sem_gen[e] += 1
        self.sems[nm] = self.stack.enter_context(self.nc.semaphore(nm))
        self.cnt[nm] = 0
        self.cur_sem[e] = nm

    def sbuf(self, name, shape, dtype=F32):
        h = self.stack.enter_context(self.nc.sbuf_tensor(name, list(shape), dtype))
        return TT(name, h)

    def psum(self, name, shape, dtype=F32):
        h = self.stack.enter_context(self.nc.psum_tensor(name, list(shape), dtype))
        return TT(name, h)

    def dram(self, name, shape, kind, dtype=F32):
        h = self.nc.dram_tensor(name, list(shape), dtype, kind=kind)
        t = TT(name, h)
        return t

    def _regs(self, lst):
        out = []
        for x in lst:
            if isinstance(x, TT):
                out.append((x, None))
            else:
                out.append(x)
        return out

    def _collect(self, reads, writes):
        deps = {}

        def add(ev):
            if ev is None:
                return
            s, v = ev
            if deps.get(s, 0) < v:
                deps[s] = v

        for (t, k) in self._regs(reads):
            add(t.whole.last_w)
            if k is None:
                for r in t.sub.values():
                    add(r.last_w)
            else:
                r = t.sub.get(k)
                if r is not None:
                    add(r.last_w)
        for (t, k) in self._regs(writes):
            regs = [t.whole]
            if k is None:
                regs += list(t.sub.values())
            else:
                r = t.sub.get(k)
                if r is not None:
                    regs.append(r)
            for r in regs:
                add(r.last_w)
                for s, v in r.reads.items():
                    add((s, v))
        return deps

    def _record(self, reads, writes, ev):
        s, v = ev
        for (t, k) in self._regs(reads):
            if k is None:
                r = t.whole
            else:
                r = t.sub.get(k)
                if r is None:
                    r = t.sub[k] = Reg(f"{t.name}:{k}")
            if r.reads.get(s, 0) < v:
                r.reads[s] = v
        for (t, k) in self._regs(writes):
            if k is None:
                t.whole.last_w = ev
                t.whole.reads = {}
                t.sub = {}
            else:
                r = t.sub.get(k)
                if r is None:
                    r = t.sub[k] = Reg(f"{t.name}:{k}")
                r.last_w = ev
                r.reads = {}

    def _emit_waits(self, e, deps, skip_self=False):
        eng = self.engs[e]
        w = self.waited[e]
        for s, v in deps.items():
            if skip_self and s.startswith(f"s_{e}_"):
                continue
            if w.get(s, 0) >= v:
                continue
            eng.wait_ge(self.sems[s], v)
            w[s] = v
            self.n_wait += 1

    def op(self, e, fn, reads=(), writes=(), skip_self=False):
        deps = self._collect(reads, writes)
        self._emit_waits(e, deps, skip_self=skip_self)
        s = self.cur_sem[e]
        if self.cnt[s] >= self.SEM_ROLL:
            self._new_eng_sem(e)
            s = self.cur_sem[e]
        ins = fn(self.engs[e])
        ins.then_inc(self.sems[s], 1)
        self.cnt[s] += 1
        ev = (s, self.cnt[s])
        self._record(reads, writes, ev)
        self.n_inst += 1
        return ev

    def dma(self, q, out, in_, reads=(), writes=(), **kw):
        deps = self._collect(reads, writes)
        s = self.dma_sems[self.dma_rr % len(self.dma_sems)]
        self.dma_rr += 1
        n = self.cnt[s]
        if n > 0:
            if deps.get(s, 0) < 16 * n:
                deps[s] = 16 * n
        self._emit_waits(q, deps)
        ins = self.engs[q].dma_start(out=out, in_=in_, **kw)
        ins.then_inc(self.sems[s], 16)
        self.cnt[s] = n + 1
        ev = (s, 16 * (n + 1))
        self._record(reads, writes, ev)
        self.n_inst += 1
        return ev

    def finish(self, q="sp"):
        deps = {}
        for s, c in self.cnt.items():
            if c == 0:
                continue
            v = c * 16 if s.startswith("dq") else c
            deps[s] = v
        self._emit_waits(q, deps)

    def close(self):
        self.stack.close()


NCORES = 8
D = 2048
KC = 16
EPS = 1e-6


def _run(p, in_maps):
    p.finish()
    res = run_bass_kernel_spmd(p.nc, in_maps, core_ids=list(range(NCORES)))
    p.close()
    return res.results


def fm(x2d):
    T, F = x2d.shape
    return np.ascontiguousarray(x2d.T.reshape(F // 128, 128, T).transpose(1, 0, 2))


def unfm(a):
    P, C, T = a.shape
    return np.ascontiguousarray(a.transpose(2, 1, 0).reshape(T, C * P))


def vec_fm(v):
    return np.ascontiguousarray(v.reshape(-1, 128).T)


def wblocks(w):
    K, N = w.shape
    return np.ascontiguousarray(w.reshape(K // 128, 128, N // 128, 128).transpose(2, 1, 0, 3).reshape(N // 128, 128, (K // 128) * 128))


def phase_A(c, c_ctx, ada_w, ada_b):
    NJ = 12
    p = Prog()
    cT = p.dram("cT", [128, KC, 2], "ExternalInput")
    aw = p.dram("aw", [2, KC, 128, NJ * 128], "ExternalInput")
    ab = p.dram("ab", [128, 2, NJ], "ExternalInput")
    mo = p.dram("mo", [128, 2, NJ, 2], "ExternalOutput")
    cs = p.sbuf("cs", [128, KC, 2]); ss = p.sbuf("ss", [128, KC, 2]); abs_ = p.sbuf("abs", [128, 2, NJ])
    wb = p.sbuf("wb", [128, KC, NJ * 128]); mos = p.sbuf("mos", [128, 2, NJ, 2])
    ps = p.psum("ps", [128, 512])
    p.dma("sp", cs[:], cT.ap(), writes=[cs])
    p.dma("sp", abs_[:], ab.ap(), writes=[abs_])
    p.op("act", lambda e: e.activation(out=ss[:], in_=cs[:], func=AF.Silu), reads=[cs], writes=[ss])
    for l in range(2):
        for kc in range(KC):
            p.dma("sp", wb[:, kc, :], aw.ap()[l, kc], writes=[wb.r(kc)])
        for j in range(NJ):
            for kc in range(KC):
                p.op("pe", lambda e: e.matmul(ps[:, 2 * j:2 * j + 2], wb[:, kc, j * 128:(j + 1) * 128], ss[:, kc, :],
                                              start=(kc == 0), stop=(kc == KC - 1)),
                     reads=[wb.r(kc), ss], writes=[ps.r(j)], skip_self=True)
            p.op("act", lambda e: e.activation(out=mos[:, l, j, :], in_=ps[:, 2 * j:2 * j + 2], func=AF.Identity,
                                               bias=abs_[:, l, j:j + 1]), reads=[ps.r(j), abs_], writes=[mos.r((l, j))])
    p.dma("sp", mo.ap(), mos[:], reads=[mos], writes=[mo])
    cTh = np.stack([vec_fm(c.reshape(-1)), vec_fm(c_ctx.reshape(-1))], axis=2)
    in_maps = []
    for i in range(NCORES):
        cols = slice(i * NJ * 128, (i + 1) * NJ * 128)
        awi = np.ascontiguousarray(ada_w[:, :, cols].reshape(2, KC, 128, NJ * 128))
        abi = np.ascontiguousarray(ada_b[:, cols].reshape(2, NJ, 128).transpose(2, 0, 1))
        in_maps.append({"cT": cTh, "aw": awi, "ab": abi})
    res = _run(p, in_maps)
    mod = np.zeros((2, 12288, 2), np.float32)
    for i in range(NCORES):
        r = res[i]["mo"]
        mod[:, i * NJ * 128:(i + 1) * NJ * 128, :] = r.transpose(1, 2, 0, 3).reshape(2, NJ * 128, 2)
    return mod


class Ring:
    def __init__(self, items):
        self.items = items
        self.i = 0

    def next(self):
        t = self.items[self.i % len(self.items)]
        self.i += 1
        return t


def emit_consts(p):
    ones = p.sbuf("c_ones", [128, 128])
    p.op("dve", lambda e: e.memset(ones[:], 1.0), writes=[ones])
    return ones


def emit_gp(p, g, sc, gp):
    p.op("dve", lambda e: e.tensor_scalar(out=gp[:], in0=sc[:], scalar1=1.0, scalar2=None, op0=ALU.add), reads=[sc], writes=[gp])
    p.op("dve", lambda e: e.tensor_tensor(gp[:], gp[:], g[:], ALU.mult), reads=[gp, g], writes=[gp])


def emit_norm_mod(p, src, dst, C, off, n, gp, sh, ones, scr, stag, dtag, doff=None):
    if doff is None:
        doff = off
    sq, ps, rt, rstd = scr["sq"], scr["ps"], scr["rt"], scr["rstd"]
    p.op("act", lambda e: e.activation(out=sq[:, 0:C, 0:n], in_=src[:, 0:C, off:off + n], func=AF.Square),
         reads=[src.r(stag)], writes=[sq])
    for c in range(C):
        p.op("pe", lambda e: e.matmul(ps[:, 0:n], ones[:], sq[:, c, 0:n], start=(c == 0), stop=(c == C - 1)),
             reads=[sq, ones], writes=[ps], skip_self=True)
    p.op("act", lambda e: e.activation(out=rt[:, 0:n], in_=ps[:, 0:n], func=AF.Sqrt, scale=1.0 / (128 * C), bias=EPS),
         reads=[ps], writes=[rt])
    p.op("dve", lambda e: e.reciprocal(rstd[:, 0:n], rt[:, 0:n]), reads=[rt], writes=[rstd])
    p.op("dve", lambda e: e.tensor_tensor(sq[:, 0:C, 0:n], src[:, 0:C, off:off + n],
                                          rstd[:, 0:n].unsqueeze(1).broadcast_to([128, C, n]), ALU.mult),
         reads=[src.r(stag), rstd], writes=[sq])
    for c in range(C):
        if sh is not None:
            p.op("act", lambda e: e.activation(out=dst[:, c, doff:doff + n], in_=sq[:, c, 0:n], func=AF.Identity,
                                               scale=gp[:, c:c + 1], bias=sh[:, c:c + 1]),
                 reads=[sq, gp, sh], writes=[dst.r(dtag)])
        else:
            p.op("act", lambda e: e.activation(out=dst[:, c, doff:doff + n], in_=sq[:, c, 0:n], func=AF.Identity,
                                               scale=gp[:, c:c + 1]),
                 reads=[sq, gp], writes=[dst.r(dtag)])


def norm_scratch(p, C, nmax, pfx=""):
    return {"sq": p.sbuf(pfx + "n_sq", [128, C, nmax]), "ps": p.psum(pfx + "n_ps", [128, 512]),
            "rt": p.sbuf(pfx + "n_rt", [128, nmax]), "rstd": p.sbuf(pfx + "n_rstd", [128, nmax])}


def split_tiles(nx, nc_, tmax):
    tiles = []
    o = 0
    while o < nx:
        n = min(tmax, nx - o)
        tiles.append((o, n, "x"))
        o += n
    o = 0
    while o < nc_:
        n = min(tmax, nc_ - o)
        tiles.append((nx + o, n, "c"))
        o += n
    return tiles


def phase_B(xT_cores, mv_cores, wblk, epi, NX, NCX, gmn=None):
    M = wblk.shape[0]
    NT = NX + NCX
    tiles = split_tiles(NX, NCX, 512)
    p = Prog()
    xTd = p.dram("xT", [128, KC, NT], "ExternalInput")
    mvd = p.dram("mv", [128, KC, 5], "ExternalInput")
    wd = p.dram("w", [M, 128, KC * 128], "ExternalInput")
    od = p.dram("o", [M, 128, NT], "ExternalOutput")
    ones = emit_consts(p)
    xs = p.sbuf("xs", [128, KC, 512]); hT = p.sbuf("hT", [128, KC, NT]); mv = p.sbuf("mvs", [128, KC, 5])
    gpx = p.sbuf("gpx", [128, KC]); gpc = p.sbuf("gpc", [128, KC])
    g = p.sbuf("g_", [128, KC]); shx = p.sbuf("shx", [128, KC]); scx = p.sbuf("scx", [128, KC])
    shc = p.sbuf("shc", [128, KC]); scc = p.sbuf("scc", [128, KC])
    scr = norm_scratch(p, KC, 512)
    p.dma("sp", mv[:], mvd.ap(), writes=[mv])
    for i, t in enumerate([g, shx, scx, shc, scc]):
        p.op("dve", lambda e: e.tensor_copy(t[:], mv[:, :, i]), reads=[mv], writes=[t])
    emit_gp(p, g, scx, gpx)
    emit_gp(p, g, scc, gpc)
    for (off, n, kind) in tiles:
        p.dma("sp", xs[:, :, 0:n], xTd.ap()[:, :, off:off + n], writes=[xs])
        emit_norm_mod(p, xs, hT, KC, 0, n, gpx if kind == "x" else gpc, shx if kind == "x" else shc, ones, scr, None, off, doff=off)
    wring = Ring([p.sbuf(f"wb{i}", [128, KC * 128]) for i in range(3)])
    pring = Ring([p.psum(f"pm{i}", [128, 512]) for i in range(4)])
    oring = Ring([p.sbuf(f"ob{i}", [128, NT]) for i in range(2)])
    vgT = None
    if gmn is not None:
        vgT = p.sbuf("vgT", [128, 8, NT])
        gm = p.sbuf("gm", [128, 8])
        gmd = p.dram("gmn", [128, 8], "ExternalInput")
        p.dma("sp", gm[:], gmd.ap(), writes=[gm])
    nvg = 0
    for m in range(M):
        wb = wring.next()
        p.dma("sp", wb[:], wd.ap()[m], writes=[wb])
        if epi[m] == "vg":
            ob = None
        else:
            ob = oring.next()
        for (off, n, kind) in tiles:
            ps = pring.next()
            for kc in range(KC):
                p.op("pe", lambda e: e.matmul(ps[:, 0:n], wb[:, kc * 128:(kc + 1) * 128], hT[:, kc, off:off + n],
                                              start=(kc == 0), stop=(kc == KC - 1)),
                     reads=[wb, hT.r(off)], writes=[ps], skip_self=True)
            if epi[m] == "copy":
                p.op("dve", lambda e: e.tensor_copy(ob[:, off:off + n], ps[:, 0:n]), reads=[ps], writes=[ob.r(off)])
            elif epi[m] == "gelu":
                p.op("act", lambda e: e.activation(out=ob[:, off:off + n], in_=ps[:, 0:n], func=AF.Gelu), reads=[ps], writes=[ob.r(off)])
            else:
                p.op("act", lambda e: e.activation(out=vgT[:, nvg, off:off + n], in_=ps[:, 0:n], func=AF.Gelu), reads=[ps], writes=[vgT.r(off)])
        if epi[m] == "vg":
            nvg += 1
        else:
            p.dma("pool", od.ap()[m], ob[:], reads=[ob], writes=[od.r(m)])
    if gmn is not None:
        vo = vgT
        m0 = epi.index("vg")
        for (off, n, kind) in tiles:
            emit_norm_mod(p, vgT, vo, 8, off, n, gm, None, ones, scr, off, off)
        for j in range(8):
            p.dma("pool", od.ap()[m0 + j], vo[:, j, :], reads=[vo], writes=[od.r(m0 + j)])
    in_maps = []
    for i in range(NCORES):
        mp = {"xT": xT_cores[i], "mv": mv_cores[i], "w": wblk}
        if gmn is not None:
            mp["gmn"] = gmn
        in_maps.append(mp)
    res = _run(p, in_maps)
    return [r["o"] for r in res]


NEG = -1.0e30
_DBG_NA = 128
_DBG_NP = 99
PASS_N = 352
GA = 2


def phase_D(catT_cores, xT_cores, mv_cores, woblk, pqblk, keysT, uT, vt, NX, NCX, final_g=None, pool=None):
    NT = NX + NCX
    tiles = split_tiles(NX, NCX, 512)
    passes = []
    o = 0
    while o < NT:
        n = min(PASS_N, NT - o)
        passes.append((o, n))
        o += n
    p = Prog()
    catd = p.dram("catT", [128, KC, NT], "ExternalInput")
    xd = p.dram("xT", [128, KC, NT], "ExternalInput")
    mvd = p.dram("mv", [128, KC, 9], "ExternalInput")
    wod = p.dram("wo", [16, 128, 2048], "ExternalInput")
    pqd = p.dram("pq", [16, 128, 2048], "ExternalInput")
    kd = p.dram("keysT", [16, 128, 128], "ExternalInput")
    ud = p.dram("uT", [128, 128, 2048], "ExternalInput")
    vd = p.dram("vt", [128, 128, 2048], "ExternalInput")
    idd = p.dram("ident", [128, 128], "ExternalInput")
    x1d = p.dram("x1s", [128, KC, NT], "Internal")
    outd = p.dram("o", [128, KC, NT], "ExternalOutput")
    ones = emit_consts(p)
    ident = p.sbuf("ident_s", [128, 128])
    p.dma("sp", ident[:], idd.ap(), writes=[ident])
    mv = p.sbuf("mvs", [128, KC, 9])
    p.dma("sp", mv[:], mvd.ap(), writes=[mv])
    names = ["g1x", "g1c", "gf", "sh2x", "sc2x", "sh2c", "sc2c", "g2x", "g2c"]
    V = {}
    for i, nm in enumerate(names):
        V[nm] = p.sbuf(nm, [128, KC])
        p.op("dve", lambda e: e.tensor_copy(V[nm][:], mv[:, :, i]), reads=[mv], writes=[V[nm]])
    gpx = p.sbuf("gpx", [128, KC]); gpc = p.sbuf("gpc", [128, KC])
    emit_gp(p, V["gf"], V["sc2x"], gpx)
    emit_gp(p, V["gf"], V["sc2c"], gpc)
    if final_g is not None:
        fgd = p.dram("fg", [128, KC], "ExternalInput")
        fg = p.sbuf("fg_s", [128, KC])
        p.dma("sp", fg[:], fgd.ap(), writes=[fg])
    wring = Ring([p.sbuf(f"wb{i}", [128, 2048]) for i in range(2)])
    vring = Ring([p.sbuf(f"vb{i}", [128, 2048]) for i in range(2 * GA)])
    pm = Ring([p.psum(f"pm{i}", [128, 512]) for i in range(2)])
    pact = Ring([p.psum(f"pa{i}", [128, 512]) for i in range(2)])
    pbc = Ring([p.psum(f"pb{i}", [128, 512]) for i in range(2)])
    pso = p.psum("pso", [128, 512])
    pbc_main = Ring(pbc.items + pm.items[:1])
    SaT = p.sbuf("SaT", [128, 8, PASS_N]); SbT = p.sbuf("SbT", [128, 8, PASS_N])
    alias_views = []
    BIGN = max(KC * NT, 3 * KC * PASS_N)
    bigf = p.sbuf("bigf", [128, BIGN])
    big = TT("big", bigf.h[:, 0:KC * NT].rearrange("p (c t) -> p c t", c=KC))
    kbc = p.sbuf("kbc", [128, 8, PASS_N])
    assert 8 * PASS_N >= 2 * NT
    xrows = [TT(f"xrow{i}", kbc.h[:].rearrange("p a b -> p (a b)")[:, i * NT:(i + 1) * NT]) for i in range(2)]
    p.dma("sp", big[:], catd.ap(), writes=[big])
    if pool is not None:
        pwd = p.dram("pw", [4, 2, 128, 256], "ExternalInput")
        ppd = p.dram("pp", [128, 8, 2], "ExternalInput")
        pw = TT("pw_v", SbT.h[:].rearrange("p a b -> p (a b)")[:, 0:2048].rearrange("p (g i j) -> p g i j", g=4, i=2))
        pp = p.sbuf("pp_s", [128, 8, 2]); pbs = p.sbuf("pbs", [128, 8])
        p.dma("sp", pw[:].rearrange("p g i j -> p (g i) j"), pwd.ap().rearrange("g i p j -> p (g i) j"), writes=[pw])
        p.dma("sp", pp[:], ppd.ap(), writes=[pp])
        p.op("dve", lambda e: e.tensor_tensor(pbs[:], pp[:, :, 0], pp[:, :, 1], ALU.mult), reads=[pp], writes=[pbs])
        assert 8 * PASS_N >= 2 * NT
        tmpg = TT("tmpg", SaT.h[:].rearrange("p a b -> p (a b)")[:, 0:2 * NT].rearrange("p (a b) -> p a b", a=2))
        alias_views += [pw, tmpg]
        for g_ in range(4):
            for (off, n, kind) in tiles:
                for jb in range(2):
                    ps = pm.next()
                    for ib in range(2):
                        p.op("pe", lambda e: e.matmul(ps[:, 0:n], pw[:, g_, ib, jb * 128:(jb + 1) * 128], big[:, 8 + 2 * g_ + ib, off:off + n],
                                                      start=(ib == 0), stop=(ib == 1)), reads=[pw, big], writes=[ps], skip_self=True)
                    ob = 2 * g_ + jb
                    p.op("act", lambda e: e.activation(out=tmpg[:, jb, off:off + n], in_=ps[:, 0:n], func=AF.Identity, scale=pp[:, ob, 0:1], bias=pbs[:, ob:ob + 1]),
                         reads=[ps, pp, pbs], writes=[tmpg])
            for jb in range(2):
                p.op("dve", lambda e: e.tensor_copy(big[:, 8 + 2 * g_ + jb, :], tmpg[:, jb, :]), reads=[tmpg], writes=[big])
    xrow = Ring(xrows)
    for m in range(16):
        wb = wring.next()
        p.dma("sp", wb[:], wod.ap()[m], writes=[wb])
        xr = xrow.next()
        p.dma("sp", xr[:], xd.ap()[:, m, :], writes=[xr])
        for (off, n, kind) in tiles:
            ps = pm.next()
            for kc in range(KC):
                p.op("pe", lambda e: e.matmul(ps[:, 0:n], wb[:, kc * 128:(kc + 1) * 128], big[:, kc, off:off + n],
                                              start=(kc == 0), stop=(kc == KC - 1)), reads=[wb, big], writes=[ps], skip_self=True)
            g1 = V["g1x"] if kind == "x" else V["g1c"]
            p.op("dve", lambda e: e.scalar_tensor_tensor(out=xr[:, off:off + n], in0=ps[:, 0:n], scalar=g1[:, m:m + 1],
                                                         in1=xr[:, off:off + n], op0=ALU.mult, op1=ALU.add),
                 reads=[ps, g1, xr.r(off)], writes=[xr.r(off)])
        p.dma("pool", x1d.ap()[:, m, :], xr[:], reads=[xr], writes=[x1d])
    PN = PASS_N
    def _view(nm, i):
        return TT(nm, bigf.h[:, i * KC * PN:(i + 1) * KC * PN].rearrange("p (c t) -> p c t", c=KC))
    xp = _view("xp", 0); h2 = _view("h2", 1); acc = _view("acc", 2)
    scr = {"sq": acc, "ps": p.psum("dn_ps", [128, 512]), "rt": p.sbuf("dn_rt", [128, PN]), "rstd": p.sbuf("dn_rstd", [128, PN])}
    qT = acc
    kT = p.sbuf("kT", [128, 16, 128])
    p.dma("sp", kT[:].rearrange("p a b -> p a b"), kd.ap().rearrange("a p b -> p a b"), writes=[kT])
    Stm = p.sbuf("Stm", [128, 16, 128]); top = p.sbuf("top", [128, 16, 24]); tmpS = p.sbuf("tmpS", [128, 128]); tmpS2 = p.sbuf("tmpS2", [128, 128])
    cand = p.sbuf("cand", [128, 576]); cand2 = p.sbuf("cand2", [128, 576]); c24 = p.sbuf("c24", [128, 24])
    st = {nm: p.sbuf("st_" + nm, [128, 8]) for nm in ["thr", "ncmax", "Z", "kap", "t1"]}
    e16 = p.sbuf("e16", [128, 16]); diag = Ring([p.sbuf(f"diag{i}", [128, 128]) for i in range(2)])
    actS = Ring([p.sbuf(f"actS{i}", [128, PN]) for i in range(2)])
    dS = Ring([p.sbuf(f"dS{i}", [128, PN]) for i in range(4)])
    eS = Ring([p.sbuf(f"eS{i}", [128, PN]) for i in range(4)])
    wS = Ring([p.sbuf(f"wS{i}", [128, PN]) for i in range(2)])
    w2S = Ring([p.sbuf(f"w2S{i}", [128, PN]) for i in range(3)])
    Wacc = Ring([p.sbuf(f"Wacc{i}", [128, PN]) for i in range(2)])
    pso_ring = Ring([pso] + pbc.items + pm.items[:1])
    saD = p.dram("saD", [8, 128, PN], "Internal")
    Gt = Ring([p.sbuf(f"Gt{i}", [128, PN]) for i in range(2 * GA)])
    for (p0, pn) in passes[:_DBG_NP]:
        p.dma("sp", xp[:, :, 0:pn], x1d.ap()[:, :, p0:p0 + pn], reads=[x1d], writes=[xp] + ([big, h2, acc] + xrows + [kbc, SaT, SbT] + alias_views if p0 == 0 else []))
        segs = []
        if p0 < NX:
            segs.append((0, min(pn, NX - p0), "x"))
        if p0 + pn > NX:
            s0 = max(0, NX - p0)
            segs.append((s0, pn - s0, "c"))
        for (s0, sn, kind) in segs:
            emit_norm_mod(p, xp, h2, KC, s0, sn, gpx if kind == "x" else gpc, V["sh2x"] if kind == "x" else V["sh2c"], ones, scr, None, None)
        for hs in range(16):
            wb = wring.next()
            p.dma("sp", wb[:], pqd.ap()[hs], writes=[wb])
            ps = pm.next()
            for kc in range(KC):
                p.op("pe", lambda e: e.matmul(ps[:, 0:pn], wb[:, kc * 128:(kc + 1) * 128], h2[:, kc, 0:pn],
                                              start=(kc == 0), stop=(kc == KC - 1)), reads=[wb, h2], writes=[ps], skip_self=True)
            p.op("dve", lambda e: e.tensor_copy(qT[:, hs, 0:pn], ps[:, 0:pn]), reads=[ps], writes=[qT.r(hs)])
        for hs in range(16):
            ps = pm.next()
            p.op("pe", lambda e: e.matmul(ps[:, 0:pn], kT[:, hs, :], qT[:, hs, 0:pn], start=True, stop=True),
                 reads=[kT, qT.r(hs)], writes=[ps], skip_self=True)
            dst = SaT if hs % 2 == 0 else SbT
            p.op("act", lambda e: e.activation(out=dst[:, hs // 2, 0:pn], in_=ps[:, 0:pn], func=AF.Identity), reads=[ps], writes=[dst.r(hs // 2)])
        t0 = 0
        while t0 < pn:
            nt = min(128, pn - t0)
            for hs in range(16):
                ps = pm.next()
                p.op("pe", lambda e: e.matmul(ps[0:nt, 0:128], qT[:, hs, t0:t0 + nt], kT[:, hs, :], start=True, stop=True),
                     reads=[kT, qT.r(hs)], writes=[ps], skip_self=True)
                p.op("act", lambda e: e.activation(out=Stm[0:nt, hs, :], in_=ps[0:nt, 0:128], func=AF.Identity), reads=[ps], writes=[Stm.r(hs)])
                p.op("dve", lambda e: e.max(top[0:nt, hs, 0:8], Stm[0:nt, hs, :]), reads=[Stm.r(hs)], writes=[top.r(hs)])
                p.op("dve", lambda e: e.match_replace(tmpS[0:nt, :], top[0:nt, hs, 0:8], Stm[0:nt, hs, :], NEG), reads=[Stm.r(hs), top.r(hs)], writes=[tmpS])
                p.op("dve", lambda e: e.max(top[0:nt, hs, 8:16], tmpS[0:nt, :]), reads=[tmpS], writes=[top.r(hs)])
                p.op("dve", lambda e: e.match_replace(tmpS2[0:nt, :], top[0:nt, hs, 8:16], tmpS[0:nt, :], NEG), reads=[tmpS, top.r(hs)], writes=[tmpS2])
                p.op("dve", lambda e: e.max(top[0:nt, hs, 16:24], tmpS2[0:nt, :]), reads=[tmpS2], writes=[top.r(hs)])
            for h in range(8):
                p.op("dve", lambda e: e.tensor_tensor(cand[0:nt, :].rearrange("p (i j) -> p i j", i=24),
                                                      top[0:nt, 2 * h, :].unsqueeze(2).broadcast_to([nt, 24, 24]),
                                                      top[0:nt, 2 * h + 1, :].unsqueeze(1).broadcast_to([nt, 24, 24]), ALU.add),
                     reads=[top.r(2 * h), top.r(2 * h + 1)], writes=[cand])
                p.op("dve", lambda e: e.max(c24[0:nt, 0:8], cand[0:nt, :]), reads=[cand], writes=[c24])
                p.op("dve", lambda e: e.match_replace(cand2[0:nt, :], c24[0:nt, 0:8], cand[0:nt, :], NEG), reads=[cand, c24], writes=[cand2])
                p.op("dve", lambda e: e.max(c24[0:nt, 8:16], cand2[0:nt, :]), reads=[cand2], writes=[c24])
                p.op("dve", lambda e: e.match_replace(cand[0:nt, :], c24[0:nt, 8:16], cand2[0:nt, :], NEG), reads=[cand2, c24], writes=[cand])
                p.op("dve", lambda e: e.max(c24[0:nt, 16:24], cand[0:nt, :]), reads=[cand], writes=[c24])
                p.op("dve", lambda e: e.tensor_tensor(st["thr"][0:nt, h:h + 1], c24[0:nt, 15:16], c24[0:nt, 16:17], ALU.add), reads=[c24], writes=[st["thr"]])
                p.op("dve", lambda e: e.tensor_scalar(out=st["thr"][0:nt, h:h + 1], in0=st["thr"][0:nt, h:h + 1], scalar1=0.5, scalar2=None, op0=ALU.mult), reads=[st["thr"]], writes=[st["thr"]])
                p.op("dve", lambda e: e.tensor_scalar(out=st["ncmax"][0:nt, h:h + 1], in0=c24[0:nt, 0:1], scalar1=-1.0, scalar2=None, op0=ALU.mult), reads=[c24], writes=[st["ncmax"]])
                p.op("act", lambda e: e.activation(out=e16[0:nt, :], in_=c24[0:nt, 0:16], func=AF.Exp, bias=st["ncmax"][0:nt, h:h + 1],
                                                   accum_out=st["Z"][0:nt, h:h + 1]), reads=[c24, st["ncmax"]], writes=[e16, st["Z"]])
                p.op("act", lambda e: e.activation(out=st["t1"][0:nt, h:h + 1], in_=st["thr"][0:nt, h:h + 1], func=AF.Exp, bias=st["ncmax"][0:nt, h:h + 1]),
                     reads=[st["thr"], st["ncmax"]], writes=[st["t1"]])
                p.op("dve", lambda e: e.reciprocal(st["kap"][0:nt, h:h + 1], st["Z"][0:nt, h:h + 1]), reads=[st["Z"]], writes=[st["kap"]])
                p.op("dve", lambda e: e.tensor_tensor(st["kap"][0:nt, h:h + 1], st["kap"][0:nt, h:h + 1], st["t1"][0:nt, h:h + 1], ALU.mult), reads=[st["kap"], st["t1"]], writes=[st["kap"]])
                for which in ("thr", "kap"):
                    dg = diag.next()
                    p.op("dve", lambda e: e.tensor_scalar(out=dg[0:nt, 0:nt], in0=ident[0:nt, 0:nt], scalar1=st[which][0:nt, h:h + 1], scalar2=None, op0=ALU.mult),
                         reads=[ident, st[which]], writes=[dg])
                    ps = pbc.next()
                    p.op("pe", lambda e: e.matmul(ps[:, 0:nt], ones[0:nt, :], dg[0:nt, 0:nt], start=True, stop=True), reads=[ones, dg], writes=[ps], skip_self=True)
                    if which == "thr":
                        p.op("dve", lambda e: e.tensor_tensor(SaT[:, h, t0:t0 + nt], SaT[:, h, t0:t0 + nt], ps[:, 0:nt], ALU.subtract),
                             reads=[ps, SaT.r(h)], writes=[SaT.r(h)])
                    else:
                        p.op("act", lambda e: e.activation(out=kbc[:, h, t0:t0 + nt], in_=ps[:, 0:nt], func=AF.Identity), reads=[ps], writes=[kbc.r(h)])
            t0 += nt
        NA_ = _DBG_NA
        G_ = 8 * NA_
        for h in range(8):
            p.dma("sp", saD.ap()[h, :, 0:pn], SaT[:, h, 0:pn], reads=[SaT.r(h)], writes=[saD.r(h)])
        ustate = {}
        blk = {}
        first_v = [True]
        T_d = {}; T_e = {}; T_w = {}; T_w2 = {}
        evq = []

        def _bc_dma(g):
            a_, h = divmod(g, 8)
            p.dma("sp", SaT[:, h, 0:pn], saD.ap()[h, a_:a_ + 1, 0:pn].broadcast_to([128, pn]), reads=[saD.r(h)], writes=[SaT.r(h)])

        def _u_load(a_):
            ub_ = wring.next()
            p.dma("sp", ub_[:], ud.ap()[a_], writes=[ub_])
            ustate[a_] = (ub_, pact.next())

        def _u_mm(a_, kcs):
            ub_, psa_ = ustate[a_]
            for kc in kcs:
                p.op("pe", lambda e: e.matmul(psa_[:, 0:pn], ub_[:, kc * 128:(kc + 1) * 128], h2[:, kc, 0:pn],
                                              start=(kc == 0), stop=(kc == KC - 1)), reads=[ub_, h2], writes=[psa_], skip_self=True)

        def _A(g):
            h = g % 8
            d_ = dS.next()
            p.op("pool", lambda e: e.tensor_tensor(d_[:, 0:pn], SaT[:, h, 0:pn], SbT[:, h, 0:pn], ALU.add), reads=[SaT.r(h), SbT.r(h)], writes=[d_])
            T_d[g] = d_
            if g + 8 < G_:
                _bc_dma(g + 8)

        def _E(g):
            e_ = eS.next()
            p.op("act", lambda e: e.activation(out=e_[:, 0:pn], in_=T_d[g][:, 0:pn], func=AF.Exp), reads=[T_d[g]], writes=[e_])
            T_e[g] = e_

        def _M(g):
            w_ = wS.next()
            d_ = T_d.pop(g); e_ = T_e.pop(g)
            p.op("dve", lambda e: e.scalar_tensor_tensor(out=w_[:, 0:pn], in0=d_[:, 0:pn], scalar=0.0, in1=e_[:, 0:pn], op0=ALU.is_gt, op1=ALU.mult),
                 reads=[d_, e_], writes=[w_])
            T_w[g] = w_

        def _K(g):
            a_, h = divmod(g, 8)
            w_ = T_w.pop(g)
            if h == 0:
                wa = Wacc.next()
                blk[a_]["wa"] = wa
                p.op("pool", lambda e: e.tensor_tensor(wa[:, 0:pn], w_[:, 0:pn], kbc[:, h, 0:pn], ALU.mult), reads=[w_, kbc.r(h)], writes=[wa])
                T_w2[g] = None
            else:
                w2 = w2S.next()
                p.op("pool", lambda e: e.tensor_tensor(w2[:, 0:pn], w_[:, 0:pn], kbc[:, h, 0:pn], ALU.mult), reads=[w_, kbc.r(h)], writes=[w2])
                T_w2[g] = w2

        def _W(g):
            a_, h = divmod(g, 8)
            w2 = T_w2.pop(g)
            if w2 is not None:
                wa = blk[a_]["wa"]
                p.op("dve", lambda e: e.tensor_tensor(wa[:, 0:pn], wa[:, 0:pn], w2[:, 0:pn], ALU.add), reads=[wa, w2], writes=[wa])

        def _gt(a_):
            b_ = blk[a_]
            gt = Gt.next()
            p.op("dve", lambda e: e.tensor_tensor(gt[:, 0:pn], b_["aS"][:, 0:pn], b_["wa"][:, 0:pn], ALU.mult), reads=[b_["aS"], b_["wa"]], writes=[gt])
            b_["gt"] = gt

        def _v_mm(a_, dbs):
            b_ = blk[a_]
            for db in dbs:
                ps_ = pso_ring.next()
                p.op("pe", lambda e: e.matmul(ps_[:, 0:pn], b_["vb"][:, db * 128:(db + 1) * 128], b_["gt"][:, 0:pn], start=True, stop=True),
                     reads=[b_["vb"], b_["gt"]], writes=[ps_], skip_self=True)
                evq.append((db, ps_))

        def _evac():
            while evq:
                db, ps_ = evq.pop(0)
                if first_v[0]:
                    p.op("dve", lambda e: e.tensor_copy(acc[:, db, 0:pn], ps_[:, 0:pn]), reads=[ps_], writes=[acc.r(db)])
                else:
                    p.op("dve", lambda e: e.tensor_tensor(acc[:, db, 0:pn], acc[:, db, 0:pn], ps_[:, 0:pn], ALU.add), reads=[ps_, acc.r(db)], writes=[acc.r(db)])

        VSCHED = {3: [0, 1, 2], 4: [3, 4, 5], 5: [6, 7, 8], 6: [9, 10, 11], 7: [12, 13, 14, 15]}
        for g in range(8):
            _bc_dma(g)
        _u_load(0)
        _u_mm(0, range(KC))
        for a in range(NA_ + 1):
            if a < NA_:
                vb = vring.next()
                p.dma("sp", vb[:], vd.ap()[a], writes=[vb])
                ub, psa = ustate.pop(a)
                if a + 1 < NA_:
                    _u_load(a + 1)
                aS = actS.next()
                p.op("act", lambda e: e.activation(out=aS[:, 0:pn], in_=psa[:, 0:pn], func=AF.Gelu), reads=[psa], writes=[aS])
                blk[a] = {"aS": aS, "vb": vb}
                if a == 0:
                    for g in range(3):
                        _A(g)
                    for g in range(2):
                        _E(g)
            for h in range(8):
                g = 8 * a + h
                if a + 1 < NA_:
                    _u_mm(a + 1, [2 * h, 2 * h + 1])
                if a >= 1 and h in VSCHED:
                    _evac()
                    _v_mm(a - 1, VSCHED[h])
                if g - 1 >= 0 and g - 1 < G_:
                    _K(g - 1)
                if g + 3 < G_:
                    _A(g + 3)
                if g + 2 < G_:
                    _E(g + 2)
                if g < G_:
                    _M(g)
                if 0 <= g - 2 < G_:
                    _W(g - 2)
                if a >= 1 and h == 2:
                    _gt(a - 1)
            if a >= 1:
                _evac()
                if first_v[0]:
                    first_v[0] = False
                blk.pop(a - 1)
        for (s0, sn, kind) in segs:
            g2 = V["g2x"] if kind == "x" else V["g2c"]
            for db in range(16):
                p.op("dve", lambda e: e.scalar_tensor_tensor(out=xp[:, db, s0:s0 + sn], in0=acc[:, db, s0:s0 + sn], scalar=g2[:, db:db + 1],
                                                             in1=xp[:, db, s0:s0 + sn], op0=ALU.mult, op1=ALU.add),
                     reads=[acc.r(db), g2, xp], writes=[xp])
        if final_g is not None:
            emit_norm_mod(p, xp, h2, KC, 0, pn, fg, None, ones, scr, None, None)
            p.dma("pool", outd.ap()[:, :, p0:p0 + pn], h2[:, :, 0:pn], reads=[h2], writes=[outd])
        else:
            p.dma("pool", outd.ap()[:, :, p0:p0 + pn], xp[:, :, 0:pn], reads=[xp], writes=[outd])
    in_maps = []
    idn = np.eye(128, dtype=np.float32)
    for i in range(NCORES):
        mp = {"catT": catT_cores[i], "xT": xT_cores[i], "mv": mv_cores[i], "wo": woblk, "pq": pqblk, "keysT": keysT,
              "uT": uT, "vt": vt, "ident": idn}
        if final_g is not None:
            mp["fg"] = final_g
        if pool is not None:
            mp["pw"] = pool[0]; mp["pp"] = pool[1]
        in_maps.append(mp)
    print("phase_D: n_inst", p.n_inst, "n_wait", p.n_wait, flush=True)
    res = _run(p, in_maps)
    return [r["o"] for r in res]


TWO_PI = 6.283185307179586
I32 = mybir.dt.int32


def hyena_feats(L):
    t = np.arange(L, dtype=np.float32)
    t_norm = (t / np.float32(max(L - 1, 1))).astype(np.float32)
    bands = np.linspace(1e-4, 15, 16, dtype=np.float32)
    ang = (np.float32(2.0 * np.pi / L) * t[:, None] * bands[None, :]).astype(np.float32)
    feats = np.concatenate([t_norm[:, None], np.cos(ang), -np.sin(ang)], axis=-1).astype(np.float32)
    return feats, t_norm


def phase_C(P3_cores, cw_cores, cb_cores, w3_cores, dl_cores, hb_cores, vg_cores, u_cores, wsT_cores, bs_cores,
            w1, b1, w2, b2, freq, LX, LC):
    LT = LX + LC
    p = Prog()
    P3d = p.dram("P3", [3, 128, LT], "ExternalInput")
    cwd = p.dram("cw", [128, 9], "ExternalInput")
    cbd = p.dram("cb", [128, 3], "ExternalInput")
    w3d = p.dram("w3", [64, 4 * 128], "ExternalInput")
    dld = p.dram("dl", [128, 4], "ExternalInput")
    hbd = p.dram("hb", [128, 2], "ExternalInput")
    vgd = p.dram("vg", [LT // 128, 128, 128], "ExternalInput")
    ud = p.dram("u", [128, LT], "ExternalInput")
    wsd = p.dram("wsT", [128, 128], "ExternalInput")
    bsd = p.dram("bs", [1, 128], "ExternalInput")
    w1d = p.dram("w1", [33, 64], "ExternalInput"); w2d = p.dram("w2", [64, 64], "ExternalInput")
    mlpd = p.dram("mlpv", [64, 3], "ExternalInput")
    seqs = [("x", 0, LX), ("c", LX, LC)]
    fd = {}; tnd = {}
    for nm, o, L in seqs:
        fd[nm] = p.dram("f_" + nm, [33, 2 * L], "ExternalInput")
        tnd[nm] = p.dram("tn_" + nm, [1, 2 * L], "ExternalInput")
    outd = p.dram("o", [2, 128, LT], "ExternalOutput")
    cw = p.sbuf("cw_s", [128, 9]); cb = p.sbuf("cb_s", [128, 3]); w3 = p.sbuf("w3_s", [64, 512]); dl = p.sbuf("dl_s", [128, 4])
    hb = p.sbuf("hb_s", [128, 2]); ws = p.sbuf("ws_s", [128, 128]); bsb = p.sbuf("bs_s", [128, 128])
    w1s = p.sbuf("w1_s", [33, 64]); w2s = p.sbuf("w2_s", [64, 64]); mlpv = p.sbuf("mlpv_s", [64, 3]); fq = p.sbuf("fq_s", [64, 1])
    for t, dd in [(cw, cwd), (cb, cbd), (w3, w3d), (dl, dld), (hb, hbd), (ws, wsd), (w1s, w1d), (w2s, w2d), (mlpv, mlpd)]:
        p.dma("sp", t[:], dd.ap(), writes=[t])
    p.dma("sp", bsb[:], bsd.ap().broadcast_to([128, 128]), writes=[bsb])
    p.op("dve", lambda e: e.tensor_scalar(out=fq[:], in0=mlpv[:, 2:3], scalar1=1.0 / TWO_PI, scalar2=None, op0=ALU.mult), reads=[mlpv], writes=[fq])
    nad = p.sbuf("nad", [128, 4])
    p.op("act", lambda e: e.activation(out=nad[:], in_=dl[:], func=AF.Abs), reads=[dl], writes=[nad])
    p.op("dve", lambda e: e.tensor_scalar(out=nad[:], in0=nad[:], scalar1=-1.0, scalar2=None, op0=ALU.mult), reads=[nad], writes=[nad])
    pg = Ring([p.psum(f"pg{i}", [128, 512]) for i in range(2)])
    vgr = Ring([p.sbuf(f"vgc{i}", [128, 2, 128]) for i in range(2)])
    ur = Ring([p.sbuf(f"uc{i}", [128, 256]) for i in range(2)])
    tr = Ring([p.sbuf(f"tc{i}", [128, 256]) for i in range(2)])
    nch = LT // 128
    for g0 in range(0, nch, 2):
        vgt = vgr.next(); ut = ur.next(); tt = tr.next(); ps = pg.next()
        p.dma("sp", vgt[:], vgd.ap()[g0:g0 + 2].rearrange("a q c -> q a c"), writes=[vgt])
        p.dma("sp", ut[:], ud.ap()[:, g0 * 128:(g0 + 2) * 128], writes=[ut])
        for k in range(2):
            p.op("pe", lambda e: e.matmul(ps[:, k * 128:(k + 1) * 128], vgt[:, k, :], ws[:], start=True, stop=True),
                 reads=[vgt, ws], writes=[ps.r(k)], skip_self=True)
        p.op("dve", lambda e: e.tensor_tensor(tt[:].rearrange("p (a b) -> p a b", a=2), ps[:, 0:256].rearrange("p (a b) -> p a b", a=2),
                                              bsb[:].unsqueeze(1).broadcast_to([128, 2, 128]), ALU.add), reads=[ps, bsb], writes=[tt])
        p.op("pool", lambda e: e.tensor_tensor(tt[:], tt[:], ut[:], ALU.mult), reads=[tt, ut], writes=[tt])
        p.dma("pool", outd.ap()[1, :, g0 * 128:(g0 + 2) * 128], tt[:], reads=[tt], writes=[outd.r(("yb", g0))])
    NFFT = 2 * LX
    assert NFFT == 128 * 128
    CG = 16
    zb = p.sbuf("zb", [128, LT]); yb_ = p.sbuf("ybuf", [128, LT]); tmp = yb_; xg = p.sbuf("xg", [128, LT])
    kkc = p.sbuf("kk_c", [128, 2 * LC])
    gD = p.dram("gD", [128, NFFT], "Internal"); uD = p.dram("uD", [128, LX], "Internal"); yD = p.dram("yD", [128, LX], "Internal")
    tabd = p.dram("ffttab", [128, 10, 128], "ExternalInput")
    tab = p.sbuf("tab_s", [128, 10, 128])
    p.dma("sp", tab[:], tabd.ap(), writes=[tab])
    Cm = tab[:, 0, :]; Sm = tab[:, 1, :]; nSm = tab[:, 2, :]; Tc = tab[:, 3, :]; Ts = tab[:, 4, :]
    F1cat = tab[:, 5:7, :].rearrange("p a b -> p (a b)")
    Fi1 = tab[:, 0:2, :].rearrange("p a b -> p (a b)")
    Fi2 = tab[:, 7:9, :].rearrange("p a b -> p (a b)")
    pmm = Ring([p.psum(f"pc{i}", [128, 512]) for i in range(2)])
    pf = Ring([p.psum(f"pf{i}", [128, 512]) for i in range(4)])
    ft = Ring([p.sbuf(f"ft{i}", [33, 512]) for i in range(2)])
    tnb = Ring([p.sbuf(f"tnb{i}", [128, 512]) for i in range(2)])
    hA = Ring([p.sbuf(f"hA{i}", [64, 512]) for i in range(2)])
    hI = p.sbuf("hI", [64, 512], I32); hF = p.sbuf("hF", [64, 512])
    dec = Ring([p.sbuf(f"dec{i}", [128, 512]) for i in range(2)])
    kt = Ring([p.sbuf(f"kt{i}", [128, 512]) for i in range(2)])
    absd = p.sbuf("absd", [128, 512]); part = p.sbuf("part", [128, 64]); den = p.sbuf("den", [128, 1]); rden = p.sbuf("rden", [128, 1])
    Tu = p.sbuf("Tu", [64, CG, 128]); Tg = p.sbuf("Tg", [128, CG, 128]); Ty = p.sbuf("Ty", [64, CG, 128])
    Btr = Ring([p.sbuf(f"Bt{i}", [128, 2, 4, 128]) for i in range(2)])
    Gs = p.sbuf("Gs", [128, 2, 512]); Yt = p.sbuf("Yt", [128, 2, 4, 128]); Et = p.sbuf("Et", [128, 2, 4, 128])
    tq = Ring([p.sbuf(f"tq{i}", [128, 512]) for i in range(6)])

    def sin_layer(ps, n, bcol, out):
        p.op("dve", lambda e: e.tensor_scalar(out=out[:, 0:n], in0=ps[0:64, 0:n], scalar1=mlpv[:, bcol:bcol + 1], scalar2=fq[:, 0:1], op0=ALU.add, op1=ALU.mult),
             reads=[ps, mlpv, fq], writes=[out])
        p.op("dve", lambda e: e.tensor_copy(hI[:, 0:n], out[:, 0:n]), reads=[out], writes=[hI])
        p.op("dve", lambda e: e.tensor_copy(hF[:, 0:n], hI[:, 0:n]), reads=[hI], writes=[hF])
        p.op("dve", lambda e: e.tensor_tensor(out[:, 0:n], out[:, 0:n], hF[:, 0:n], ALU.subtract), reads=[out, hF], writes=[out])
        p.op("act", lambda e: e.activation(out=out[:, 0:n], in_=out[:, 0:n], func=AF.Sin, scale=TWO_PI * (1 - 2e-7)), reads=[out], writes=[out])

    def conv_into(dst, j):
        p.dma("sp", tmp[:], P3d.ap()[j], writes=[tmp])
        for nm, o, L in seqs:
            p.op("dve", lambda e: e.tensor_scalar(out=dst[:, o:o + L], in0=tmp[:, o:o + L], scalar1=cw[:, 3 * j + 1:3 * j + 2], scalar2=cb[:, j:j + 1], op0=ALU.mult, op1=ALU.add),
                 reads=[tmp, cw, cb], writes=[dst])
            p.op("dve", lambda e: e.scalar_tensor_tensor(out=dst[:, o + 1:o + L], in0=tmp[:, o:o + L - 1], scalar=cw[:, 3 * j:3 * j + 1], in1=dst[:, o + 1:o + L], op0=ALU.mult, op1=ALU.add),
                 reads=[tmp, cw, dst], writes=[dst])
            p.op("dve", lambda e: e.scalar_tensor_tensor(out=dst[:, o:o + L - 1], in0=tmp[:, o + 1:o + L], scalar=cw[:, 3 * j + 2:3 * j + 3], in1=dst[:, o:o + L - 1], op0=ALU.mult, op1=ALU.add),
                 reads=[tmp, cw, dst], writes=[dst])

    def cmul_tw(ps, Bt, j0, conj):
        v = ps[:, 0:512].rearrange("p (c r k) -> p c r k", c=2, r=2)
        Ar = v[:, :, 0, :]; Ai = v[:, :, 1, :]
        Tcb = Tc.unsqueeze(1).broadcast_to([128, 2, 128]); Tsb = Ts.unsqueeze(1).broadcast_to([128, 2, 128])
        t = [tq.next() for _ in range(4)]
        tv = [x[:, 0:256].rearrange("p (c k) -> p c k", c=2) for x in t]
        p.op("dve", lambda e: e.tensor_tensor(tv[0], Ar, Tcb, ALU.mult), reads=[ps, tab], writes=[t[0]])
        p.op("dve", lambda e: e.tensor_tensor(tv[1], Ai, Tsb, ALU.mult), reads=[ps, tab], writes=[t[1]])
        p.op("dve", lambda e: e.tensor_tensor(tv[2], Ai, Tcb, ALU.mult), reads=[ps, tab], writes=[t[2]])
        p.op("dve", lambda e: e.tensor_tensor(tv[3], Ar, Tsb, ALU.mult), reads=[ps, tab], writes=[t[3]])
        p.op("pool", lambda e: e.tensor_tensor(Bt[:, 0, j0:j0 + 2, :], tv[0], tv[1], ALU.subtract if conj else ALU.add), reads=[t[0], t[1]], writes=[Bt.r((0, j0))])
        p.op("pool", lambda e: e.tensor_tensor(Bt[:, 1, j0:j0 + 2, :], tv[2], tv[3], ALU.add if conj else ALU.subtract), reads=[t[2], t[3]], writes=[Bt.r((1, j0))])

    def fwd_fft4(T, K, c0):
        Bt = Btr.next()
        for pr_ in range(2):
            ps = pf.next()
            for j in range(2):
                c = c0 + 2 * pr_ + j
                p.op("pe", lambda e: e.matmul(ps[:, j * 256:(j + 1) * 256], T[0:K, c, :], F1cat[0:K, :], start=True, stop=True),
                     reads=[T, tab], writes=[ps.r(j)], skip_self=True)
            cmul_tw(ps, Bt, 2 * pr_, False)
        Br = Bt[:, 0, :, :].rearrange("p c k -> p (c k)"); Bi = Bt[:, 1, :, :].rearrange("p c k -> p (c k)")
        pxr = pf.next(); pxi = pf.next()
        p.op("pe", lambda e: e.matmul(pxr[:, 0:512], Cm, Br, start=True, stop=False), reads=[tab, Bt], writes=[pxr], skip_self=True)
        p.op("pe", lambda e: e.matmul(pxr[:, 0:512], Sm, Bi, start=False, stop=True), reads=[tab, Bt], writes=[pxr], skip_self=True)
        p.op("pe", lambda e: e.matmul(pxi[:, 0:512], Cm, Bi, start=True, stop=False), reads=[tab, Bt], writes=[pxi], skip_self=True)
        p.op("pe", lambda e: e.matmul(pxi[:, 0:512], nSm, Br, start=False, stop=True), reads=[tab, Bt], writes=[pxi], skip_self=True)
        return pxr, pxi

    def fft_conv():
        for c0g in range(0, 128, CG):
            p.dma("sp", Tu[:], uD.ap()[c0g:c0g + CG, :].rearrange("c (a b) -> a c b", b=128), reads=[uD], writes=[Tu])
            p.dma("sp", Tg[:], gD.ap()[c0g:c0g + CG, :].rearrange("c (a b) -> a c b", b=128), reads=[gD], writes=[Tg])
            for c0 in range(0, CG, 4):
                gr, gi = fwd_fft4(Tg, 128, c0)
                p.op("act", lambda e: e.activation(out=Gs[:, 0, :], in_=gr[:, 0:512], func=AF.Identity), reads=[gr], writes=[Gs.r(0)])
                p.op("act", lambda e: e.activation(out=Gs[:, 1, :], in_=gi[:, 0:512], func=AF.Identity), reads=[gi], writes=[Gs.r(1)])
                ur, ui = fwd_fft4(Tu, 64, c0)
                t = [tq.next() for _ in range(4)]
                p.op("dve", lambda e: e.tensor_tensor(t[0][:], ur[:, 0:512], Gs[:, 0, :], ALU.mult), reads=[ur, Gs.r(0)], writes=[t[0]])
                p.op("dve", lambda e: e.tensor_tensor(t[1][:], ui[:, 0:512], Gs[:, 1, :], ALU.mult), reads=[ui, Gs.r(1)], writes=[t[1]])
                p.op("dve", lambda e: e.tensor_tensor(t[2][:], ur[:, 0:512], Gs[:, 1, :], ALU.mult), reads=[ur, Gs.r(1)], writes=[t[2]])
                p.op("dve", lambda e: e.tensor_tensor(t[3][:], ui[:, 0:512], Gs[:, 0, :], ALU.mult), reads=[ui, Gs.r(0)], writes=[t[3]])
                p.op("pool", lambda e: e.tensor_tensor(Yt[:, 0, :, :].rearrange("p c k -> p (c k)"), t[0][:], t[1][:], ALU.subtract), reads=[t[0], t[1]], writes=[Yt.r(0)])
                p.op("pool", lambda e: e.tensor_tensor(Yt[:, 1, :, :].rearrange("p c k -> p (c k)"), t[2][:], t[3][:], ALU.add), reads=[t[2], t[3]], writes=[Yt.r(1)])
                for pr_ in range(2):
                    ps = pf.next()
                    for j in range(2):
                        c = 2 * pr_ + j
                        p.op("pe", lambda e: e.matmul(ps[:, j * 256:(j + 1) * 256], Yt[:, 0, c, :], Fi1, start=True, stop=False), reads=[Yt.r(0), tab], writes=[ps.r(j)], skip_self=True)
                        p.op("pe", lambda e: e.matmul(ps[:, j * 256:(j + 1) * 256], Yt[:, 1, c, :], Fi2, start=False, stop=True), reads=[Yt.r(1), tab], writes=[ps.r(j)], skip_self=True)
                    cmul_tw(ps, Et, 2 * pr_, True)
                Er = Et[:, 0, :, :].rearrange("p c k -> p (c k)"); Ei = Et[:, 1, :, :].rearrange("p c k -> p (c k)")
                py = pf.next()
                p.op("pe", lambda e: e.matmul(py[0:64, 0:512], tab[:, 0, 0:64], Er, start=True, stop=False), reads=[tab, Et], writes=[py], skip_self=True)
                p.op("pe", lambda e: e.matmul(py[0:64, 0:512], tab[:, 2, 0:64], Ei, start=False, stop=True), reads=[tab, Et], writes=[py], skip_self=True)
                p.op("act", lambda e: e.activation(out=Ty[:, c0:c0 + 4, :].rearrange("p c k -> p (c k)"), in_=py[0:64, 0:512], func=AF.Identity), reads=[py], writes=[Ty.r(c0)])
            p.dma("pool", yD.ap()[c0g:c0g + CG, :].rearrange("c (a b) -> a c b", b=128), Ty[:], reads=[Ty], writes=[yD.r(c0g)])

    conv_into(zb, 0)
    for n in range(2):
        conv_into(xg, n + 1)
        for nm, o, L in seqs:
            use_fft = (nm == "x")
            TS = min(512, L)
            ntile = 2 * L // TS
            for ti in range(ntile):
                c0 = ti * TS
                if use_fft:
                    dirn = 0 if c0 < L else 1
                else:
                    dirn = 1 if c0 < L else 0
                f = ft.next(); tb = tnb.next()
                p.dma("sp", f[:, 0:TS], fd[nm].ap()[:, c0:c0 + TS], writes=[f])
                p.dma("sp", tb[:, 0:TS], tnd[nm].ap()[:, c0:c0 + TS].broadcast_to([128, TS]), writes=[tb])
                ps = pmm.next()
                p.op("pe", lambda e: e.matmul(ps[0:64, 0:TS], w1s[:], f[:, 0:TS], start=True, stop=True), reads=[w1s, f], writes=[ps], skip_self=True)
                h1 = hA.next()
                sin_layer(ps, TS, 0, h1)
                ps = pmm.next()
                p.op("pe", lambda e: e.matmul(ps[0:64, 0:TS], w2s[:], h1[:, 0:TS], start=True, stop=True), reads=[w2s, h1], writes=[ps], skip_self=True)
                h2 = hA.next()
                sin_layer(ps, TS, 1, h2)
                ps = pmm.next()
                col = (dirn * 2 + n) * 128
                p.op("pe", lambda e: e.matmul(ps[:, 0:TS], w3[:, col:col + 128], h2[:, 0:TS], start=True, stop=True), reads=[w3, h2], writes=[ps], skip_self=True)
                dc = dec.next()
                p.op("act", lambda e: e.activation(out=dc[:, 0:TS], in_=tb[:, 0:TS], func=AF.Exp, scale=nad[:, dirn * 2 + n:dirn * 2 + n + 1]), reads=[tb, nad], writes=[dc])
                if use_fft:
                    k_ = kt.next()
                    p.op("dve", lambda e: e.tensor_tensor(k_[:, 0:TS], ps[:, 0:TS], dc[:, 0:TS], ALU.mult), reads=[ps, dc], writes=[k_])
                    p.op("act", lambda e: e.activation(out=absd[:, 0:TS], in_=k_[:, 0:TS], func=AF.Abs, accum_out=part[:, ti:ti + 1]), reads=[k_], writes=[absd, part.r(ti)])
                    p.dma("pool", gD.ap()[:, c0:c0 + TS], k_[:, 0:TS], reads=[k_], writes=[gD.r(ti)])
                else:
                    p.op("dve", lambda e: e.tensor_tensor(kkc[:, c0:c0 + TS], ps[:, 0:TS], dc[:, 0:TS], ALU.mult), reads=[ps, dc], writes=[kkc.r(ti)])
                    p.op("act", lambda e: e.activation(out=absd[:, 0:TS], in_=kkc[:, c0:c0 + TS], func=AF.Abs, accum_out=part[:, ti:ti + 1]), reads=[kkc.r(ti)], writes=[absd, part.r(ti)])
            p.op("dve", lambda e: e.reduce_sum(den[:], part[:, 0:ntile], AX.X), reads=[part], writes=[den])
            p.op("dve", lambda e: e.reciprocal(rden[:], den[:]), reads=[den], writes=[rden])
            Y = yb_
            if use_fft:
                p.op("dve", lambda e: e.tensor_scalar(out=rden[:], in0=rden[:], scalar1=1.0 / NFFT, scalar2=None, op0=ALU.mult), reads=[rden], writes=[rden])
                p.dma("sp", uD.ap(), zb[:, o:o + L], reads=[zb], writes=[uD])
                fft_conv()
                p.dma("sp", Y[:, o:o + L], yD.ap(), reads=[yD], writes=[Y.r(nm)])
            else:
                p.op("dve", lambda e: e.tensor_scalar(out=Y[:, o:o + L], in0=kkc[:, L:2 * L], scalar1=zb[:, o:o + 1], scalar2=None, op0=ALU.mult),
                     reads=[kkc, zb], writes=[Y.r(nm)])
                for s_ in range(1, L):
                    p.op("dve", lambda e: e.scalar_tensor_tensor(out=Y[:, o:o + L], in0=kkc[:, L - s_:2 * L - s_], scalar=zb[:, o + s_:o + s_ + 1], in1=Y[:, o:o + L], op0=ALU.mult, op1=ALU.add),
                         reads=[kkc, zb, Y.r(nm)], writes=[Y.r(nm)])
            p.op("dve", lambda e: e.tensor_scalar(out=Y[:, o:o + L], in0=Y[:, o:o + L], scalar1=rden[:, 0:1], scalar2=None, op0=ALU.mult), reads=[Y.r(nm), rden], writes=[Y.r(nm)])
            p.op("dve", lambda e: e.scalar_tensor_tensor(out=Y[:, o:o + L], in0=zb[:, o:o + L], scalar=hb[:, n:n + 1], in1=Y[:, o:o + L], op0=ALU.mult, op1=ALU.add),
                 reads=[zb, hb, Y.r(nm)], writes=[Y.r(nm)])
            p.op("dve", lambda e: e.tensor_tensor(zb[:, o:o + L], xg[:, o:o + L], Y[:, o:o + L], ALU.mult), reads=[xg, Y.r(nm)], writes=[zb])
    p.dma("pool", outd.ap()[0], zb[:], reads=[zb], writes=[outd.r("z")])
    consts = {}
    for nm, o, L in seqs:
        feats, tn = hyena_feats(L)
        if nm == "x":
            pos = np.concatenate([np.arange(L), np.arange(L - 1, -1, -1)])
        else:
            pos = np.concatenate([np.arange(L - 1, -1, -1), np.arange(L)])
        consts["f_" + nm] = np.ascontiguousarray(feats[pos].T)
        consts["tn_" + nm] = np.ascontiguousarray(tn[pos][None, :])
    mlpvh = np.ascontiguousarray(np.stack([b1, b2, freq], axis=1).astype(np.float32))
    jk = np.outer(np.arange(128), np.arange(128)).astype(np.float64)
    Cn = np.cos(2 * np.pi * jk / 128); Sn = np.sin(2 * np.pi * jk / 128)
    Tcn = np.cos(2 * np.pi * jk / (128 * 128)); Tsn = np.sin(2 * np.pi * jk / (128 * 128))
    consts["ffttab"] = np.ascontiguousarray(np.stack([Cn, Sn, -Sn, Tcn, Tsn, Cn, -Sn, -Sn, Cn, Cn], axis=1).astype(np.float32))
    in_maps = []
    for i in range(NCORES):
        mp = {"P3": P3_cores[i], "cw": cw_cores[i], "cb": cb_cores[i], "w3": w3_cores[i], "dl": dl_cores[i], "hb": hb_cores[i],
              "vg": vg_cores[i], "u": u_cores[i], "wsT": wsT_cores[i], "bs": bs_cores[i], "w1": w1, "w2": w2, "mlpv": mlpvh}
        mp.update(consts)
        in_maps.append(mp)
    print("phase_C: n_inst", p.n_inst, "n_wait", p.n_wait, flush=True)
    res = _run(p, in_maps)
    return [r["o"] for r in res]


def phase_F(xr_cores, gate_cores, xp_cores, cw_cores, wa_cores, wx_cores, vec_cores, selw_cores, btab_cores, LX, LC):
    LT = LX + LC
    p = Prog()
    xrd = p.dram("xr", [128, LT], "ExternalInput"); gd = p.dram("gate", [128, LX], "ExternalInput"); xpd = p.dram("xp", [128, LX], "ExternalInput")
    cwd = p.dram("cw", [128, 5], "ExternalInput")
    wad = p.dram("wa", [2, 128, 128], "ExternalInput"); wxd = p.dram("wx", [2, 128, 128], "ExternalInput")
    vd = p.dram("vec", [128, 6], "ExternalInput")
    sd = p.dram("selw", [128, 4], "ExternalInput"); bd = p.dram("btab", [128, 4, 16], "ExternalInput")
    outd = p.dram("o", [2, 128, LX], "ExternalOutput")
    cw = p.sbuf("cw_s", [128, 5]); wa = p.sbuf("wa_s", [128, 2, 128]); wx = p.sbuf("wx_s", [128, 2, 128]); vec = p.sbuf("vec_s", [128, 6])
    selw = p.sbuf("selw_s", [128, 4]); btab = p.sbuf("btab_s", [128, 4, 16])
    for t, dd in [(cw, cwd), (vec, vd), (selw, sd), (btab, bd)]:
        p.dma("sp", t[:], dd.ap(), writes=[t])
    p.dma("sp", wa[:], wad.ap().rearrange("d i j -> i d j"), writes=[wa])
    p.dma("sp", wx[:], wxd.ap().rearrange("d i j -> i d j"), writes=[wx])
    ones = emit_consts(p)
    raw = p.sbuf("raw", [128, LT]); xr = p.sbuf("xrs", [128, LT])
    A = p.sbuf("A", [128, LX]); Bv = p.sbuf("Bv", [128, LX]); Hf = p.sbuf("Hf", [128, LX]); Hb = p.sbuf("Hb", [128, LX])
    Hc = p.sbuf("Hc", [128, 2, LC])
    seqs = [("c", LX, LC), ("x", 0, LX)]
    p.dma("sp", raw[:], xrd.ap(), writes=[raw])
    for nm, o, L in seqs:
        p.op("dve", lambda e: e.tensor_scalar(out=xr[:, o:o + L], in0=raw[:, o:o + L], scalar1=cw[:, 1:2], scalar2=cw[:, 4:5], op0=ALU.mult, op1=ALU.add), reads=[raw, cw], writes=[xr])
        p.op("dve", lambda e: e.scalar_tensor_tensor(out=xr[:, o + 1:o + L], in0=raw[:, o:o + L - 1], scalar=cw[:, 0:1], in1=xr[:, o + 1:o + L], op0=ALU.mult, op1=ALU.add), reads=[raw, cw, xr], writes=[xr])
        p.op("dve", lambda e: e.scalar_tensor_tensor(out=xr[:, o:o + L - 1], in0=raw[:, o + 1:o + L], scalar=cw[:, 2:3], in1=xr[:, o:o + L - 1], op0=ALU.mult, op1=ALU.add), reads=[raw, cw, xr], writes=[xr])
        p.op("dve", lambda e: e.scalar_tensor_tensor(out=xr[:, o:o + L - 2], in0=raw[:, o + 2:o + L], scalar=cw[:, 3:4], in1=xr[:, o:o + L - 2], op0=ALU.mult, op1=ALU.add), reads=[raw, cw, xr], writes=[xr])
    m8 = p.sbuf("m8", [128, 2])
    p.op("act", lambda e: e.activation(out=m8[:], in_=vec[:, 4:6], func=AF.Exp, scale=-1.0), reads=[vec], writes=[m8])
    p.op("act", lambda e: e.activation(out=m8[:], in_=m8[:], func=AF.Ln, bias=1.0), reads=[m8], writes=[m8])
    p.op("dve", lambda e: e.tensor_scalar(out=m8[:], in0=m8[:], scalar1=-8.0, scalar2=None, op0=ALU.mult), reads=[m8], writes=[m8])
    pr = Ring([p.psum(f"pr{i}", [128, 512]) for i in range(4)])
    rT = Ring([p.sbuf(f"rT{i}", [128, 512]) for i in range(1)]); iT = Ring([p.sbuf(f"iT{i}", [128, 512]) for i in range(1)])
    a2 = Ring([p.sbuf(f"a2{i}", [128, 512]) for i in range(1)])
    for d_ in range(2):
        H = Hf if d_ == 0 else Hb
        for nm, o, L in seqs:
            for c0 in range(0, L, 512):
                n = min(512, L - c0)
                ps1 = pr.next(); ps2 = pr.next()
                p.op("pe", lambda e: e.matmul(ps1[:, 0:n], wa[:, d_, :], xr[:, o + c0:o + c0 + n], start=True, stop=True), reads=[wa, xr], writes=[ps1], skip_self=True)
                p.op("pe", lambda e: e.matmul(ps2[:, 0:n], wx[:, d_, :], xr[:, o + c0:o + c0 + n], start=True, stop=True), reads=[wx, xr], writes=[ps2], skip_self=True)
                r_ = rT.next(); i_ = iT.next(); q_ = a2.next()
                p.op("act", lambda e: e.activation(out=r_[:, 0:n], in_=ps1[:, 0:n], func=AF.Sigmoid, bias=vec[:, d_:d_ + 1]), reads=[ps1, vec], writes=[r_])
                p.op("act", lambda e: e.activation(out=i_[:, 0:n], in_=ps2[:, 0:n], func=AF.Sigmoid, bias=vec[:, 2 + d_:3 + d_]), reads=[ps2, vec], writes=[i_])
                p.op("act", lambda e: e.activation(out=A[:, c0:c0 + n], in_=r_[:, 0:n], func=AF.Exp, scale=m8[:, d_:d_ + 1]), reads=[r_, m8], writes=[A.r(c0)])
                p.op("dve", lambda e: e.tensor_tensor(q_[:, 0:n], A[:, c0:c0 + n], A[:, c0:c0 + n], ALU.mult), reads=[A.r(c0)], writes=[q_])
                p.op("dve", lambda e: e.tensor_scalar(out=q_[:, 0:n], in0=q_[:, 0:n], scalar1=-1.0, scalar2=1.0, op0=ALU.mult, op1=ALU.add), reads=[q_], writes=[q_])
                p.op("act", lambda e: e.activation(out=q_[:, 0:n], in_=q_[:, 0:n], func=AF.Sqrt), reads=[q_], writes=[q_])
                p.op("dve", lambda e: e.tensor_tensor(i_[:, 0:n], i_[:, 0:n], xr[:, o + c0:o + c0 + n], ALU.mult), reads=[i_, xr], writes=[i_])
                p.op("dve", lambda e: e.tensor_tensor(Bv[:, c0:c0 + n], i_[:, 0:n], q_[:, 0:n], ALU.mult), reads=[i_, q_], writes=[Bv.r(c0)])
            if nm == "c":
                dst = Hc[:, d_, :]; dreg = Hc
                init = 0.0
            else:
                dst = H[:, 0:L]; dreg = H
                init = Hc[:, 0, LC - 1:LC] if d_ == 0 else Hc[:, 1, 0:1]
            if d_ == 0:
                p.op("dve", lambda e: e.tensor_tensor_scan(dst, A[:, 0:L], Bv[:, 0:L], init, ALU.mult, ALU.add), reads=[A, Bv, Hc], writes=[dreg])
            else:
                p.op("dve", lambda e: e.tensor_tensor_scan(dst[:, ::-1], A[:, 0:L][:, ::-1], Bv[:, 0:L][:, ::-1], init, ALU.mult, ALU.add), reads=[A, Bv, Hc], writes=[dreg])
    p.dma("sp", A[:], gd.ap(), writes=[A])
    p.op("dve", lambda e: e.tensor_tensor(Hf[:], Hf[:], Hb[:], ALU.add), reads=[Hf, Hb], writes=[Hf])
    p.op("dve", lambda e: e.tensor_tensor(Hf[:], Hf[:], A[:], ALU.mult), reads=[Hf, A], writes=[Hf])
    p.dma("pool", outd.ap()[0], Hf[:], reads=[Hf], writes=[outd.r(0)])
    L = LX
    xpb = Bv; diff = Hb; accp = A
    csp = raw
    p.dma("sp", xpb[:], xpd.ap(), writes=[xpb])
    p.op("dve", lambda e: e.memset(csp[:, 0:9], 0.0), writes=[csp])
    p.op("dve", lambda e: e.tensor_tensor_scan(csp[:, 9:9 + L], ones[:, 0:1].broadcast_to([128, L]), xpb[:], 0.0, ALU.mult, ALU.add), reads=[ones, xpb, csp], writes=[csp])
    p.op("dve", lambda e: e.tensor_copy(csp[:, 9 + L:17 + L], csp[:, 8 + L:9 + L].broadcast_to([128, 8])), reads=[csp], writes=[csp])
    bacc = p.sbuf("bacc", [128, 16]); bt = p.sbuf("bt", [128, 16])
    for wi, w in enumerate((2, 4, 8, 16)):
        hf_ = w // 2
        p.op("dve", lambda e: e.tensor_tensor(diff[:], csp[:, 8 + hf_:8 + hf_ + L], csp[:, 8 - hf_:8 - hf_ + L], ALU.subtract), reads=[csp], writes=[diff])
        if wi == 0:
            p.op("dve", lambda e: e.tensor_scalar(out=accp[:], in0=diff[:], scalar1=selw[:, wi:wi + 1], scalar2=None, op0=ALU.mult), reads=[diff, selw], writes=[accp])
        else:
            p.op("dve", lambda e: e.scalar_tensor_tensor(out=accp[:], in0=diff[:], scalar=selw[:, wi:wi + 1], in1=accp[:], op0=ALU.mult, op1=ALU.add), reads=[diff, selw, accp], writes=[accp])
        for half, cols in ((0, slice(0, 8)), (1, slice(L - 8, L))):
            bs_ = slice(8 * half, 8 * half + 8)
            if wi == 0:
                p.op("dve", lambda e: e.tensor_tensor(bacc[:, bs_], diff[:, cols], btab[:, wi, bs_], ALU.mult), reads=[diff, btab], writes=[bacc])
            else:
                p.op("dve", lambda e: e.tensor_tensor(bt[:, bs_], diff[:, cols], btab[:, wi, bs_], ALU.mult), reads=[diff, btab], writes=[bt])
                p.op("dve", lambda e: e.tensor_tensor(bacc[:, bs_], bacc[:, bs_], bt[:, bs_], ALU.add), reads=[bacc, bt], writes=[bacc])
    p.op("dve", lambda e: e.tensor_copy(accp[:, 0:8], bacc[:, 0:8]), reads=[bacc], writes=[accp])
    p.op("dve", lambda e: e.tensor_copy(accp[:, L - 8:L], bacc[:, 8:16]), reads=[bacc], writes=[accp])
    p.op("dve", lambda e: e.tensor_tensor(accp[:], accp[:], xpb[:], ALU.subtract), reads=[accp, xpb], writes=[accp])
    p.dma("pool", outd.ap()[1], accp[:], reads=[accp], writes=[outd.r(1)])
    in_maps = []
    for i in range(NCORES):
        in_maps.append({"xr": xr_cores[i], "gate": gate_cores[i], "xp": xp_cores[i], "cw": cw_cores[i], "wa": wa_cores[i], "wx": wx_cores[i],
                        "vec": vec_cores[i], "selw": selw_cores[i], "btab": btab_cores[i]})
    res = _run(p, in_maps)
    return [r["o"] for r in res]


def _assemble(projs, NX, NCX):
    px = np.concatenate([o[:, :, :NX] for o in projs], axis=2)
    if NCX:
        pc = np.concatenate([o[:, :, NX:] for o in projs], axis=2)
        return np.concatenate([px, pc], axis=2)
    return px


def _token_shard(full, NX, NCX, LX):
    outs = []
    for i in range(NCORES):
        parts = [full[:, :, i * NX:(i + 1) * NX]]
        if NCX:
            parts.append(full[:, :, LX + i * NCX:LX + (i + 1) * NCX])
        outs.append(np.ascontiguousarray(np.concatenate(parts, axis=2).transpose(1, 0, 2)))
    return outs


def kernel(x, c, ctx, c_ctx, ada_w, ada_b, norm_mix, norm_ffn, norm_final,
           ev_w_in, ev_conv_w, ev_conv_b, hy_w1, hy_b1, hy_w2, hy_b2, hy_w3, hy_freq, hy_deltas, hy_bias,
           gm_norm, gm_ws, gm_bs, ev_w_out,
           od_w_in, od_conv_w, od_conv_b, lru_wa, lru_ba, lru_wx, lru_bx, lru_lam,
           pool_w, pool_b, pool_scale, od_w_out,
           peer_q, peer_keys, peer_u, peer_v):
    f32 = lambda a: np.ascontiguousarray(np.asarray(a, dtype=np.float32))
    x = f32(x)[0]; ctxa = f32(ctx)[0]
    LX, LC, NX, NCX = 8192, 256, 1024, 32
    mod = phase_A(f32(c), f32(c_ctx), f32(ada_w), f32(ada_b))

    def mvv(l, which, col):
        return vec_fm(mod[l][which * 2048:(which + 1) * 2048, col])

    def mvB(l):
        return np.ascontiguousarray(np.stack([vec_fm(f32(norm_mix)[l]), mvv(l, 0, 0), mvv(l, 1, 0), mvv(l, 0, 1), mvv(l, 1, 1)], axis=2))

    def mvD(l):
        return np.ascontiguousarray(np.stack([mvv(l, 2, 0), mvv(l, 2, 1), vec_fm(f32(norm_ffn)[l]), mvv(l, 3, 0), mvv(l, 4, 0),
                                              mvv(l, 3, 1), mvv(l, 4, 1), mvv(l, 5, 0), mvv(l, 5, 1)], axis=2))

    def peer_args(l):
        keysT = np.ascontiguousarray(f32(peer_keys)[l].reshape(16, 128, 128).transpose(0, 2, 1))
        uT = np.ascontiguousarray(f32(peer_u)[l].reshape(128, 128, 16, 128).transpose(0, 3, 2, 1).reshape(128, 128, 2048))
        vt = np.ascontiguousarray(f32(peer_v)[l].reshape(128, 128, 2048))
        return wblocks(f32(peer_q)[l]), keysT, uT, vt

    xTs = [fm(np.concatenate([x[i * NX:(i + 1) * NX], ctxa[i * NCX:(i + 1) * NCX]], 0)) for i in range(NCORES)]
    epi0 = ["copy"] * 24 + ["gelu"] * 8 + ["vg"] * 8
    projs = phase_B(xTs, [mvB(0)] * NCORES, wblocks(f32(ev_w_in)[0]), epi0, NX, NCX, gmn=vec_fm(f32(gm_norm)[0]))
    PT = _assemble(projs, NX, NCX)
    cwf = f32(ev_conv_w)[0]; cbf = f32(ev_conv_b)[0]; w3f = f32(hy_w3)[0]; dlf = f32(hy_deltas)[0]; hbf = f32(hy_bias)[0]
    P3c, cwc, cbc, w3c, dlc, hbc, vgc, uc, wsc, bsc = [], [], [], [], [], [], [], [], [], []
    for i in range(NCORES):
        blk = slice(128 * i, 128 * i + 128)
        P3c.append(np.ascontiguousarray(PT[[i, 8 + i, 16 + i]]))
        cwc.append(np.ascontiguousarray(np.stack([cwf[k, j * 1024 + 128 * i:j * 1024 + 128 * i + 128] for j in range(3) for k in range(3)], axis=1)))
        cbc.append(np.ascontiguousarray(np.stack([cbf[j * 1024 + 128 * i:j * 1024 + 128 * i + 128] for j in range(3)], axis=1)))
        cols = np.concatenate([np.arange(q * 1024 + 128 * i, q * 1024 + 128 * i + 128) for q in range(4)])
        w3c.append(np.ascontiguousarray(w3f[:, cols]))
        dlc.append(np.ascontiguousarray(dlf[cols].reshape(4, 128).T))
        hbc.append(np.ascontiguousarray(hbf[:, blk].T))
        vgc.append(np.ascontiguousarray(PT[32 + i].T.reshape((LX + LC) // 128, 128, 128)))
        uc.append(np.ascontiguousarray(PT[24 + i]))
        wsc.append(np.ascontiguousarray(f32(gm_ws)[0][i].T))
        bsc.append(np.ascontiguousarray(f32(gm_bs)[0][i][None, :]))
    couts = phase_C(P3c, cwc, cbc, w3c, dlc, hbc, vgc, uc, wsc, bsc, f32(hy_w1)[0], f32(hy_b1)[0], f32(hy_w2)[0], f32(hy_b2)[0],
                    f32(hy_freq)[0], LX, LC)
    CAT = np.concatenate([np.stack([o[0] for o in couts]), np.stack([o[1] for o in couts])], axis=0)
    cats = _token_shard(CAT, NX, NCX, LX)
    pq, keysT, uT, vt = peer_args(0)
    x1Ts = phase_D(cats, xTs, [mvD(0)] * NCORES, wblocks(f32(ev_w_out)[0]), pq, keysT, uT, vt, NX, NCX)
    del uT, vt
    epi1 = ["gelu"] * 8 + ["copy"] * 16
    projs1 = phase_B(x1Ts, [mvB(1)] * NCORES, wblocks(f32(od_w_in)[0]), epi1, NX, NCX)
    PT1 = _assemble(projs1, NX, NCX)
    ocw = f32(od_conv_w)[0]; ocb = f32(od_conv_b)[0]
    xrc, gc, xpc, cw1, wac, wxc, vcc, swc, btc = [], [], [], [], [], [], [], [], []
    for i in range(NCORES):
        blk = slice(128 * i, 128 * i + 128)
        xrc.append(np.ascontiguousarray(PT1[8 + i]))
        gc.append(np.ascontiguousarray(PT1[i][:, :LX]))
        xpc.append(np.ascontiguousarray(PT1[16 + i][:, :LX]))
        cw1.append(np.ascontiguousarray(np.stack([ocw[0, blk], ocw[1, blk], ocw[2, blk], ocw[3, blk], ocb[blk]], axis=1)))
        wac.append(np.ascontiguousarray(f32(lru_wa)[0][:, i]))
        wxc.append(np.ascontiguousarray(f32(lru_wx)[0][:, i]))
        vcc.append(np.ascontiguousarray(np.stack([f32(lru_ba)[0][0, blk], f32(lru_ba)[0][1, blk], f32(lru_bx)[0][0, blk], f32(lru_bx)[0][1, blk],
                                                  f32(lru_lam)[0][0, blk], f32(lru_lam)[0][1, blk]], axis=1)))
        g = i // 2
        sw = np.zeros((128, 4), np.float32); bt = np.zeros((128, 4, 16), np.float32)
        w = (2, 4, 8, 16)[g]; half = w // 2
        sw[:, g] = 1.0 / w
        tcols = np.concatenate([np.arange(8), np.arange(LX - 8, LX)])
        cnt = np.minimum(tcols + half, LX) - np.maximum(tcols - half, 0)
        bt[:, g, :] = (1.0 / cnt.astype(np.float32))[None, :]
        swc.append(sw); btc.append(bt)
    fouts = phase_F(xrc, gc, xpc, cw1, wac, wxc, vcc, swc, btc, LX, LC)
    CAT1 = np.concatenate([np.stack([o[0] for o in fouts]), np.stack([o[1] for o in fouts])], axis=0)
    cats1 = _token_shard(CAT1, NX, 0, LX)
    x1only = [np.ascontiguousarray(a[:, :, :NX]) for a in x1Ts]
    pq, keysT, uT, vt = peer_args(1)
    pw = np.ascontiguousarray(f32(pool_w)[0].reshape(4, 2, 128, 256))
    pp = np.ascontiguousarray(np.stack([vec_fm(f32(pool_scale)[0]), vec_fm(f32(pool_b)[0].reshape(-1))], axis=2))
    outs = phase_D(cats1, x1only, [mvD(1)] * NCORES, wblocks(f32(od_w_out)[0]), pq, keysT, uT, vt, NX, 0,
                   final_g=vec_fm(f32(norm_final)), pool=(pw, pp))
    out = np.concatenate([unfm(o) for o in outs], axis=0)
    return out[None].astype(np.float32)
```
